# Optimizing a Trainium2 kernel written in Bass

```python
import jax, jax.numpy as jnp
from jax import lax
import numpy as np

D_MODEL = 1024
BATCH = 2
SEQ = 16384
DEPTH = 2

SC_WIDTH = D_MODEL // 4
SC_GROUPS = 4
SC_KERNEL = 3
SB_HEAD_DIM = 64
SB_HEADS = (D_MODEL // 4) // SB_HEAD_DIM
SB_WIDTH = SB_HEADS * SB_HEAD_DIM
SB_BLOCK = 128
SSM_INNER = D_MODEL // 2
SSM_HEAD_DIM = 64
SSM_HEADS = SSM_INNER // SSM_HEAD_DIM
SSM_GROUPS = 2
SSM_STATE = 64
SSM_CONV = 4
SSM_CHUNK = 256
SSM_CONV_DIM = SSM_INNER + 2 * SSM_GROUPS * SSM_STATE
N_BRANCH = 3
FFN_HIDDEN = -(-8 * D_MODEL // (3 * 256)) * 256
NORM_EPS = 1e-6
N_MOD = 6
PROJ_SIZES = (SC_WIDTH, SC_WIDTH, SC_WIDTH,
              SB_WIDTH, SB_WIDTH, SB_WIDTH,
              SSM_INNER, SSM_CONV_DIM, SSM_HEADS,
              D_MODEL, D_MODEL, D_MODEL)
IN_PROJ = sum(PROJ_SIZES)

kernel_name = "hybrid_shortconv_stickbreak_ssd_block"


def rms_norm(x, g):
    x32 = x.astype(jnp.float32)
    y = x32 * lax.rsqrt(jnp.mean(x32 * x32, axis=-1, keepdims=True) + NORM_EPS)
    return (y * g.astype(jnp.float32)).astype(x.dtype)


def causal_depthwise_conv(x, w):
    k = w.shape[0]
    return lax.conv_general_dilated(
        x, w[:, None, :].astype(x.dtype), window_strides=(1,), padding=[(k - 1, 0)],
        dimension_numbers=("NWC", "WIO", "NWC"), feature_group_count=x.shape[-1])


def split_columns(p):
    offsets = [int(o) for o in np.cumsum(PROJ_SIZES)[:-1]]
    return jnp.split(p, offsets, axis=-1)


def short_conv_mixer(b_gate, c_gate, xa, w_conv):
    return b_gate * causal_depthwise_conv(c_gate * xa, w_conv)


def stick_breaking_attention(q, k, v):
    bsz, seq, heads, dh = q.shape
    n_blk = seq // SB_BLOCK
    scale = dh ** -0.5
    qh = q.transpose(0, 2, 1, 3)
    kh = k.transpose(0, 2, 1, 3)
    vh = v.transpose(0, 2, 1, 3)
    strict = jnp.tril(jnp.ones((SB_BLOCK, SB_BLOCK), jnp.float32), -1)
    outs = []
    for i in range(n_blk):
        start, end = i * SB_BLOCK, (i + 1) * SB_BLOCK
        z = jnp.einsum("bhqd,bhkd->bhqk", qh[:, :, start:end], kh[:, :, :end],
                       preferred_element_type=jnp.float32) * scale
        mask = jnp.arange(end)[None, :] < (start + jnp.arange(SB_BLOCK))[:, None]
        log_keep = jnp.where(mask, jax.nn.log_sigmoid(-z), 0.0)
        lk = log_keep.reshape(bsz, heads, SB_BLOCK, i + 1, SB_BLOCK)
        within = jnp.einsum("bhqcj,js->bhqcs", lk, strict)
        blk_tot = jnp.sum(lk, axis=-1)
        later = lax.cumsum(blk_tot, axis=3, reverse=True) - blk_tot
        log_rest = (within + later[..., None]).reshape(bsz, heads, SB_BLOCK, end)
        att = jnp.exp(jnp.where(mask, z + log_keep + log_rest, -jnp.inf))
        outs.append(jnp.einsum("bhqk,bhkd->bhqd", att.astype(vh.dtype), vh[:, :, :end]))
    out = jnp.concatenate(outs, axis=2)
    return out.transpose(0, 2, 1, 3).reshape(bsz, seq, heads * dh)


def segsum_exp(a_cs):
    l = a_cs.shape[-1]
    mask = jnp.tril(jnp.ones((l, l), dtype=bool))
    diff = a_cs[..., :, None] - a_cs[..., None, :]
    return jnp.exp(jnp.where(mask, diff, -jnp.inf))


def ssd_scan(xh, dt, a_neg, bm, cm):
    bsz, seq, heads, hd = xh.shape
    reps = heads // bm.shape[2]
    bh = jnp.repeat(bm, reps, axis=2)
    ch = jnp.repeat(cm, reps, axis=2)
    xdt = xh * dt[..., None]
    a = dt * a_neg
    pad = (-seq) % SSM_CHUNK
    if pad:
        pw = ((0, 0), (0, pad), (0, 0), (0, 0))
        xdt, bh, ch = jnp.pad(xdt, pw), jnp.pad(bh, pw), jnp.pad(ch, pw)
        a = jnp.pad(a, ((0, 0), (0, pad), (0, 0)))
    n_c = (seq + pad) // SSM_CHUNK
    xc = xdt.reshape(bsz, n_c, SSM_CHUNK, heads, hd)
    bc = bh.reshape(bsz, n_c, SSM_CHUNK, heads, -1)
    cc = ch.reshape(bsz, n_c, SSM_CHUNK, heads, -1)
    a_cs = jnp.cumsum(a.reshape(bsz, n_c, SSM_CHUNK, heads), axis=2)
    decay_in = segsum_exp(a_cs.transpose(0, 1, 3, 2))
    scores = jnp.einsum("bclhn,bcshn->bchls", cc, bc) * decay_in
    y_diag = jnp.einsum("bchls,bcshp->bclhp", scores, xc)
    decay_to_end = jnp.exp(a_cs[:, :, -1:, :] - a_cs)
    chunk_states = jnp.einsum("bclhn,bclhp->bchpn", bc * decay_to_end[..., None], xc)
    chunk_decay = jnp.exp(a_cs[:, :, -1, :])

    def step(state, inp):
        s_c, d_c = inp
        return state * d_c[..., None, None] + s_c, state

    init = jnp.zeros((bsz, heads, hd, bc.shape[-1]), jnp.float32)
    _, prev = lax.scan(step, init, (chunk_states.transpose(1, 0, 2, 3, 4),
                                    chunk_decay.transpose(1, 0, 2)))
    prev = prev.transpose(1, 0, 2, 3, 4)
    y_off = jnp.einsum("bclhn,bchpn->bclhp", cc, prev) * jnp.exp(a_cs)[..., None]
    return (y_diag + y_off).reshape(bsz, seq + pad, heads, hd)[:, :seq]


def mamba2_mixer(z, xbc, dt_raw, conv_w, conv_b, dt_bias, a_log, d_skip, norm_w):
    xbc = jax.nn.silu(causal_depthwise_conv(xbc, conv_w) + conv_b)
    xs, bm, cm = jnp.split(xbc, [SSM_INNER, SSM_INNER + SSM_GROUPS * SSM_STATE], axis=-1)
    bsz, seq, _ = xs.shape
    xh = xs.reshape(bsz, seq, SSM_HEADS, SSM_HEAD_DIM).astype(jnp.float32)
    bm = bm.reshape(bsz, seq, SSM_GROUPS, SSM_STATE).astype(jnp.float32)
    cm = cm.reshape(bsz, seq, SSM_GROUPS, SSM_STATE).astype(jnp.float32)
    dt = jax.nn.softplus(dt_raw.astype(jnp.float32) + dt_bias.astype(jnp.float32))
    a_neg = -jnp.exp(a_log.astype(jnp.float32))
    y = ssd_scan(xh, dt, a_neg, bm, cm) + xh * d_skip.astype(jnp.float32)[:, None]
    y = y.reshape(bsz, seq, SSM_INNER) * jax.nn.silu(z.astype(jnp.float32))
    yg = y.reshape(bsz, seq, SSM_GROUPS, SSM_INNER // SSM_GROUPS)
    yg = yg * lax.rsqrt(jnp.mean(yg * yg, axis=-1, keepdims=True) + NORM_EPS)
    return (yg.reshape(bsz, seq, SSM_INNER) * norm_w.astype(jnp.float32)).astype(z.dtype)


def hybrid_layer(x, c, mod_w, mod_b, g_pre_mix, g_post_mix, g_pre_ffn, g_post_ffn,
                 w_in, sc_conv_w, ssm_conv_w, ssm_conv_b, ssm_dt_bias, ssm_a_log, ssm_d,
                 ssm_norm_w, w_sc_out, w_sb_out, w_ssm_out, w_o, w_ffn_in, w_ffn_out):
    bsz, seq, _ = x.shape
    mod = jax.nn.silu(c) @ mod_w + mod_b
    shift1, scale1, gate1, shift2, scale2, gate2 = [m[:, None, :] for m in jnp.split(mod, N_MOD, axis=-1)]

    h = rms_norm(x, g_pre_mix) * (1 + scale1) + shift1
    (sc_b, sc_c, sc_x, q, k, v, z, xbc, dt_raw,
     gl_a, gl_b, gl_c) = split_columns(h @ w_in)
    y_a = short_conv_mixer(sc_b, sc_c, sc_x, sc_conv_w) @ w_sc_out
    y_b = stick_breaking_attention(
        q.reshape(bsz, seq, SB_HEADS, SB_HEAD_DIM),
        k.reshape(bsz, seq, SB_HEADS, SB_HEAD_DIM),
        v.reshape(bsz, seq, SB_HEADS, SB_HEAD_DIM)) @ w_sb_out
    y_c = mamba2_mixer(z, xbc, dt_raw, ssm_conv_w, ssm_conv_b, ssm_dt_bias, ssm_a_log,
                       ssm_d, ssm_norm_w) @ w_ssm_out
    merged = (jax.nn.sigmoid(gl_a) * y_a + jax.nn.sigmoid(gl_b) * y_b
              + jax.nn.sigmoid(gl_c) * y_c)
    mix_out = merged @ w_o
    x = x + (gate1 * rms_norm(mix_out, g_post_mix)).astype(x.dtype)

    h2 = rms_norm(x, g_pre_ffn) * (1 + scale2) + shift2
    gt, up = jnp.split(h2 @ w_ffn_in, 2, axis=-1)
    f = (jax.nn.silu(gt) * up) @ w_ffn_out
    x = x + (gate2 * rms_norm(f, g_post_ffn)).astype(x.dtype)
    return x


def setup_inputs(seed: int = 0) -> dict:
    key = jax.random.key(seed)
    ks = jax.random.split(key, 24)
    f32 = jnp.float32

    def nrm(k, shape, fan_in):
        return jax.random.normal(k, shape, f32) * (fan_in ** -0.5)

    def gain(k, shape):
        return 1.0 + 0.05 * jax.random.normal(k, shape, f32)

    dt0 = jnp.exp(jax.random.uniform(ks[14], (DEPTH, SSM_HEADS), f32,
                                     jnp.log(1e-3), jnp.log(1e-1)))
    return {
        "x": jax.random.normal(ks[0], (BATCH, SEQ, D_MODEL), f32),
        "c": jax.random.normal(ks[1], (BATCH, D_MODEL), f32),
        "mod_w": nrm(ks[2], (DEPTH, D_MODEL, N_MOD * D_MODEL), D_MODEL),
        "mod_b": 0.02 * jax.random.normal(ks[3], (DEPTH, N_MOD * D_MODEL), f32),
        "g_pre_mix": gain(ks[4], (DEPTH, D_MODEL)),
        "g_post_mix": gain(ks[5], (DEPTH, D_MODEL)),
        "g_pre_ffn": gain(ks[6], (DEPTH, D_MODEL)),
        "g_post_ffn": gain(ks[7], (DEPTH, D_MODEL)),
        "w_in": nrm(ks[8], (DEPTH, D_MODEL, IN_PROJ), D_MODEL),
        "sc_conv_w": nrm(ks[9], (DEPTH, SC_KERNEL, SC_WIDTH), SC_KERNEL),
        "ssm_conv_w": nrm(ks[10], (DEPTH, SSM_CONV, SSM_CONV_DIM), SSM_CONV),
        "ssm_conv_b": 0.02 * jax.random.normal(ks[11], (DEPTH, SSM_CONV_DIM), f32),
        "ssm_dt_bias": dt0 + jnp.log(-jnp.expm1(-dt0)),
        "ssm_a_log": jnp.log(jax.random.uniform(ks[12], (DEPTH, SSM_HEADS), f32, 1.0, 16.0)),
        "ssm_d": 1.0 + 0.1 * jax.random.normal(ks[13], (DEPTH, SSM_HEADS), f32),
        "ssm_norm_w": gain(ks[15], (DEPTH, SSM_INNER)),
        "w_sc_out": nrm(ks[16], (DEPTH, SC_WIDTH, D_MODEL), SC_WIDTH),
        "w_sb_out": nrm(ks[17], (DEPTH, SB_WIDTH, D_MODEL), SB_WIDTH),
        "w_ssm_out": nrm(ks[18], (DEPTH, SSM_INNER, D_MODEL), SSM_INNER),
        "w_o": nrm(ks[19], (DEPTH, D_MODEL, D_MODEL), D_MODEL),
        "w_ffn_in": nrm(ks[20], (DEPTH, D_MODEL, 2 * FFN_HIDDEN), D_MODEL),
        "w_ffn_out": nrm(ks[21], (DEPTH, FFN_HIDDEN, D_MODEL), FFN_HIDDEN),
    }


def reference(x, c, mod_w, mod_b, g_pre_mix, g_post_mix, g_pre_ffn, g_post_ffn, w_in,
              sc_conv_w, ssm_conv_w, ssm_conv_b, ssm_dt_bias, ssm_a_log, ssm_d, ssm_norm_w,
              w_sc_out, w_sb_out, w_ssm_out, w_o, w_ffn_in, w_ffn_out):
    for l in range(DEPTH):
        x = hybrid_layer(x, c, mod_w[l], mod_b[l], g_pre_mix[l], g_post_mix[l],
                         g_pre_ffn[l], g_post_ffn[l], w_in[l], sc_conv_w[l], ssm_conv_w[l],
                         ssm_conv_b[l], ssm_dt_bias[l], ssm_a_log[l], ssm_d[l], ssm_norm_w[l],
                         w_sc_out[l], w_sb_out[l], w_ssm_out[l], w_o[l], w_ffn_in[l],
                         w_ffn_out[l])
    return x
```

```python
import contextlib
import numpy as np
import ml_dtypes
import concourse.bass as bass
import concourse.mybir as mybir
from concourse.bass_utils import run_bass_kernel_spmd

F32 = mybir.dt.float32
BF16 = mybir.dt.bfloat16
AF = mybir.ActivationFunctionType
ALU = mybir.AluOpType

D = 1024
SEQ = 16384
NB = 2
DEPTH = 2
NT = 4096
TT = 512
NTT = NT // TT
EPS = 1e-6
FFH = 2816
NFC = FFH // 128
SAME_SYNC = True
NDS = 24
NDS_POOL = 4


class Buf:
    __slots__ = ("name", "w", "r")

    def __init__(self, name=""):
        self.name = name
        self.w = {}
        self.r = {}


class Sched:
    def __init__(self, nc, es):
        self.nc = nc
        self.engs = {"pe": nc.tensor, "act": nc.scalar, "dve": nc.vector, "pool": nc.gpsimd, "sp": nc.sync}
        self.sems = []
        self.esem = {}
        self.ecnt = {}
        for k in ("pe", "act", "dve", "pool"):
            self.esem[k] = self._newsem(es, "e_" + k)
            self.ecnt[k] = 0
        self.dsem = {}
        self.dcnt = {}
        self.dnext = {}
        self.nds = {"sp": NDS, "pool": NDS_POOL, "act": 4}
        for q in ("sp", "pool", "act"):
            self.dsem[q] = [self._newsem(es, f"d_{q}{i}") for i in range(self.nds[q])]
            self.dcnt[q] = [0] * self.nds[q]
            self.dnext[q] = 0
        self.known = {k: {} for k in self.engs}
        self.nwaits = 0
        self.nins = 0

    def _newsem(self, es, name):
        self.sems.append(es.enter_context(self.nc.semaphore(name)))
        return len(self.sems) - 1

    def _wait(self, ek, si, val):
        k = self.known[ek]
        if k.get(si, 0) >= val:
            return
        self.engs[ek].wait_ge(self.sems[si], val)
        k[si] = val
        self.nwaits += 1

    def _deps(self, ek, reads, writes):
        need = {}
        for b in reads:
            for si, v in b.w.items():
                if need.get(si, 0) < v:
                    need[si] = v
        for b in writes:
            for si, v in b.w.items():
                if need.get(si, 0) < v:
                    need[si] = v
            for si, v in b.r.items():
                if need.get(si, 0) < v:
                    need[si] = v
        own = self.esem.get(ek)
        for si, v in need.items():
            if si == own and (ek == "pe" or not SAME_SYNC):
                continue
            self._wait(ek, si, v)

    def _mark(self, si, val, reads, writes):
        for b in reads:
            if b.r.get(si, 0) < val:
                b.r[si] = val
        for b in writes:
            b.w = {si: val}
            b.r = {}

    def op(self, ek, fn, reads=(), writes=()):
        self._deps(ek, reads, writes)
        ins = fn(self.engs[ek])
        self.ecnt[ek] += 1
        si = self.esem[ek]
        ins.then_inc(self.sems[si], 1)
        self._mark(si, self.ecnt[ek], reads, writes)
        self.nins += 1

    def dma(self, q, out, in_, reads=(), writes=(), adds=()):
        self._deps(q, reads, writes)
        i = self.dnext[q]
        self.dnext[q] = (i + 1) % self.nds[q]
        si = self.dsem[q][i]
        if self.dcnt[q][i] > 0:
            self._wait(q, si, self.dcnt[q][i])
        self.engs[q].dma_start(out=out, in_=in_).then_inc(self.sems[si], 16)
        self.dcnt[q][i] += 16
        self._mark(si, self.dcnt[q][i], reads, writes)
        for b in adds:
            b.w[si] = self.dcnt[q][i]
        self.nins += 1

    def barrier(self):
        for ek in self.engs:
            for q in self.dsem:
                for i, si in enumerate(self.dsem[q]):
                    if self.dcnt[q][i] > 0:
                        self._wait(ek, si, self.dcnt[q][i])
            for k, si in self.esem.items():
                if self.ecnt[k] > 0 and k != ek:
                    self._wait(ek, si, self.ecnt[k])

    def finish(self):
        for q in self.dsem:
            for i, si in enumerate(self.dsem[q]):
                if self.dcnt[q][i] > 0:
                    self._wait("sp", si, self.dcnt[q][i])
        for k, si in self.esem.items():
            if self.ecnt[k] > 0:
                self._wait("sp", si, self.ecnt[k])


class Ctx:
    def __init__(self, nc, es):
        self.nc = nc
        self.es = es
        self.S = Sched(nc, es)
        self.n = 0
        self.cur_es = None

    def sb(self, shape, dt, name=None, es=None):
        self.n += 1
        t = (es or self.cur_es or self.es).enter_context(self.nc.sbuf_tensor(f"{name or 't'}_{self.n}", list(shape), dt))
        return t

    @contextlib.contextmanager
    def scope(self):
        prev = self.cur_es
        with contextlib.ExitStack() as es:
            self.cur_es = es
            try:
                yield es
            finally:
                self.S.barrier()
                self.cur_es = prev

    def ps(self, shape=(128, 512), dt=F32, name=None):
        self.n += 1
        return self.es.enter_context(self.nc.psum_tensor(f"{name or 'p'}_{self.n}", list(shape), dt))

    def dram(self, name, shape, dt, kind="Internal"):
        return self.nc.dram_tensor(name, list(shape), dt, kind=kind).ap()


class Rot:
    def __init__(self, items):
        self.items = items
        self.i = 0

    def next(self):
        it = self.items[self.i]
        self.i = (self.i + 1) % len(self.items)
        return it


WIN_GROUPS = [(0, 512), (512, 512), (1024, 512), (1536, 512), (2048, 512), (2560, 264)] + \
             [(2824 + 512 * i, 512) for i in range(6)]


def prep_weights(C, specs):
    S = C.S
    nc = C.nc
    out = {}
    with contextlib.ExitStack() as es:
        st32 = [(C.sb([128, 5632], F32, "st32", es), Buf()) for _ in range(2)]
        st16 = [(C.sb([128, 5632], BF16, "st16", es), Buf()) for _ in range(2)]
        r32 = Rot(st32)
        r16 = Rot(st16)
        engs = ["pool", "dve", "act"]
        ei = 0
        for key, src, KC, groups in specs:
            srcv = src.rearrange("(kc p) n -> p kc n", p=128)
            lst = []
            for gi, pieces in enumerate(groups):
                GW = sum(p[1] for p in pieces)
                dst = C.dram(f"wb_{key}_{gi}", [128, KC, GW], BF16)
                dbuf = Buf()
                t32, b32 = r32.next()
                t16, b16 = r16.next()
                v32 = t32[:, 0:KC * GW].rearrange("p (k n) -> p k n", k=KC)
                v16 = t16[:, 0:KC * GW].rearrange("p (k n) -> p k n", k=KC)
                o = 0
                for (c0, ncol) in pieces:
                    S.dma("sp", v32[:, :, o:o + ncol], srcv[:, :, c0:c0 + ncol], writes=[b32])
                    o += ncol
                ek = engs[ei % 3]
                ei += 1
                if ek == "act":
                    S.op("act", lambda e: e.copy(out=t16[:, 0:KC * GW], in_=t32[:, 0:KC * GW]), [b32], [b16])
                else:
                    S.op(ek, lambda e: e.tensor_copy(out=t16[:, 0:KC * GW], in_=t32[:, 0:KC * GW]), [b32], [b16])
                S.dma("pool", dst, v16, reads=[b16], writes=[dbuf])
                lst.append((dst, dbuf))
            out[key] = lst
        S.barrier()
    return out


class WStream:
    def __init__(self, C, nbuf=3, elems=5632):
        self.C = C
        self.bufs = Rot([(C.sb([128, elems], BF16, "wst"), Buf()) for _ in range(nbuf)])

    def load(self, grp):
        dst, dbuf = grp
        KC, GW = dst.shape[1], dst.shape[2]
        t, b = self.bufs.next()
        v = t[:, 0:KC * GW].rearrange("p (k n) -> p k n", k=KC)
        self.C.S.dma("sp", v, dst, reads=[dbuf], writes=[b])
        return v, b


class TokState:
    pass


def tok_setup(C, l_params):
    S = C.S
    T = TokState()
    T.ones_d = C.sb([128, 128], BF16, "ones_d")
    T.ones_g = C.sb([128, 128], BF16, "ones_g")
    T.b_const = Buf()
    S.op("pool", lambda e: e.memset(T.ones_d[:], 1.0 / 1024.0), [], [T.b_const])
    S.op("pool", lambda e: e.memset(T.ones_g[:], 1.0 / 256.0), [], [T.b_const])
    T.eps_t = C.sb([128, 1], F32, "eps_t")
    S.op("pool", lambda e: e.memset(T.eps_t[:], EPS), [], [T.b_const])
    T.xt = C.sb([128, 8, TT], F32, "xt"); T.b_xt = Buf()
    T.hT = C.sb([128, 8, TT], BF16, "hT"); T.b_hT = Buf()
    T.sq = C.sb([128, 8, TT], BF16, "sq"); T.b_sq = Buf()
    T.rstd = C.sb([128, TT], F32, "rstd"); T.b_rstd = Buf()
    T.tmp = [(C.sb([128, TT], F32, "tmp"), Buf()) for _ in range(4)]
    T.rtmp = Rot(T.tmp)
    T.psum = Rot([(C.ps(), Buf()) for _ in range(8)])
    return T


def compute_mod(C, T, P, es_scope=None):
    S = C.S
    nc = C.nc
    T.vec = C.sb([128, 48], F32, "vec"); T.b_vec = Buf()
    with contextlib.ExitStack() as es:
        cT = C.sb([128, 8], F32, "cT", es); b_c = Buf()
        sc = C.sb([128, 8], F32, "sc", es); b_sc = Buf()
        mb = C.sb([128, 48], F32, "mb", es); b_mb = Buf()
        g4 = C.sb([128, 32], F32, "g4", es); b_g4 = Buf()
        modv = C.sb([128, 48], F32, "modv", es); b_modv = Buf()
        wbufs = Rot([(C.sb([128, 8, 512], F32, "mw", es), Buf()) for _ in range(2)])
        S.dma("sp", cT[:], P["cT"], writes=[b_c])
        S.dma("sp", mb[:], P["mod_b6"], writes=[b_mb])
        S.dma("sp", g4[:], P["g4"], writes=[b_g4])
        S.op("act", lambda e: e.activation(out=sc[:], in_=cT[:], func=AF.Silu), [b_c], [b_sc])
        mw = P["mod_w"].rearrange("(kc p) n -> p kc n", p=128)
        pt, pb = T.psum.next()
        for gi in range(12):
            wt, wb = wbufs.next()
            S.dma("sp", wt[:], mw[:, :, gi * 512:(gi + 1) * 512], writes=[wb])
            for fc in range(4):
                col = gi * 4 + fc
                for k in range(8):
                    S.op("pe", lambda e, k=k, fc=fc, col=col: e.matmul(
                        pt[:, col:col + 1], lhsT=wt[:, k, fc * 128:(fc + 1) * 128], rhs=sc[:, k:k + 1],
                        start=(k == 0), stop=(k == 7)), [wb, b_sc], [pb])
        S.op("dve", lambda e: e.tensor_tensor(out=modv[:], in0=pt[:, 0:48], in1=mb[:], op=ALU.add), [pb, b_mb], [b_modv])
        v = T.vec
        S.op("dve", lambda e: e.scalar_tensor_tensor(out=v[:, 0:8], in0=modv[:, 8:16], scalar=1.0, in1=g4[:, 0:8],
                                                     op0=ALU.add, op1=ALU.mult), [b_modv, b_g4], [T.b_vec])
        S.op("dve", lambda e: e.tensor_copy(out=v[:, 8:16], in_=modv[:, 0:8]), [b_modv], [T.b_vec])
        S.op("dve", lambda e: e.tensor_tensor(out=v[:, 16:24], in0=modv[:, 16:24], in1=g4[:, 8:16], op=ALU.mult), [b_modv, b_g4], [T.b_vec])
        S.op("dve", lambda e: e.scalar_tensor_tensor(out=v[:, 24:32], in0=modv[:, 32:40], scalar=1.0, in1=g4[:, 16:24],
                                                     op0=ALU.add, op1=ALU.mult), [b_modv, b_g4], [T.b_vec])
        S.op("dve", lambda e: e.tensor_copy(out=v[:, 32:40], in_=modv[:, 24:32]), [b_modv], [T.b_vec])
        S.op("dve", lambda e: e.tensor_tensor(out=v[:, 40:48], in0=modv[:, 40:48], in1=g4[:, 24:32], op=ALU.mult), [b_modv, b_g4], [T.b_vec])
        S.barrier()
    return T


def rms_stats(C, T, src_t, src_b, nch, ones_t):
    S = C.S
    S.op("act", lambda e: e.activation(out=T.sq[:, 0:nch, :], in_=src_t[:, 0:nch, :], func=AF.Square), [src_b], [T.b_sq])
    pt, pb = T.psum.next()
    for k in range(nch):
        S.op("pe", lambda e, k=k: e.matmul(pt[:], lhsT=ones_t[:], rhs=T.sq[:, k, :], start=(k == 0), stop=(k == nch - 1)),
             [T.b_sq, T.b_const], [pb])
    S.op("act", lambda e: e.activation(out=T.rstd[:], in_=pt[:], func=AF.Sqrt, bias=T.eps_t[:, 0:1], scale=1.0), [pb, T.b_const], [T.b_rstd])
    S.op("dve", lambda e: e.reciprocal(out=T.rstd[:], in_=T.rstd[:]), [T.b_rstd], [T.b_rstd])


def norm_mod(C, T, gcol, scol):
    S = C.S
    rms_stats(C, T, T.xt, T.b_xt, 8, T.ones_d)
    for k in range(8):
        tt, tb = T.rtmp.next()
        S.op("dve", lambda e, k=k, tt=tt: e.scalar_tensor_tensor(
            out=tt[:], in0=T.xt[:, k, :], scalar=T.vec[:, gcol + k:gcol + k + 1], in1=T.rstd[:],
            op0=ALU.mult, op1=ALU.mult), [T.b_xt, T.b_vec, T.b_rstd], [tb])
        S.op("act", lambda e, k=k, tt=tt: e.activation(
            out=T.hT[:, k, :], in_=tt[:], func=AF.Identity, bias=T.vec[:, scol + k:scol + k + 1], scale=1.0),
            [tb, T.b_vec], [T.b_hT])


def proj_chunk(C, T, wv, wb, c0, M=128):
    S = C.S
    pt, pb = T.psum.next()
    for k in range(8):
        S.op("pe", lambda e, k=k: e.matmul(pt[0:M, :], lhsT=wv[:, k, c0:c0 + M], rhs=T.hT[:, k, :],
                                           start=(k == 0), stop=(k == 7)), [wb, T.b_hT], [pb])
    return pt, pb


def tok_a(C, T, xT_in, wg, send, send_dt, ws):
    S = C.S
    xv = xT_in.rearrange("(kc p) t -> p kc t", p=128)
    stage = Rot([(C.sb([128, TT], BF16, "stg"), Buf()) for _ in range(4)])
    dts = Rot([(C.sb([8, TT], F32, "dts"), Buf()) for _ in range(2)])
    sb_send = Buf()
    for ti in range(NTT):
        t0 = ti * TT
        S.dma("sp", T.xt[:], xv[:, :, t0:t0 + TT], writes=[T.b_xt])
        norm_mod(C, T, 0, 8)

        def store_halves(st, sb_, row0, dest_of_half):
            for hf in range(2):
                S.dma("pool", send[dest_of_half[hf], row0:row0 + 64, t0:t0 + TT], st[hf * 64:(hf + 1) * 64, :],
                      reads=[sb_], adds=[sb_send])

        w0, b0 = ws.load(wg[0])
        w1, b1 = ws.load(wg[1])
        for i in range(2):
            pc, pcb = proj_chunk(C, T, w0, b0, 256 + i * 128)
            tt, tb = T.rtmp.next()
            S.op("act", lambda e, tt=tt, pc=pc: e.copy(out=tt[:], in_=pc[:]), [pcb], [tb])
            px, pxb = proj_chunk(C, T, w1, b1, i * 128)
            st, sb_ = stage.next()
            S.op("dve", lambda e, tt=tt, px=px, st=st: e.tensor_tensor(out=st[:], in0=px[:], in1=tt[:], op=ALU.mult),
                 [pxb, tb], [sb_])
            store_halves(st, sb_, 0, (2 * i, 2 * i + 1))
        w2, b2 = ws.load(wg[2])
        for i in range(2):
            pq, pqb = proj_chunk(C, T, w1, b1, 256 + i * 128)
            st, sb_ = stage.next()
            S.op("act", lambda e, pq=pq, st=st: e.activation(out=st[:], in_=pq[:], func=AF.Copy, scale=0.125), [pqb], [sb_])
            store_halves(st, sb_, 64, (2 * i, 2 * i + 1))
        w4, b4 = ws.load(wg[4])
        for j in range(2):
            for i in range(2):
                pk, pkb = proj_chunk(C, T, w2, b2, j * 256 + i * 128)
                st, sb_ = stage.next()
                S.op("act", lambda e, pk=pk, st=st: e.copy(out=st[:], in_=pk[:]), [pkb], [sb_])
                store_halves(st, sb_, 128 + 64 * j, (2 * i, 2 * i + 1))
        w5, b5 = ws.load(wg[5])
        for i in range(4):
            px, pxb = proj_chunk(C, T, w4, b4, i * 128)
            st, sb_ = stage.next()
            S.op("act", lambda e, px=px, st=st: e.copy(out=st[:], in_=px[:]), [pxb], [sb_])
            S.dma("pool", send[i, 256:384, t0:t0 + TT], st[:], reads=[sb_], adds=[sb_send])
        for j in range(2):
            pb_, pbb = proj_chunk(C, T, w5, b5, j * 128)
            st, sb_ = stage.next()
            S.op("act", lambda e, pb_=pb_, st=st: e.copy(out=st[:], in_=pb_[:]), [pbb], [sb_])
            for gi in range(2):
                for dd in range(2):
                    S.dma("pool", send[2 * gi + dd, 384 + 64 * j:448 + 64 * j, t0:t0 + TT], st[gi * 64:(gi + 1) * 64, :],
                          reads=[sb_], adds=[sb_send])
        pd, pdb = proj_chunk(C, T, w5, b5, 256, M=8)
        dt_t, dt_b = dts.next()
        S.op("act", lambda e, pd=pd, dt_t=dt_t: e.copy(out=dt_t[:], in_=pd[0:8, :]), [pdb], [dt_b])
        for g in range(4):
            S.dma("pool", send_dt[g, :, t0:t0 + TT], dt_t[2 * g:2 * g + 2, :], reads=[dt_b], adds=[sb_send])
    return sb_send


NBLK = SEQ // 128
NCH = SEQ // 256


def host_consts():
    c = np.zeros((128, 896), np.float32)
    i = np.arange(128)
    c[:, 0:128] = np.eye(128)
    c[:, 128:256] = (i[:, None] <= i[None, :])
    c[:, 256:384] = 1.0
    c[:, 384:512] = (i[:, None] > i[None, :])
    c[:, 512:640] = (i[:, None] < i[None, :])
    c[:, 640:768] = np.where(i[None, :] < i[:, None], -30000.0, 0.0)
    c[127, 768:896] = 1.0
    return c


def mix_piece(C, R, Rdt, O2, P, do_conv=True, do_attn=True, do_ssd=True, nq=NBLK):
    S = C.S
    nc = C.nc
    cf = C.sb([128, 896], F32, "cf"); b_cf = Buf()
    cb = C.sb([128, 640], BF16, "cb"); b_cb = Buf()
    S.dma("sp", cf[:], P["consts"], writes=[b_cf])
    S.op("dve", lambda e: e.tensor_copy(out=cb[:], in_=cf[:, 0:640]), [b_cf], [b_cb])
    ident_b = cb[:, 0:128]; tri_b = cb[:, 128:256]; ones_b = cb[:, 256:384]; U_b = cb[:, 384:512]
    M_f = cf[:, 512:640]
    o2buf = Buf()
    psA = Rot([(C.ps(), Buf()) for _ in range(2)])
    psB = Rot([(C.ps(), Buf()) for _ in range(2)])
    psC = Rot([(C.ps(), Buf()) for _ in range(2)])
    psL = (C.ps(), Buf())
    psM = (C.ps(), Buf())

    T1 = C.sb([128, SEQ], BF16, "T1")
    T2 = C.sb([128, 2 + SEQ], BF16, "T2")
    b_q = [Buf() for _ in range(4)]; b_v = [Buf() for _ in range(4)]
    b_k = [Buf() for _ in range(4)]; b_u = [Buf() for _ in range(4)]
    b_upad = Buf()
    S.op("pool", lambda e: e.memset(T2[64:128, 0:2], 0.0), [], [b_upad])
    for j in range(4):
        S.dma("sp", T1[0:64, j * NT:(j + 1) * NT], R[j, 64:128, :], writes=[b_q[j]])
        S.dma("sp", T2[0:64, 2 + j * NT:2 + (j + 1) * NT], R[j, 128:192, :], writes=[b_k[j]])
        S.dma("sp", T1[64:128, j * NT:(j + 1) * NT], R[j, 192:256, :], writes=[b_v[j]])
        S.dma("sp", T2[64:128, 2 + j * NT:2 + (j + 1) * NT], R[j, 0:64, :], writes=[b_u[j]])

    if do_conv:
      with C.scope():
        cw = C.sb([128, 3], F32, "cw"); b_cw = Buf()
        S.dma("sp", cw[64:128, :], P["cwA"], writes=[b_cw])
        tA = [(C.sb([128, 2048], F32, "tA"), Buf()) for _ in range(1)]
        oA = Rot([(C.sb([128, 2048], BF16, "oA"), Buf()) for _ in range(2)])
        for ch in range(8):
            t0 = ch * 2048
            j = t0 // NT
            rb = [b_u[j], b_cw, b_upad] + ([b_u[j - 1]] if (j > 0 and t0 % NT == 0) else [])
            ta, tb = tA[0]
            ot, ob = oA.next()
            S.op("dve", lambda e: e.tensor_scalar(out=ta[64:128, :], in0=T2[64:128, 2 + t0:2 + t0 + 2048], scalar1=cw[64:128, 2:3],
                                                   scalar2=None, op0=ALU.mult), rb, [tb])
            S.op("dve", lambda e: e.scalar_tensor_tensor(out=ta[64:128, :], in0=T2[64:128, 1 + t0:1 + t0 + 2048], scalar=cw[64:128, 1:2],
                                                          in1=ta[64:128, :], op0=ALU.mult, op1=ALU.add), rb + [tb], [tb])
            S.op("dve", lambda e: e.scalar_tensor_tensor(out=ot[64:128, :], in0=T2[64:128, t0:t0 + 2048], scalar=cw[64:128, 0:1],
                                                          in1=ta[64:128, :], op0=ALU.mult, op1=ALU.add), rb + [tb], [ob])
            S.dma("pool", O2[j, 0:64, t0 % NT:t0 % NT + 2048], ot[64:128, :], reads=[ob], adds=[o2buf])

    if do_ssd:
      with C.scope():
        ssd_piece(C, R, Rdt, O2, P, cf, b_cf, cb, b_cb, psA, psB, psC, psL, psM, o2buf)

    if do_attn:
        vt = C.sb([128, NBLK, 64], BF16, "vt"); b_vt = [Buf() for _ in range(16)]
        for bb in range(16):
            pt, pb = psA.next()
            for k in range(8):
                blk = bb * 8 + k
                S.op("pe", lambda e: e.matmul(pt[:, k * 64:(k + 1) * 64], lhsT=T1[64:128, blk * 128:(blk + 1) * 128],
                                              rhs=ident_b[64:128, 64:128], start=True, stop=True), [b_v[blk // 32], b_cb], [pb])
            S.op("dve", lambda e: e.tensor_copy(out=vt[:, bb * 8:(bb + 1) * 8, :].rearrange("p a b -> p (a b)"), in_=pt[:]),
                 [pb], [b_vt[bb]])
        E_r = Rot([(C.sb([128, 512], F32, "E"), Buf()) for _ in range(2)])
        SP_r = Rot([(C.sb([128, 512], F32, "SP"), Buf()) for _ in range(2)])
        SPb_r = Rot([(C.sb([128, 512], BF16, "SPb"), Buf()) for _ in range(2)])
        t1_r = Rot([(C.sb([128, 512], F32, "t1"), Buf()) for _ in range(2)])
        att_r = Rot([(C.sb([128, 512], BF16, "att"), Buf()) for _ in range(2)])
        lb_r = Rot([(C.sb([128, 128], BF16, "lbsb"), Buf()) for _ in range(2)])
        ost_r = Rot([(C.sb([64, 512], BF16, "ost"), Buf()) for _ in range(2)])
        ost = None
        for i in range(nq):
            ngrp = i // 4 + 1
            qs = slice(i * 128, (i + 1) * 128)
            bq = b_q[i // 32]
            ob_t, ob_b = psC.next()
            lb_t, lb_b = psL
            lbs = None
            nkb = i + 1
            kdone = 0
            for gidx in reversed(range(ngrp)):
                c0 = 4 * gidx
                n = min(4, i - c0 + 1)
                W = n * 128
                top = (gidx == ngrp - 1)
                z_t, z_b = psA.next()
                for k in range(n):
                    cblk = c0 + k
                    S.op("pe", lambda e: e.matmul(z_t[:, k * 128:(k + 1) * 128], lhsT=T2[0:64, 2 + cblk * 128:2 + (cblk + 1) * 128],
                                                  rhs=T1[0:64, qs], start=True, stop=True), [b_k[cblk // 32], bq], [z_b])
                e_t, e_b = E_r.next()
                sp_t, sp_b = SP_r.next()
                spb_t, spb_b = SPb_r.next()
                S.op("act", lambda e: e.activation(out=e_t[:, 0:W], in_=z_t[:, 0:W], func=AF.Exp), [z_b], [e_b])
                S.op("act", lambda e: e.activation(out=sp_t[:, 0:W], in_=e_t[:, 0:W], func=AF.Ln, bias=1.0, scale=1.0), [e_b], [sp_b])
                if top:
                    if n > 1:
                        S.op("pool", lambda e: e.tensor_copy(out=spb_t[:, 0:W - 128], in_=sp_t[:, 0:W - 128]), [sp_b], [spb_b])
                    S.op("pool", lambda e: e.tensor_tensor(out=spb_t[:, W - 128:W], in0=sp_t[:, W - 128:W], in1=M_f, op=ALU.mult),
                         [sp_b, b_cf], [spb_b])
                else:
                    S.op("pool", lambda e: e.tensor_copy(out=spb_t[:, 0:W], in_=sp_t[:, 0:W]), [sp_b], [spb_b])
                x_t, x_b = psB.next()
                for k in range(n):
                    nlast = (k == n - 1) and top
                    S.op("pe", lambda e: e.matmul(x_t[:, k * 128:(k + 1) * 128], lhsT=U_b, rhs=spb_t[:, k * 128:(k + 1) * 128],
                                                  start=True, stop=nlast), [spb_b, b_cb], [x_b])
                    for k2 in range(k + 1, n):
                        S.op("pe", lambda e: e.matmul(x_t[:, k * 128:(k + 1) * 128], lhsT=ones_b, rhs=spb_t[:, k2 * 128:(k2 + 1) * 128],
                                                      start=False, stop=(k2 == n - 1) and top), [spb_b, b_cb], [x_b])
                    if not top:
                        S.op("pe", lambda e: e.matmul(x_t[:, k * 128:(k + 1) * 128], lhsT=ident_b, rhs=lbs[0][:],
                                                      start=False, stop=True), [lbs[1], b_cb], [x_b])
                t1_t, t1_b = t1_r.next()
                S.op("dve", lambda e: e.tensor_tensor(out=t1_t[:, 0:W], in0=z_t[:, 0:W], in1=sp_t[:, 0:W], op=ALU.subtract), [z_b, sp_b], [t1_b])
                S.op("dve", lambda e: e.tensor_tensor(out=t1_t[:, 0:W], in0=t1_t[:, 0:W], in1=x_t[:, 0:W], op=ALU.subtract), [t1_b, x_b], [t1_b])
                a_t, a_b = att_r.next()
                S.op("act", lambda e: e.activation(out=a_t[:, 0:W], in_=t1_t[:, 0:W], func=AF.Exp), [t1_b], [a_b])
                if top:
                    S.op("pool", lambda e: e.tensor_tensor(out=a_t[:, W - 128:W], in0=a_t[:, W - 128:W], in1=M_f, op=ALU.mult),
                         [a_b, b_cf], [a_b])
                for k in range(n):
                    cblk = c0 + k
                    S.op("pe", lambda e: e.matmul(ob_t[0:64, 0:128], lhsT=vt[:, cblk, :], rhs=a_t[:, k * 128:(k + 1) * 128],
                                                  start=(kdone == 0), stop=(kdone == nkb - 1)), [b_vt[cblk // 8], a_b], [ob_b])
                    kdone += 1
                if gidx > 0:
                    for k in range(n):
                        S.op("pe", lambda e: e.matmul(lb_t[:, 0:128], lhsT=ones_b, rhs=spb_t[:, k * 128:(k + 1) * 128],
                                                      start=(top and k == 0), stop=(k == n - 1)), [spb_b, b_cb], [lb_b])
                    lbs = lb_r.next()
                    S.op("dve", lambda e: e.tensor_copy(out=lbs[0][:], in_=lb_t[:, 0:128]), [lb_b], [lbs[1]])
            if i % 4 == 0:
                ost = ost_r.next()
            S.op("dve", lambda e: e.tensor_copy(out=ost[0][:, (i % 4) * 128:(i % 4 + 1) * 128], in_=ob_t[0:64, 0:128]), [ob_b], [ost[1]])
            if i % 4 == 3 or i == nq - 1:
                i0 = (i // 4) * 4
                t0 = i0 * 128
                w = (i - i0 + 1) * 128
                S.dma("pool", O2[t0 // NT, 64:128, t0 % NT:t0 % NT + w], ost[0][:, 0:w], reads=[ost[1]], adds=[o2buf])
    return o2buf


def ssd_piece(C, R, Rdt, O2, P, cf, b_cf, cb, b_cb, psA, psB, psC, psL, psM, o2buf):
    S = C.S
    ident_f = cf[:, 0:128]; tri_f = cf[:, 128:256]; ones_f = cf[:, 256:384]; trirow0_f = cf[:, 128:384]
    neg_f = cf[:, 640:768]; sel127_f = cf[:, 768:896]
    ident_b = cb[:, 0:128]
    xpre = C.sb([128, 3 + SEQ], BF16, "xpre"); b_xp = [Buf() for _ in range(4)]
    bcpre = C.sb([128, 3 + SEQ], BF16, "bcpre"); b_bc = [Buf() for _ in range(4)]
    b_pad = Buf()
    S.op("pool", lambda e: e.memset(xpre[:, 0:3], 0.0), [], [b_pad])
    S.op("pool", lambda e: e.memset(bcpre[:, 0:3], 0.0), [], [b_pad])
    for j in range(4):
        S.dma("sp", xpre[:, 3 + j * NT:3 + (j + 1) * NT], R[j, 256:384, :], writes=[b_xp[j]])
        S.dma("sp", bcpre[:, 3 + j * NT:3 + (j + 1) * NT], R[j, 384:512, :], writes=[b_bc[j]])
    prm = C.sb([128, 16], F32, "prm"); b_prm = Buf()
    S.dma("sp", prm[:, 0:4], P["xw"], adds=[b_prm])
    S.dma("sp", prm[:, 4:5], P["xb"], adds=[b_prm])
    S.dma("sp", prm[:, 5:9], P["bcw"], adds=[b_prm])
    S.dma("sp", prm[:, 9:10], P["bcb"], adds=[b_prm])
    S.dma("sp", prm[:, 10:16], P["hp"], adds=[b_prm])
    cbt = C.sb([64, 1], F32, "cbt")
    S.dma("sp", cbt[:], P["cbias"], adds=[b_prm])
    rows = C.sb([1, 192], F32, "rows"); b_rows = Buf()
    S.dma("sp", rows[:, 0:128], P["xb_row"], adds=[b_rows])
    S.dma("sp", rows[:, 128:192], P["bb_row"], adds=[b_rows])
    rows_b = C.sb([1, 320], BF16, "rows_b"); b_rowsb = Buf()
    S.op("dve", lambda e: e.tensor_copy(out=rows_b[:, 0:192], in_=rows[:, :]), [b_rows], [b_rowsb])
    S.op("dve", lambda e: e.memset(rows_b[:, 192:320], 1.0), [], [b_rowsb])
    cm = C.sb([128, 4, 256], BF16, "cm"); b_cm = Buf()
    for k in range(4):
        S.op("dve", lambda e: e.tensor_scalar(out=cm[:, k, 0:128], in0=ident_f, scalar1=prm[:, k:k + 1], scalar2=None, op0=ALU.mult),
             [b_cf, b_prm], [b_cm])
        S.op("dve", lambda e: e.tensor_scalar(out=cm[:, k, 128:256], in0=ident_f, scalar1=prm[:, 5 + k:6 + k], scalar2=None, op0=ALU.mult),
             [b_cf, b_prm], [b_cm])
    aneg = C.sb([128, 2], F32, "aneg"); b_aneg = Buf()
    S.op("act", lambda e: e.activation(out=aneg[:], in_=prm[:, 12:14], func=AF.Exp), [b_prm], [b_aneg])
    S.op("dve", lambda e: e.tensor_scalar(out=aneg[:], in0=aneg[:], scalar1=-1.0, scalar2=None, op0=ALU.mult), [b_aneg], [b_aneg])

    dtr = Rot([(C.sb([2, 1024], F32, "dtr"), Buf()) for _ in range(2)])
    dtv = C.sb([128, NBLK, 2], F32, "dtv"); b_dt = Buf()
    av = C.sb([128, NBLK, 2], F32, "av"); b_a = Buf()
    pt, pb = psA.next()
    for pc in range(16):
        d_t, d_b = dtr.next()
        j = (pc * 1024) // NT
        o = (pc * 1024) % NT
        S.dma("sp", d_t[:], Rdt[j, :, o:o + 1024], writes=[d_b])
        for k in range(8):
            blk = pc * 8 + k
            S.op("pe", lambda e: e.matmul(pt[:, blk * 2:blk * 2 + 2], lhsT=d_t[:, k * 128:(k + 1) * 128], rhs=ident_f[0:2, 0:2],
                                          start=True, stop=True), [d_b, b_cf], [pb])
    dtf = dtv[:].rearrange("p a b -> p (a b)")
    S.op("dve", lambda e: e.tensor_tensor(out=dtv[:], in0=pt[:, 0:256].rearrange("p (a b) -> p a b", b=2),
                                          in1=prm[:, 10:12].unsqueeze(1).to_broadcast([128, NBLK, 2]), op=ALU.add), [pb, b_prm], [b_dt])
    S.op("act", lambda e: e.activation(out=dtf, in_=dtf, func=AF.Exp), [b_dt], [b_dt])
    S.op("act", lambda e: e.activation(out=dtf, in_=dtf, func=AF.Ln, bias=1.0, scale=1.0), [b_dt], [b_dt])
    S.op("dve", lambda e: e.tensor_tensor(out=av[:], in0=dtv[:], in1=aneg[:].unsqueeze(1).to_broadcast([128, NBLK, 2]), op=ALU.mult),
         [b_dt, b_aneg], [b_a])
    acs = C.sb([128, NBLK, 2], F32, "acs"); b_acs = Buf()
    nacs = C.sb([128, NBLK, 2], F32, "nacs")
    eacs = C.sb([128, NBLK, 2], F32, "eacs")
    wdec = C.sb([128, NBLK, 2], F32, "wdec")
    acl = C.sb([128, NCH, 2], F32, "acl")
    dch = C.sb([128, NCH, 2], F32, "dch")
    pt, pb = psA.next()
    for c in range(NCH):
        b0, b1 = 2 * c, 2 * c + 1
        S.op("pe", lambda e: e.matmul(pt[:, b0 * 2:b0 * 2 + 2], lhsT=tri_f, rhs=av[:, b0, :], start=True, stop=True), [b_a, b_cf], [pb])
        S.op("pe", lambda e: e.matmul(pt[:, b1 * 2:b1 * 2 + 2], lhsT=tri_f, rhs=av[:, b1, :], start=True, stop=False), [b_a, b_cf], [pb])
        S.op("pe", lambda e: e.matmul(pt[:, b1 * 2:b1 * 2 + 2], lhsT=ones_f, rhs=av[:, b0, :], start=False, stop=True), [b_a, b_cf], [pb])
    S.op("dve", lambda e: e.tensor_copy(out=acs[:].rearrange("p a b -> p (a b)"), in_=pt[:, 0:256]), [pb], [b_acs])
    S.op("dve", lambda e: e.tensor_scalar(out=nacs[:].rearrange("p a b -> p (a b)"), in0=acs[:].rearrange("p a b -> p (a b)"),
                                          scalar1=-1.0, scalar2=None, op0=ALU.mult), [b_acs], [b_acs])
    S.op("act", lambda e: e.activation(out=eacs[:].rearrange("p a b -> p (a b)"), in_=acs[:].rearrange("p a b -> p (a b)"), func=AF.Exp),
         [b_acs], [b_acs])
    pt2, pb2 = psB.next()
    acs_last = acs[:].rearrange("p (c t) h -> p c t h", t=2)[:, :, 1, :]
    S.op("dve", lambda e: e.tensor_copy(out=acl[:], in_=acs_last), [b_acs], [b_acs])
    S.op("pe", lambda e: e.matmul(pt2[:, 0:128], lhsT=sel127_f, rhs=acl[:].rearrange("p a b -> p (a b)"), start=True, stop=True),
         [b_acs, b_cf], [pb2])
    S.op("dve", lambda e: e.tensor_copy(out=acl[:].rearrange("p a b -> p (a b)"), in_=pt2[:, 0:128]), [pb2], [b_acs])
    S.op("act", lambda e: e.activation(out=dch[:].rearrange("p a b -> p (a b)"), in_=acl[:].rearrange("p a b -> p (a b)"), func=AF.Exp),
         [b_acs], [b_acs])
    wd4 = wdec[:].rearrange("p (c t) h -> p c t h", t=2)
    S.op("dve", lambda e: e.tensor_tensor(out=wd4, in0=acl[:].unsqueeze(2).to_broadcast([128, NCH, 2, 2]),
                                          in1=acs[:].rearrange("p (c t) h -> p c t h", t=2), op=ALU.subtract), [b_acs], [b_acs])
    S.op("act", lambda e: e.activation(out=wdec[:].rearrange("p a b -> p (a b)"), in_=wdec[:].rearrange("p a b -> p (a b)"), func=AF.Exp),
         [b_acs], [b_acs])

    prev32 = C.sb([64, 128], F32, "prev32"); b_p32 = Buf()
    prevb = C.sb([64, 128], BF16, "prevb"); b_pb = Buf()
    S.op("dve", lambda e: e.memset(prev32[:], 0.0), [], [b_p32])
    S.op("dve", lambda e: e.memset(prevb[:], 0.0), [], [b_pb])
    xs_r = Rot([(C.sb([128, 2, 128], F32, "xs"), Buf()) for _ in range(2)])
    xdt_r = Rot([(C.sb([128, 2, 128], BF16, "xdt"), Buf()) for _ in range(2)])
    xw_r = Rot([(C.sb([128, 2, 128], BF16, "xw"), Buf()) for _ in range(2)])
    btok_r = Rot([(C.sb([128, 2, 64], BF16, "btok"), Buf()) for _ in range(2)])
    bct_r = Rot([(C.sb([64, 2, 256], BF16, "bct"), Buf()) for _ in range(2)])
    at_r = Rot([(C.sb([128, 2, 256], F32, "aT"), Buf()) for _ in range(2)])
    lt_r = Rot([(C.sb([128, 384], F32, "LT"), Buf()) for _ in range(2)])
    st_r = Rot([(C.sb([128, 384], BF16, "ST"), Buf()) for _ in range(2)])
    yo_r = Rot([(C.sb([128, 2, 128], F32, "yo"), Buf()) for _ in range(2)])
    yt_r = Rot([(C.sb([128, 2, 128], BF16, "yt"), Buf()) for _ in range(2)])
    yst_r = Rot([(C.sb([128, 256], BF16, "yst"), Buf()) for _ in range(2)])
    for c in range(NCH):
        b0 = 2 * c
        tok0 = c * 256
        j = tok0 // NT
        rx = [b_xp[j], b_pad] + ([b_xp[j - 1]] if (j > 0 and tok0 % NT == 0) else [])
        rbc = [b_bc[j], b_pad] + ([b_bc[j - 1]] if (j > 0 and tok0 % NT == 0) else [])
        xs_t, xs_b = xs_r.next()
        px, pxb = psA.next()
        for t in range(2):
            for k in range(4):
                o = tok0 + t * 128 + k
                S.op("pe", lambda e: e.matmul(px[:, t * 128:(t + 1) * 128], lhsT=xpre[:, o:o + 128], rhs=cm[:, k, 0:128],
                                              start=(k == 0), stop=False), rx + [b_cm], [pxb])
            S.op("pe", lambda e: e.matmul(px[:, t * 128:(t + 1) * 128], lhsT=rows_b[:, 192:320], rhs=rows_b[:, 0:128],
                                          start=False, stop=True), [b_rowsb], [pxb])
        S.op("act", lambda e: e.activation(out=xs_t[:].rearrange("p a b -> p (a b)"), in_=px[:, 0:256], func=AF.Silu), [pxb], [xs_b])
        bt_t, bt_b = btok_r.next()
        pbt, pbtb = psB.next()
        for t in range(2):
            for k in range(4):
                o = tok0 + t * 128 + k
                S.op("pe", lambda e: e.matmul(pbt[:, t * 64:(t + 1) * 64], lhsT=bcpre[:, o:o + 128], rhs=cm[:, k, 128:192],
                                              start=(k == 0), stop=False), rbc + [b_cm], [pbtb])
            S.op("pe", lambda e: e.matmul(pbt[:, t * 64:(t + 1) * 64], lhsT=rows_b[:, 192:320], rhs=rows_b[:, 128:192],
                                          start=False, stop=True), [b_rowsb], [pbtb])
        S.op("act", lambda e: e.activation(out=bt_t[:].rearrange("p a b -> p (a b)"), in_=pbt[:, 0:128], func=AF.Silu), [pbtb], [bt_b])
        bct_t, bct_b = bct_r.next()
        pbc, pbcb = psC.next()
        for which in range(2):
            for k in range(4):
                o = tok0 + k
                S.op("pe", lambda e: e.matmul(pbc[0:64, which * 256:(which + 1) * 256], lhsT=cm[:, k, 128 + 64 * which:192 + 64 * which],
                                              rhs=bcpre[:, o:o + 256], start=(k == 0), stop=(k == 3)), rbc + [b_cm], [pbcb])
        S.op("act", lambda e: e.activation(out=bct_t[:, 0, :], in_=pbc[0:64, 0:256], func=AF.Silu, bias=prm[0:64, 9:10], scale=1.0),
             [pbcb, b_prm], [bct_b])
        S.op("act", lambda e: e.activation(out=bct_t[:, 1, :], in_=pbc[0:64, 256:512], func=AF.Silu, bias=cbt[:, 0:1], scale=1.0),
             [pbcb, b_prm], [bct_b])
        xdt_t, xdt_b = xdt_r.next()
        xw_t, xw_b = xw_r.next()
        for t in range(2):
            for h in range(2):
                S.op("dve", lambda e: e.tensor_scalar(out=xdt_t[:, t, h * 64:(h + 1) * 64], in0=xs_t[:, t, h * 64:(h + 1) * 64],
                                                      scalar1=dtv[:, b0 + t, h:h + 1], scalar2=None, op0=ALU.mult), [xs_b, b_dt], [xdt_b])
                S.op("dve", lambda e: e.tensor_scalar(out=xw_t[:, t, h * 64:(h + 1) * 64], in0=xs_t[:, t, h * 64:(h + 1) * 64],
                                                      scalar1=dtv[:, b0 + t, h:h + 1], scalar2=wdec[:, b0 + t, h:h + 1],
                                                      op0=ALU.mult, op1=ALU.mult), [xs_b, b_dt, b_acs], [xw_b])
        pcb, pcbb = psA.next()
        S.op("pe", lambda e: e.matmul(pcb[:, 0:256], lhsT=bct_t[:, 0, 0:128], rhs=bct_t[:, 1, 0:256], start=True, stop=True), [bct_b], [pcbb])
        S.op("pe", lambda e: e.matmul(pcb[:, 256:384], lhsT=bct_t[:, 0, 128:256], rhs=bct_t[:, 1, 128:256], start=True, stop=True), [bct_b], [pcbb])
        pyo, pyob = psB.next()
        for t in range(2):
            S.op("pe", lambda e: e.matmul(pyo[:, t * 128:(t + 1) * 128], lhsT=bct_t[:, 1, t * 128:(t + 1) * 128], rhs=prevb[:],
                                          start=True, stop=True), [bct_b, b_pb], [pyob])
        yo_t, yo_b = yo_r.next()
        for t in range(2):
            for h in range(2):
                S.op("act", lambda e: e.activation(out=yo_t[:, t, h * 64:(h + 1) * 64], in_=pyo[:, t * 128 + h * 64:t * 128 + (h + 1) * 64],
                                                   func=AF.Copy, scale=eacs[:, b0 + t, h:h + 1]), [pyob, b_acs], [yo_b])
        py, pyb = psC.next()
        for h in range(2):
            at_t, at_b = at_r.next()
            S.op("dve", lambda e: e.tensor_scalar(out=at_t[:, 0, :], in0=trirow0_f, scalar1=av[:, b0, h:h + 1], scalar2=None, op0=ALU.mult),
                 [b_cf, b_a], [at_b])
            S.op("dve", lambda e: e.tensor_scalar(out=at_t[:, 1, 0:128], in0=tri_f, scalar1=av[:, b0 + 1, h:h + 1], scalar2=None, op0=ALU.mult),
                 [b_cf, b_a], [at_b])
            pr, prb = psM
            S.op("pe", lambda e: e.matmul(pr[:, 0:256], lhsT=ones_f, rhs=at_t[:, 0, :], start=True, stop=False), [at_b, b_cf], [prb])
            S.op("pe", lambda e: e.matmul(pr[:, 128:256], lhsT=ones_f, rhs=at_t[:, 1, 0:128], start=False, stop=False), [at_b, b_cf], [prb])
            S.op("pe", lambda e: e.matmul(pr[:, 0:128], lhsT=ident_f, rhs=neg_f, start=False, stop=True), [b_cf], [prb])
            S.op("pe", lambda e: e.matmul(pr[:, 256:384], lhsT=ones_f, rhs=at_t[:, 0, 128:256], start=True, stop=False), [at_b, b_cf], [prb])
            S.op("pe", lambda e: e.matmul(pr[:, 256:384], lhsT=ones_f, rhs=at_t[:, 1, 0:128], start=False, stop=False), [at_b, b_cf], [prb])
            S.op("pe", lambda e: e.matmul(pr[:, 256:384], lhsT=ident_f, rhs=neg_f, start=False, stop=True), [b_cf], [prb])
            lt_t, lt_b = lt_r.next()
            S.op("act", lambda e: e.activation(out=lt_t[:, 0:256], in_=pr[:, 0:256], func=AF.Exp, bias=nacs[:, b0, h:h + 1], scale=1.0),
                 [prb, b_acs], [lt_b])
            S.op("act", lambda e: e.activation(out=lt_t[:, 256:384], in_=pr[:, 256:384], func=AF.Exp, bias=nacs[:, b0 + 1, h:h + 1], scale=1.0),
                 [prb, b_acs], [lt_b])
            st_t, st_b = st_r.next()
            S.op("dve", lambda e: e.tensor_tensor(out=st_t[:, 0:384], in0=pcb[:, 0:384], in1=lt_t[:, 0:384], op=ALU.mult), [pcbb, lt_b], [st_b])
            hs = slice(h * 64, (h + 1) * 64)
            S.op("pe", lambda e: e.matmul(py[:, h * 64:(h + 1) * 64], lhsT=st_t[:, 0:128], rhs=xdt_t[:, 0, hs], start=True, stop=True),
                 [st_b, xdt_b], [pyb])
            S.op("pe", lambda e: e.matmul(py[:, 128 + h * 64:128 + (h + 1) * 64], lhsT=st_t[:, 128:256], rhs=xdt_t[:, 0, hs], start=True, stop=False),
                 [st_b, xdt_b], [pyb])
            S.op("pe", lambda e: e.matmul(py[:, 128 + h * 64:128 + (h + 1) * 64], lhsT=st_t[:, 256:384], rhs=xdt_t[:, 1, hs], start=False, stop=True),
                 [st_b, xdt_b], [pyb])
        yt_t, yt_b = yt_r.next()
        S.op("dve", lambda e: e.tensor_tensor(out=yo_t[:].rearrange("p a b -> p (a b)"), in0=py[:, 0:256], in1=yo_t[:].rearrange("p a b -> p (a b)"),
                                              op=ALU.add), [pyb, yo_b], [yo_b])
        for h in range(2):
            S.op("dve", lambda e: e.scalar_tensor_tensor(out=yt_t[:, :, h * 64:(h + 1) * 64], in0=xs_t[:, :, h * 64:(h + 1) * 64],
                                                         scalar=prm[:, 14 + h:15 + h], in1=yo_t[:, :, h * 64:(h + 1) * 64],
                                                         op0=ALU.mult, op1=ALU.add), [xs_b, yo_b, b_prm], [yt_b])
        pT, pTb = psA.next()
        for t in range(2):
            S.op("pe", lambda e: e.matmul(pT[:, t * 128:(t + 1) * 128], lhsT=yt_t[:, t, :], rhs=ident_b, start=True, stop=True),
                 [yt_b, b_cb], [pTb])
        ys_t, ys_b = yst_r.next()
        S.op("act", lambda e: e.copy(out=ys_t[:], in_=pT[:, 0:256]), [pTb], [ys_b])
        S.dma("pool", O2[j, 128:256, tok0 % NT:tok0 % NT + 256], ys_t[:], reads=[ys_b], adds=[o2buf])
        pst, pstb = psB.next()
        for t in range(2):
            S.op("pe", lambda e: e.matmul(pst[0:64, 0:128], lhsT=bt_t[:, t, :], rhs=xw_t[:, t, :], start=(t == 0), stop=(t == 1)),
                 [bt_b, xw_b], [pstb])
        for h in range(2):
            S.op("dve", lambda e: e.scalar_tensor_tensor(out=prev32[:, h * 64:(h + 1) * 64], in0=prev32[:, h * 64:(h + 1) * 64],
                                                         scalar=dch[0:64, c, h:h + 1], in1=pst[0:64, h * 64:(h + 1) * 64],
                                                         op0=ALU.mult, op1=ALU.add), [b_p32, pstb, b_acs], [b_p32])
        S.op("dve", lambda e: e.tensor_copy(out=prevb[:], in_=prev32[:]), [b_p32], [b_pb])


def host_mix_params(inp, l, g):
    G = g // 2
    cw = inp["ssm_conv_w"][l]
    cbv = inp["ssm_conv_b"][l]
    f = lambda a: np.ascontiguousarray(a, dtype=np.float32)
    xsl = slice(g * 128, (g + 1) * 128)
    bsl = slice(512 + G * 64, 512 + (G + 1) * 64)
    csl = slice(640 + G * 64, 640 + (G + 1) * 64)
    hp = np.concatenate([inp["ssm_dt_bias"][l][2 * g:2 * g + 2], inp["ssm_a_log"][l][2 * g:2 * g + 2], inp["ssm_d"][l][2 * g:2 * g + 2]])
    return {
        "consts": host_consts(),
        "cwA": f(inp["sc_conv_w"][l][:, g * 64:(g + 1) * 64].T),
        "xw": f(cw[:, xsl].T),
        "xb": f(cbv[xsl][:, None]),
        "bcw": f(np.concatenate([cw[:, bsl], cw[:, csl]], axis=1).T),
        "bcb": f(np.concatenate([cbv[bsl], cbv[csl]])[:, None]),
        "cbias": f(cbv[csl][:, None]),
        "xb_row": f(cbv[xsl][None, :]),
        "bb_row": f(cbv[bsl][None, :]),
        "hp": f(np.broadcast_to(hp[None, :], (128, 6))),
    }


def tokb_weight_specs(W):
    return [
        ("winb", W["w_in"], 8, [[WIN_GROUPS[0]], [WIN_GROUPS[3]]] + [[g] for g in WIN_GROUPS[6:]]),
        ("wsc", W["w_sc_out"], 2, [[(0, 1024)]]),
        ("wsb", W["w_sb_out"], 2, [[(0, 1024)]]),
        ("wssm", W["w_ssm_out"], 4, [[(0, 1024)]]),
        ("wo", W["w_o"], 8, [[(0, 512)], [(512, 512)]]),
        ("wf1", W["w_ffn_in"], 8, [[(256 * j, 256), (FFH + 256 * j, 256)] for j in range(11)]),
        ("wf2", W["w_ffn_out"], NFC, [[(256 * j, 256)] for j in range(4)]),
    ]


def tok_b(C, T, xT_in, xT_out, Pin, wg, nw, ws):
    S = C.S
    xv = xT_in.rearrange("(kc p) t -> p kc t", p=128)
    ov = xT_out.rearrange("(kc p) t -> p kc t", p=128)
    outbuf = Buf()
    wsc = C.sb([128, 2, 1024], BF16, "wsc"); wsb = C.sb([128, 2, 1024], BF16, "wsb"); wssm = C.sb([128, 4, 1024], BF16, "wssm")
    b_wres = Buf()
    S.dma("sp", wsc[:], wg["wsc"][0][0], reads=[wg["wsc"][0][1]], adds=[b_wres])
    S.dma("sp", wsb[:], wg["wsb"][0][0], reads=[wg["wsb"][0][1]], adds=[b_wres])
    S.dma("sp", wssm[:], wg["wssm"][0][0], reads=[wg["wssm"][0][1]], adds=[b_wres])
    nwt = C.sb([128, 4], F32, "nwt")
    S.dma("sp", nwt[:], nw, adds=[b_wres])
    ya_in = C.sb([128, 2, TT], BF16, "ya_in"); b_yain = Buf()
    yb = C.sb([128, 2, TT], BF16, "yb"); b_yb = Buf()
    yc_in = C.sb([128, 4, TT], BF16, "yc_in"); b_ycin = Buf()
    ya = C.sb([128, 2, TT], BF16, "ya"); b_ya = Buf()
    gated = C.sb([128, 4, TT], F32, "gated"); b_gated = Buf()
    yc = C.sb([128, 4, TT], BF16, "yc"); b_yc = Buf()
    merged = C.sb([128, 8, TT], BF16, "merged"); b_merged = Buf()
    mo = C.sb([128, 8, TT], F32, "mo"); b_mo = Buf()
    act_a = C.sb([128, NFC, TT], BF16, "act_a"); b_acta = Buf()
    sig = Rot([(C.sb([128, TT], F32, "sig"), Buf()) for _ in range(3)])
    winb = wg["winb"]

    def post_norm_residual(gpcol):
        rms_stats(C, T, mo, b_mo, 8, T.ones_d)
        for m in range(8):
            tt, tb = T.rtmp.next()
            S.op("dve", lambda e: e.scalar_tensor_tensor(out=tt[:], in0=mo[:, m, :], scalar=T.vec[:, gpcol + m:gpcol + m + 1], in1=T.rstd[:],
                                                         op0=ALU.mult, op1=ALU.mult), [b_mo, T.b_vec, T.b_rstd], [tb])
            S.op("pool", lambda e: e.tensor_tensor(out=T.xt[:, m, :], in0=T.xt[:, m, :], in1=tt[:], op=ALU.add), [tb, T.b_xt], [T.b_xt])

    for ti in range(NTT):
        t0 = ti * TT
        S.dma("sp", T.xt[:], xv[:, :, t0:t0 + TT], writes=[T.b_xt])
        S.dma("sp", ya_in[0:64, 0, :], Pin[0, 0:64, t0:t0 + TT], writes=[b_yain])
        S.dma("sp", ya_in[64:128, 0, :], Pin[1, 0:64, t0:t0 + TT], adds=[b_yain])
        S.dma("sp", ya_in[0:64, 1, :], Pin[2, 0:64, t0:t0 + TT], adds=[b_yain])
        S.dma("sp", ya_in[64:128, 1, :], Pin[3, 0:64, t0:t0 + TT], adds=[b_yain])
        S.dma("sp", yb[0:64, 0, :], Pin[0, 64:128, t0:t0 + TT], writes=[b_yb])
        S.dma("sp", yb[64:128, 0, :], Pin[1, 64:128, t0:t0 + TT], adds=[b_yb])
        S.dma("sp", yb[0:64, 1, :], Pin[2, 64:128, t0:t0 + TT], adds=[b_yb])
        S.dma("sp", yb[64:128, 1, :], Pin[3, 64:128, t0:t0 + TT], adds=[b_yb])
        S.dma("sp", yc_in[:, 0, :], Pin[0, 128:256, t0:t0 + TT], writes=[b_ycin])
        for g in range(1, 4):
            S.dma("sp", yc_in[:, g, :], Pin[g, 128:256, t0:t0 + TT], adds=[b_ycin])
        norm_mod(C, T, 0, 8)
        w0, b0 = ws.load(winb[0])
        w3, b3 = ws.load(winb[1])
        for k in range(2):
            pp, ppb = proj_chunk(C, T, w0, b0, k * 128)
            S.op("dve", lambda e: e.tensor_tensor(out=ya[:, k, :], in0=pp[:], in1=ya_in[:, k, :], op=ALU.mult), [ppb, b_yain], [b_ya])
        for k in range(4):
            pp, ppb = proj_chunk(C, T, w3, b3, k * 128)
            st, sb_ = sig.next()
            S.op("act", lambda e: e.activation(out=st[:], in_=pp[:], func=AF.Silu), [ppb], [sb_])
            S.op("dve", lambda e: e.tensor_tensor(out=gated[:, k, :], in0=st[:], in1=yc_in[:, k, :], op=ALU.mult), [sb_, b_ycin], [b_gated])
        S.op("act", lambda e: e.activation(out=T.sq[:, 0:4, :], in_=gated[:], func=AF.Square), [b_gated], [T.b_sq])
        for grp in range(2):
            pt, pb = T.psum.next()
            for kk in range(2):
                S.op("pe", lambda e: e.matmul(pt[:], lhsT=T.ones_g[:], rhs=T.sq[:, 2 * grp + kk, :], start=(kk == 0), stop=(kk == 1)),
                     [T.b_sq, T.b_const], [pb])
            S.op("act", lambda e: e.activation(out=T.rstd[:], in_=pt[:], func=AF.Sqrt, bias=T.eps_t[:, 0:1], scale=1.0), [pb, T.b_const], [T.b_rstd])
            S.op("dve", lambda e: e.reciprocal(out=T.rstd[:], in_=T.rstd[:]), [T.b_rstd], [T.b_rstd])
            for kk in range(2):
                k = 2 * grp + kk
                S.op("dve", lambda e: e.scalar_tensor_tensor(out=yc[:, k, :], in0=gated[:, k, :], scalar=nwt[:, k:k + 1], in1=T.rstd[:],
                                                             op0=ALU.mult, op1=ALU.mult), [b_gated, b_wres, T.b_rstd], [b_yc])
        for half in range(2):
            wga, bga = ws.load(winb[2 + half])
            wgb, bgb = ws.load(winb[4 + half])
            wgc, bgc = ws.load(winb[6 + half])
            for mm in range(4):
                m = half * 4 + mm
                ms = slice(m * 128, (m + 1) * 128)
                acc = None
                for (wgt, bgt, ysrc, ybuf, nk, wres) in ((wga, bga, ya, b_ya, 2, wsc), (wgb, bgb, yb, b_yb, 2, wsb), (wgc, bgc, yc, b_yc, 4, wssm)):
                    pg, pgb = proj_chunk(C, T, wgt, bgt, mm * 128)
                    st, sb_ = sig.next()
                    S.op("act", lambda e: e.activation(out=st[:], in_=pg[:], func=AF.Sigmoid), [pgb], [sb_])
                    pbr, pbrb = T.psum.next()
                    for k in range(nk):
                        S.op("pe", lambda e: e.matmul(pbr[:], lhsT=wres[:, k, ms], rhs=ysrc[:, k, :], start=(k == 0), stop=(k == nk - 1)),
                             [b_wres, ybuf], [pbrb])
                    S.op("dve", lambda e: e.tensor_tensor(out=st[:], in0=st[:], in1=pbr[:], op=ALU.mult), [sb_, pbrb], [sb_])
                    if acc is None:
                        acc = (st, sb_)
                    else:
                        S.op("pool", lambda e: e.tensor_tensor(out=acc[0][:], in0=acc[0][:], in1=st[:], op=ALU.add), [acc[1], sb_], [acc[1]])
                S.op("pool", lambda e: e.tensor_copy(out=merged[:, m, :], in_=acc[0][:]), [acc[1]], [b_merged])
        for half in range(2):
            wo_v, wo_b = ws.load(wg["wo"][half])
            for mm in range(4):
                m = half * 4 + mm
                pt, pb = T.psum.next()
                for k in range(8):
                    S.op("pe", lambda e: e.matmul(pt[:], lhsT=wo_v[:, k, mm * 128:(mm + 1) * 128], rhs=merged[:, k, :], start=(k == 0), stop=(k == 7)),
                         [wo_b, b_merged], [pb])
                S.op("act", lambda e: e.copy(out=mo[:, m, :], in_=pt[:]), [pb], [b_mo])
        post_norm_residual(16)
        norm_mod(C, T, 24, 32)
        for j2 in range(11):
            wf, wfb = ws.load(wg["wf1"][j2])
            for jj in range(2):
                j = 2 * j2 + jj
                pgt, pgtb = proj_chunk(C, T, wf, wfb, jj * 128)
                pup, pupb = proj_chunk(C, T, wf, wfb, 256 + jj * 128)
                st, sb_ = sig.next()
                S.op("act", lambda e: e.activation(out=st[:], in_=pgt[:], func=AF.Silu), [pgtb], [sb_])
                S.op("dve", lambda e: e.tensor_tensor(out=act_a[:, j, :], in0=st[:], in1=pup[:], op=ALU.mult), [sb_, pupb], [b_acta])
        for mq in range(4):
            w2, w2b = ws.load(wg["wf2"][mq])
            for mm in range(2):
                m = mq * 2 + mm
                pt, pb = T.psum.next()
                for j in range(NFC):
                    S.op("pe", lambda e: e.matmul(pt[:], lhsT=w2[:, j, mm * 128:(mm + 1) * 128], rhs=act_a[:, j, :], start=(j == 0), stop=(j == NFC - 1)),
                         [w2b, b_acta], [pb])
                S.op("act", lambda e: e.copy(out=mo[:, m, :], in_=pt[:]), [pb], [b_mo])
        post_norm_residual(40)
        S.dma("pool", ov[:, :, t0:t0 + TT], T.xt[:], reads=[T.b_xt], adds=[outbuf])
    return outbuf


def host_tokb_params(inp, l):
    f = lambda a: np.ascontiguousarray(a, dtype=np.float32)
    return {"w_sc_out": f(inp["w_sc_out"][l]), "w_sb_out": f(inp["w_sb_out"][l]), "w_ssm_out": f(inp["w_ssm_out"][l]),
            "w_o": f(inp["w_o"][l]), "w_ffn_in": f(inp["w_ffn_in"][l]), "w_ffn_out": f(inp["w_ffn_out"][l]),
            "nw": f(inp["ssm_norm_w"][l].reshape(4, 128).T)}


def _di(nc, name, shape, dt=F32):
    return nc.dram_tensor(name, list(shape), dt, kind="ExternalInput").ap()


def _do(nc, name, shape, dt=F32):
    return nc.dram_tensor(name, list(shape), dt, kind="ExternalOutput").ap()


def _mod_aps(nc, sfx=""):
    return {"cT": _di(nc, "cT" + sfx, [128, 8]), "mod_w": _di(nc, "mod_w" + sfx, [1024, 6144]),
            "mod_b6": _di(nc, "mod_b6" + sfx, [128, 48]), "g4": _di(nc, "g4" + sfx, [128, 32])}


def _mix_aps(nc, sfx=""):
    d = lambda n, s: _di(nc, n + sfx, s)
    return {"consts": d("consts", [128, 896]), "cwA": d("cwA", [64, 3]), "xw": d("xw", [128, 4]), "xb": d("xb", [128, 1]),
            "bcw": d("bcw", [128, 4]), "bcb": d("bcb", [128, 1]), "cbias": d("cbias", [64, 1]),
            "xb_row": d("xb_row", [1, 128]), "bb_row": d("bb_row", [1, 64]), "hp": d("hp", [128, 6])}


def _tokb_w_aps(nc, sfx=""):
    d = lambda n, s: _di(nc, n + sfx, s)
    return {"w_in": d("w_in", [1024, 5896]), "w_sc_out": d("w_sc_out", [256, 1024]), "w_sb_out": d("w_sb_out", [256, 1024]),
            "w_ssm_out": d("w_ssm_out", [512, 1024]), "w_o": d("w_o", [1024, 1024]), "w_ffn_in": d("w_ffn_in", [1024, 2 * FFH]),
            "w_ffn_out": d("w_ffn_out", [FFH, 1024])}


def build_tok_a():
    nc = bass.Bass("TRN2", target_bir_lowering=False)
    with contextlib.ExitStack() as es:
        C = Ctx(nc, es)
        xT = _di(nc, "xT", [1024, NT])
        P = _mod_aps(nc)
        w_in = _di(nc, "w_in", [1024, 5896])
        send = _do(nc, "send", [4, 512, NT], BF16)
        send_dt = _do(nc, "send_dt", [4, 2, NT])
        wgs = prep_weights(C, [("win", w_in, 8, [[g] for g in WIN_GROUPS[:6]])])
        T = tok_setup(C, P)
        compute_mod(C, T, P)
        ws = WStream(C, nbuf=3)
        tok_a(C, T, xT, wgs["win"], send, send_dt, ws)
        C.S.finish()
    return nc


def build_mix():
    nc = bass.Bass("TRN2", target_bir_lowering=False)
    with contextlib.ExitStack() as es:
        C = Ctx(nc, es)
        R = _di(nc, "R", [4, 512, NT], BF16)
        Rdt = _di(nc, "Rdt", [4, 2, NT])
        O2 = _do(nc, "O2", [4, 256, NT], BF16)
        P = _mix_aps(nc)
        mix_piece(C, R, Rdt, O2, P)
        C.S.finish()
    return nc


def build_tok_b():
    nc = bass.Bass("TRN2", target_bir_lowering=False)
    with contextlib.ExitStack() as es:
        C = Ctx(nc, es)
        xT = _di(nc, "xT", [1024, NT])
        P = _mod_aps(nc)
        W = _tokb_w_aps(nc)
        nw = _di(nc, "nw", [128, 4])
        Pin = _di(nc, "Pin", [4, 256, NT], BF16)
        xo = _do(nc, "xo", [1024, NT])
        wgs = prep_weights(C, tokb_weight_specs(W))
        T = tok_setup(C, P)
        compute_mod(C, T, P)
        ws = WStream(C, nbuf=3)
        tok_b(C, T, xT, xo, Pin, wgs, nw, ws)
        C.S.finish()
    return nc


def host_mod_params(inp, l, b):
    f = lambda a: np.ascontiguousarray(a, dtype=np.float32)
    return {
        "cT": f(inp["c"][b].reshape(8, 128).T),
        "mod_w": f(inp["mod_w"][l]),
        "mod_b6": f(inp["mod_b"][l].reshape(48, 128).T),
        "g4": f(np.stack([inp["g_pre_mix"][l], inp["g_post_mix"][l], inp["g_pre_ffn"][l], inp["g_post_ffn"][l]]).reshape(32, 128).T),
    }


def kernel_unfused(**inp):
    inp = {k: np.asarray(v) for k, v in inp.items()}
    x = inp["x"]
    cores = list(range(8))
    xT = [np.ascontiguousarray(x[c // 4, (c % 4) * NT:(c % 4 + 1) * NT, :].T) for c in cores]
    nc_a, nc_m, nc_b = build_tok_a(), build_mix(), build_tok_b()
    for l in range(DEPTH):
        w_in = np.ascontiguousarray(inp["w_in"][l], dtype=np.float32)
        maps = []
        for c in cores:
            m = host_mod_params(inp, l, c // 4)
            m["xT"] = xT[c]
            m["w_in"] = w_in
            maps.append(m)
        ra = run_bass_kernel_spmd(nc_a, maps, core_ids=cores).results
        maps = []
        for c in cores:
            b, g = c // 4, c % 4
            m = host_mix_params(inp, l, g)
            m["R"] = np.stack([np.asarray(ra[4 * b + j]["send"])[g] for j in range(4)])
            m["Rdt"] = np.stack([np.asarray(ra[4 * b + j]["send_dt"])[g] for j in range(4)])
            maps.append(m)
        rm = run_bass_kernel_spmd(nc_m, maps, core_ids=cores).results
        del ra
        maps = []
        tb = host_tokb_params(inp, l)
        for c in cores:
            b, g = c // 4, c % 4
            m = host_mod_params(inp, l, b)
            m.update(tb)
            m["xT"] = xT[c]
            m["w_in"] = w_in
            m["Pin"] = np.stack([np.asarray(rm[4 * b + j]["O2"])[g] for j in range(4)])
            maps.append(m)
        rb = run_bass_kernel_spmd(nc_b, maps, core_ids=cores).results
        del rm
        xT = [np.ascontiguousarray(np.asarray(rb[c]["xo"])) for c in cores]
    out = np.empty((NB, SEQ, D), np.float32)
    for c in cores:
        out[c // 4, (c % 4) * NT:(c % 4 + 1) * NT, :] = xT[c].T
    return out


def kernel(**inp):
    return kernel_unfused(**inp)
```

```python
import contextlib
import numpy as np
import ml_dtypes
import concourse.bass as bass
import concourse.mybir as mybir
from concourse.bass_utils import run_bass_kernel_spmd

F32 = mybir.dt.float32
BF16 = mybir.dt.bfloat16
AF = mybir.ActivationFunctionType
ALU = mybir.AluOpType

D = 1024
SEQ = 16384
NB = 2
DEPTH = 2
NT = 4096
TT = 512
NTT = NT // TT
EPS = 1e-6
FFH = 2816
NFC = FFH // 128
SAME_SYNC = True
NDS = 24
NDS_POOL = 4


class Buf:
    __slots__ = ("name", "w", "r")

    def __init__(self, name=""):
        self.name = name
        self.w = {}
        self.r = {}


class Sched:
    def __init__(self, nc, es):
        self.nc = nc
        self.engs = {"pe": nc.tensor, "act": nc.scalar, "dve": nc.vector, "pool": nc.gpsimd, "sp": nc.sync}
        self.sems = []
        self.esem = {}
        self.ecnt = {}
        for k in ("pe", "act", "dve", "pool"):
            self.esem[k] = self._newsem(es, "e_" + k)
            self.ecnt[k] = 0
        self.dsem = {}
        self.dcnt = {}
        self.dnext = {}
        self.nds = {"sp": NDS, "pool": NDS_POOL, "act": 4}
        for q in ("sp", "pool", "act"):
            self.dsem[q] = [self._newsem(es, f"d_{q}{i}") for i in range(self.nds[q])]
            self.dcnt[q] = [0] * self.nds[q]
            self.dnext[q] = 0
        self.known = {k: {} for k in self.engs}
        self.nwaits = 0
        self.nins = 0

    def _newsem(self, es, name):
        self.sems.append(es.enter_context(self.nc.semaphore(name)))
        return len(self.sems) - 1

    def _wait(self, ek, si, val):
        k = self.known[ek]
        if k.get(si, 0) >= val:
            return
        self.engs[ek].wait_ge(self.sems[si], val)
        k[si] = val
        self.nwaits += 1

    def _deps(self, ek, reads, writes):
        need = {}
        for b in reads:
            for si, v in b.w.items():
                if need.get(si, 0) < v:
                    need[si] = v
        for b in writes:
            for si, v in b.w.items():
                if need.get(si, 0) < v:
                    need[si] = v
            for si, v in b.r.items():
                if need.get(si, 0) < v:
                    need[si] = v
        own = self.esem.get(ek)
        for si, v in need.items():
            if si == own and (ek == "pe" or not SAME_SYNC):
                continue
            self._wait(ek, si, v)

    def _mark(self, si, val, reads, writes):
        for b in reads:
            if b.r.get(si, 0) < val:
                b.r[si] = val
        for b in writes:
            b.w = {si: val}
            b.r = {}

    def op(self, ek, fn, reads=(), writes=()):
        self._deps(ek, reads, writes)
        ins = fn(self.engs[ek])
        self.ecnt[ek] += 1
        si = self.esem[ek]
        ins.then_inc(self.sems[si], 1)
        self._mark(si, self.ecnt[ek], reads, writes)
        self.nins += 1

    def dma(self, q, out, in_, reads=(), writes=(), adds=()):
        self._deps(q, reads, writes)
        i = self.dnext[q]
        self.dnext[q] = (i + 1) % self.nds[q]
        si = self.dsem[q][i]
        if self.dcnt[q][i] > 0:
            self._wait(q, si, self.dcnt[q][i])
        self.engs[q].dma_start(out=out, in_=in_).then_inc(self.sems[si], 16)
        self.dcnt[q][i] += 16
        self._mark(si, self.dcnt[q][i], reads, writes)
        for b in adds:
            b.w[si] = self.dcnt[q][i]
        self.nins += 1

    def barrier(self):
        for ek in self.engs:
            for q in self.dsem:
                for i, si in enumerate(self.dsem[q]):
                    if self.dcnt[q][i] > 0:
                        self._wait(ek, si, self.dcnt[q][i])
            for k, si in self.esem.items():
                if self.ecnt[k] > 0 and k != ek:
                    self._wait(ek, si, self.ecnt[k])

    def finish(self):
        for q in self.dsem:
            for i, si in enumerate(self.dsem[q]):
                if self.dcnt[q][i] > 0:
                    self._wait("sp", si, self.dcnt[q][i])
        for k, si in self.esem.items():
            if self.ecnt[k] > 0:
                self._wait("sp", si, self.ecnt[k])


class Ctx:
    def __init__(self, nc, es):
        self.nc = nc
        self.es = es
        self.S = Sched(nc, es)
        self.n = 0
        self.cur_es = None

    def sb(self, shape, dt, name=None, es=None):
        self.n += 1
        t = (es or self.cur_es or self.es).enter_context(self.nc.sbuf_tensor(f"{name or 't'}_{self.n}", list(shape), dt))
        return t

    @contextlib.contextmanager
    def scope(self):
        prev = self.cur_es
        with contextlib.ExitStack() as es:
            self.cur_es = es
            try:
                yield es
            finally:
                self.S.barrier()
                self.cur_es = prev

    def ps(self, shape=(128, 512), dt=F32, name=None):
        self.n += 1
        return self.es.enter_context(self.nc.psum_tensor(f"{name or 'p'}_{self.n}", list(shape), dt))

    def dram(self, name, shape, dt, kind="Internal"):
        return self.nc.dram_tensor(name, list(shape), dt, kind=kind).ap()


class Rot:
    def __init__(self, items):
        self.items = items
        self.i = 0

    def next(self):
        it = self.items[self.i]
        self.i = (self.i + 1) % len(self.items)
        return it


WIN_GROUPS = [(0, 512), (512, 512), (1024, 512), (1536, 512), (2048, 512), (2560, 264)] + \
             [(2824 + 512 * i, 512) for i in range(6)]


def prep_weights(C, specs):
    S = C.S
    nc = C.nc
    out = {}
    with contextlib.ExitStack() as es:
        st32 = [(C.sb([128, 5632], F32, "st32", es), Buf()) for _ in range(2)]
        st16 = [(C.sb([128, 5632], BF16, "st16", es), Buf()) for _ in range(2)]
        r32 = Rot(st32)
        r16 = Rot(st16)
        engs = ["pool", "dve", "act"]
        ei = 0
        for key, src, KC, groups in specs:
            srcv = src.rearrange("(kc p) n -> p kc n", p=128)
            lst = []
            for gi, pieces in enumerate(groups):
                GW = sum(p[1] for p in pieces)
                dst = C.dram(f"wb_{key}_{gi}", [128, KC, GW], BF16)
                dbuf = Buf()
                t32, b32 = r32.next()
                t16, b16 = r16.next()
                v32 = t32[:, 0:KC * GW].rearrange("p (k n) -> p k n", k=KC)
                v16 = t16[:, 0:KC * GW].rearrange("p (k n) -> p k n", k=KC)
                o = 0
                for (c0, ncol) in pieces:
                    S.dma("sp", v32[:, :, o:o + ncol], srcv[:, :, c0:c0 + ncol], writes=[b32])
                    o += ncol
                ek = engs[ei % 3]
                ei += 1
                if ek == "act":
                    S.op("act", lambda e: e.copy(out=t16[:, 0:KC * GW], in_=t32[:, 0:KC * GW]), [b32], [b16])
                else:
                    S.op(ek, lambda e: e.tensor_copy(out=t16[:, 0:KC * GW], in_=t32[:, 0:KC * GW]), [b32], [b16])
                S.dma("pool", dst, v16, reads=[b16], writes=[dbuf])
                lst.append((dst, dbuf))
            out[key] = lst
        S.barrier()
    return out


class WStream:
    def __init__(self, C, nbuf=3, elems=5632):
        self.C = C
        self.bufs = Rot([(C.sb([128, elems], BF16, "wst"), Buf()) for _ in range(nbuf)])

    def load(self, grp):
        dst, dbuf = grp
        KC, GW = dst.shape[1], dst.shape[2]
        t, b = self.bufs.next()
        v = t[:, 0:KC * GW].rearrange("p (k n) -> p k n", k=KC)
        self.C.S.dma("sp", v, dst, reads=[dbuf], writes=[b])
        return v, b


class TokState:
    pass


def tok_setup(C, l_params):
    S = C.S
    T = TokState()
    T.ones_d = C.sb([128, 128], BF16, "ones_d")
    T.ones_g = C.sb([128, 128], BF16, "ones_g")
    T.b_const = Buf()
    S.op("pool", lambda e: e.memset(T.ones_d[:], 1.0 / 1024.0), [], [T.b_const])
    S.op("pool", lambda e: e.memset(T.ones_g[:], 1.0 / 256.0), [], [T.b_const])
    T.eps_t = C.sb([128, 1], F32, "eps_t")
    S.op("pool", lambda e: e.memset(T.eps_t[:], EPS), [], [T.b_const])
    T.xt = C.sb([128, 8, TT], F32, "xt"); T.b_xt = Buf()
    T.hT = C.sb([128, 8, TT], BF16, "hT"); T.b_hT = Buf()
    T.sq = C.sb([128, 8, TT], BF16, "sq"); T.b_sq = Buf()
    T.rstd = C.sb([128, TT], F32, "rstd"); T.b_rstd = Buf()
    T.tmp = [(C.sb([128, TT], F32, "tmp"), Buf()) for _ in range(4)]
    T.rtmp = Rot(T.tmp)
    T.psum = Rot([(C.ps(), Buf()) for _ in range(8)])
    return T


def compute_mod(C, T, P, es_scope=None):
    S = C.S
    nc = C.nc
    T.vec = C.sb([128, 48], F32, "vec"); T.b_vec = Buf()
    with contextlib.ExitStack() as es:
        cT = C.sb([128, 8], F32, "cT", es); b_c = Buf()
        sc = C.sb([128, 8], F32, "sc", es); b_sc = Buf()
        mb = C.sb([128, 48], F32, "mb", es); b_mb = Buf()
        g4 = C.sb([128, 32], F32, "g4", es); b_g4 = Buf()
        modv = C.sb([128, 48], F32, "modv", es); b_modv = Buf()
        wbufs = Rot([(C.sb([128, 8, 512], F32, "mw", es), Buf()) for _ in range(2)])
        S.dma("sp", cT[:], P["cT"], writes=[b_c])
        S.dma("sp", mb[:], P["mod_b6"], writes=[b_mb])
        S.dma("sp", g4[:], P["g4"], writes=[b_g4])
        S.op("act", lambda e: e.activation(out=sc[:], in_=cT[:], func=AF.Silu), [b_c], [b_sc])
        mw = P["mod_w"].rearrange("(kc p) n -> p kc n", p=128)
        pt, pb = T.psum.next()
        for gi in range(12):
            wt, wb = wbufs.next()
            S.dma("sp", wt[:], mw[:, :, gi * 512:(gi + 1) * 512], writes=[wb])
            for fc in range(4):
                col = gi * 4 + fc
                for k in range(8):
                    S.op("pe", lambda e, k=k, fc=fc, col=col: e.matmul(
                        pt[:, col:col + 1], lhsT=wt[:, k, fc * 128:(fc + 1) * 128], rhs=sc[:, k:k + 1],
                        start=(k == 0), stop=(k == 7)), [wb, b_sc], [pb])
        S.op("dve", lambda e: e.tensor_tensor(out=modv[:], in0=pt[:, 0:48], in1=mb[:], op=ALU.add), [pb, b_mb], [b_modv])
        v = T.vec
        S.op("dve", lambda e: e.scalar_tensor_tensor(out=v[:, 0:8], in0=modv[:, 8:16], scalar=1.0, in1=g4[:, 0:8],
                                                     op0=ALU.add, op1=ALU.mult), [b_modv, b_g4], [T.b_vec])
        S.op("dve", lambda e: e.tensor_copy(out=v[:, 8:16], in_=modv[:, 0:8]), [b_modv], [T.b_vec])
        S.op("dve", lambda e: e.tensor_tensor(out=v[:, 16:24], in0=modv[:, 16:24], in1=g4[:, 8:16], op=ALU.mult), [b_modv, b_g4], [T.b_vec])
        S.op("dve", lambda e: e.scalar_tensor_tensor(out=v[:, 24:32], in0=modv[:, 32:40], scalar=1.0, in1=g4[:, 16:24],
                                                     op0=ALU.add, op1=ALU.mult), [b_modv, b_g4], [T.b_vec])
        S.op("dve", lambda e: e.tensor_copy(out=v[:, 32:40], in_=modv[:, 24:32]), [b_modv], [T.b_vec])
        S.op("dve", lambda e: e.tensor_tensor(out=v[:, 40:48], in0=modv[:, 40:48], in1=g4[:, 24:32], op=ALU.mult), [b_modv, b_g4], [T.b_vec])
        S.barrier()
    return T


def rms_stats(C, T, src_t, src_b, nch, ones_t):
    S = C.S
    S.op("act", lambda e: e.activation(out=T.sq[:, 0:nch, :], in_=src_t[:, 0:nch, :], func=AF.Square), [src_b], [T.b_sq])
    pt, pb = T.psum.next()
    for k in range(nch):
        S.op("pe", lambda e, k=k: e.matmul(pt[:], lhsT=ones_t[:], rhs=T.sq[:, k, :], start=(k == 0), stop=(k == nch - 1)),
             [T.b_sq, T.b_const], [pb])
    S.op("act", lambda e: e.activation(out=T.rstd[:], in_=pt[:], func=AF.Sqrt, bias=T.eps_t[:, 0:1], scale=1.0), [pb, T.b_const], [T.b_rstd])
    S.op("dve", lambda e: e.reciprocal(out=T.rstd[:], in_=T.rstd[:]), [T.b_rstd], [T.b_rstd])


def norm_mod(C, T, gcol, scol):
    S = C.S
    rms_stats(C, T, T.xt, T.b_xt, 8, T.ones_d)
    for k in range(8):
        tt, tb = T.rtmp.next()
        S.op("dve", lambda e, k=k, tt=tt: e.scalar_tensor_tensor(
            out=tt[:], in0=T.xt[:, k, :], scalar=T.vec[:, gcol + k:gcol + k + 1], in1=T.rstd[:],
            op0=ALU.mult, op1=ALU.mult), [T.b_xt, T.b_vec, T.b_rstd], [tb])
        S.op("act", lambda e, k=k, tt=tt: e.activation(
            out=T.hT[:, k, :], in_=tt[:], func=AF.Identity, bias=T.vec[:, scol + k:scol + k + 1], scale=1.0),
            [tb, T.b_vec], [T.b_hT])


def proj_chunk(C, T, wv, wb, c0, M=128):
    S = C.S
    pt, pb = T.psum.next()
    for k in range(8):
        S.op("pe", lambda e, k=k: e.matmul(pt[0:M, :], lhsT=wv[:, k, c0:c0 + M], rhs=T.hT[:, k, :],
                                           start=(k == 0), stop=(k == 7)), [wb, T.b_hT], [pb])
    return pt, pb


def tok_a(C, T, xT_in, wg, send, send_dt, ws):
    S = C.S
    xv = xT_in.rearrange("(kc p) t -> p kc t", p=128)
    stage = Rot([(C.sb([128, TT], BF16, "stg"), Buf()) for _ in range(4)])
    dts = Rot([(C.sb([8, TT], F32, "dts"), Buf()) for _ in range(2)])
    sb_send = Buf()
    for ti in range(NTT):
        t0 = ti * TT
        S.dma("sp", T.xt[:], xv[:, :, t0:t0 + TT], writes=[T.b_xt])
        norm_mod(C, T, 0, 8)

        def store_halves(st, sb_, row0, dest_of_half):
            for hf in range(2):
                S.dma("pool", send[dest_of_half[hf], row0:row0 + 64, t0:t0 + TT], st[hf * 64:(hf + 1) * 64, :],
                      reads=[sb_], adds=[sb_send])

        w0, b0 = ws.load(wg[0])
        w1, b1 = ws.load(wg[1])
        for i in range(2):
            pc, pcb = proj_chunk(C, T, w0, b0, 256 + i * 128)
            tt, tb = T.rtmp.next()
            S.op("act", lambda e, tt=tt, pc=pc: e.copy(out=tt[:], in_=pc[:]), [pcb], [tb])
            px, pxb = proj_chunk(C, T, w1, b1, i * 128)
            st, sb_ = stage.next()
            S.op("dve", lambda e, tt=tt, px=px, st=st: e.tensor_tensor(out=st[:], in0=px[:], in1=tt[:], op=ALU.mult),
                 [pxb, tb], [sb_])
            store_halves(st, sb_, 0, (2 * i, 2 * i + 1))
        w2, b2 = ws.load(wg[2])
        for i in range(2):
            pq, pqb = proj_chunk(C, T, w1, b1, 256 + i * 128)
            st, sb_ = stage.next()
            S.op("act", lambda e, pq=pq, st=st: e.activation(out=st[:], in_=pq[:], func=AF.Copy, scale=0.125), [pqb], [sb_])
            store_halves(st, sb_, 64, (2 * i, 2 * i + 1))
        w4, b4 = ws.load(wg[4])
        for j in range(2):
            for i in range(2):
                pk, pkb = proj_chunk(C, T, w2, b2, j * 256 + i * 128)
                st, sb_ = stage.next()
                S.op("act", lambda e, pk=pk, st=st: e.copy(out=st[:], in_=pk[:]), [pkb], [sb_])
                store_halves(st, sb_, 128 + 64 * j, (2 * i, 2 * i + 1))
        w5, b5 = ws.load(wg[5])
        for i in range(4):
            px, pxb = proj_chunk(C, T, w4, b4, i * 128)
            st, sb_ = stage.next()
            S.op("act", lambda e, px=px, st=st: e.copy(out=st[:], in_=px[:]), [pxb], [sb_])
            S.dma("pool", send[i, 256:384, t0:t0 + TT], st[:], reads=[sb_], adds=[sb_send])
        for j in range(2):
            pb_, pbb = proj_chunk(C, T, w5, b5, j * 128)
            st, sb_ = stage.next()
            S.op("act", lambda e, pb_=pb_, st=st: e.copy(out=st[:], in_=pb_[:]), [pbb], [sb_])
            for gi in range(2):
                for dd in range(2):
                    S.dma("pool", send[2 * gi + dd, 384 + 64 * j:448 + 64 * j, t0:t0 + TT], st[gi * 64:(gi + 1) * 64, :],
                          reads=[sb_], adds=[sb_send])
        pd, pdb = proj_chunk(C, T, w5, b5, 256, M=8)
        dt_t, dt_b = dts.next()
        S.op("act", lambda e, pd=pd, dt_t=dt_t: e.copy(out=dt_t[:], in_=pd[0:8, :]), [pdb], [dt_b])
        for g in range(4):
            S.dma("pool", send_dt[g, :, t0:t0 + TT], dt_t[2 * g:2 * g + 2, :], reads=[dt_b], adds=[sb_send])
    return sb_send


NBLK = SEQ // 128
NCH = SEQ // 256


def host_consts():
    c = np.zeros((128, 896), np.float32)
    i = np.arange(128)
    c[:, 0:128] = np.eye(128)
    c[:, 128:256] = (i[:, None] <= i[None, :])
    c[:, 256:384] = 1.0
    c[:, 384:512] = (i[:, None] > i[None, :])
    c[:, 512:640] = (i[:, None] < i[None, :])
    c[:, 640:768] = np.where(i[None, :] < i[:, None], -30000.0, 0.0)
    c[127, 768:896] = 1.0
    return c


def mix_piece(C, R, Rdt, O2, P, do_conv=True, do_attn=True, do_ssd=True, nq=NBLK):
    S = C.S
    nc = C.nc
    cf = C.sb([128, 896], F32, "cf"); b_cf = Buf()
    cb = C.sb([128, 640], BF16, "cb"); b_cb = Buf()
    S.dma("sp", cf[:], P["consts"], writes=[b_cf])
    S.op("dve", lambda e: e.tensor_copy(out=cb[:], in_=cf[:, 0:640]), [b_cf], [b_cb])
    ident_b = cb[:, 0:128]; tri_b = cb[:, 128:256]; ones_b = cb[:, 256:384]; U_b = cb[:, 384:512]
    M_f = cf[:, 512:640]
    o2buf = Buf()
    psA = Rot([(C.ps(), Buf()) for _ in range(2)])
    psB = Rot([(C.ps(), Buf()) for _ in range(2)])
    psC = Rot([(C.ps(), Buf()) for _ in range(2)])
    psL = (C.ps(), Buf())
    psM = (C.ps(), Buf())

    T1 = C.sb([128, SEQ], BF16, "T1")
    T2 = C.sb([128, 2 + SEQ], BF16, "T2")
    b_q = [Buf() for _ in range(4)]; b_v = [Buf() for _ in range(4)]
    b_k = [Buf() for _ in range(4)]; b_u = [Buf() for _ in range(4)]
    b_upad = Buf()
    S.op("pool", lambda e: e.memset(T2[64:128, 0:2], 0.0), [], [b_upad])
    for j in range(4):
        S.dma("sp", T1[0:64, j * NT:(j + 1) * NT], R[j, 64:128, :], writes=[b_q[j]])
        S.dma("sp", T2[0:64, 2 + j * NT:2 + (j + 1) * NT], R[j, 128:192, :], writes=[b_k[j]])
        S.dma("sp", T1[64:128, j * NT:(j + 1) * NT], R[j, 192:256, :], writes=[b_v[j]])
        S.dma("sp", T2[64:128, 2 + j * NT:2 + (j + 1) * NT], R[j, 0:64, :], writes=[b_u[j]])

    if do_conv:
      with C.scope():
        cw = C.sb([128, 3], F32, "cw"); b_cw = Buf()
        S.dma("sp", cw[64:128, :], P["cwA"], writes=[b_cw])
        tA = [(C.sb([128, 2048], F32, "tA"), Buf()) for _ in range(1)]
        oA = Rot([(C.sb([128, 2048], BF16, "oA"), Buf()) for _ in range(2)])
        for ch in range(8):
            t0 = ch * 2048
            j = t0 // NT
            rb = [b_u[j], b_cw, b_upad] + ([b_u[j - 1]] if (j > 0 and t0 % NT == 0) else [])
            ta, tb = tA[0]
            ot, ob = oA.next()
            S.op("dve", lambda e: e.tensor_scalar(out=ta[64:128, :], in0=T2[64:128, 2 + t0:2 + t0 + 2048], scalar1=cw[64:128, 2:3],
                                                   scalar2=None, op0=ALU.mult), rb, [tb])
            S.op("dve", lambda e: e.scalar_tensor_tensor(out=ta[64:128, :], in0=T2[64:128, 1 + t0:1 + t0 + 2048], scalar=cw[64:128, 1:2],
                                                          in1=ta[64:128, :], op0=ALU.mult, op1=ALU.add), rb + [tb], [tb])
            S.op("dve", lambda e: e.scalar_tensor_tensor(out=ot[64:128, :], in0=T2[64:128, t0:t0 + 2048], scalar=cw[64:128, 0:1],
                                                          in1=ta[64:128, :], op0=ALU.mult, op1=ALU.add), rb + [tb], [ob])
            S.dma("pool", O2[j, 0:64, t0 % NT:t0 % NT + 2048], ot[64:128, :], reads=[ob], adds=[o2buf])

    if do_ssd:
      with C.scope():
        ssd_piece(C, R, Rdt, O2, P, cf, b_cf, cb, b_cb, psA, psB, psC, psL, psM, o2buf)

    if do_attn:
        vt = C.sb([128, NBLK, 64], BF16, "vt"); b_vt = [Buf() for _ in range(16)]
        for bb in range(16):
            pt, pb = psA.next()
            for k in range(8):
                blk = bb * 8 + k
                S.op("pe", lambda e: e.matmul(pt[:, k * 64:(k + 1) * 64], lhsT=T1[64:128, blk * 128:(blk + 1) * 128],
                                              rhs=ident_b[64:128, 64:128], start=True, stop=True), [b_v[blk // 32], b_cb], [pb])
            S.op("dve", lambda e: e.tensor_copy(out=vt[:, bb * 8:(bb + 1) * 8, :].rearrange("p a b -> p (a b)"), in_=pt[:]),
                 [pb], [b_vt[bb]])
        NPIPE = 3
        zero_b = C.sb([128, 128], BF16, "zero_b"); b_zero = Buf()
        S.op("pool", lambda e: e.memset(zero_b[:], 0.0), [], [b_zero])
        M_b = C.sb([128, 128], BF16, "M_b")
        S.op("dve", lambda e: e.tensor_copy(out=M_b[:], in_=M_f), [b_cf], [b_zero])
        E_r = Rot([(C.sb([128, 512], F32, "E"), Buf()) for _ in range(2)])
        SPb_r = Rot([(C.sb([128, 512], BF16, "SPb"), Buf()) for _ in range(NPIPE)])
        t1_r = Rot([(C.sb([128, 512], F32, "t1"), Buf()) for _ in range(2)])
        att_r = Rot([(C.sb([128, 512], BF16, "att"), Buf()) for _ in range(2)])
        lb_r = Rot([(C.sb([128, 512], BF16, "lbsb"), Buf()) for _ in range(NPIPE + 1)])
        ost_r = Rot([(C.sb([64, 512], BF16, "ost"), Buf()) for _ in range(2)])
        z_r = Rot(psA.items + [psM])
        lb_t, lb_b = psL
        nquad = (nq + 3) // 4
        groups = []
        for m in range(nquad):
            for c in reversed(range(4 * m + 4)):
                r = max(0, c - 4 * m)
                groups.append(dict(m=m, c=c, r=r, diag=(c >= 4 * m), first=(c == 4 * m + 3), last=(c == 0)))
        state = {"lbs": None, "ob": None}

        def stage1(g):
            m, c, r = g["m"], g["c"], g["r"]
            lo = r * 128
            cs = slice(lo, 512)
            qcols = slice((4 * m + r) * 128, (4 * m + 4) * 128)
            bq = b_q[(4 * m) // 32]
            z_t, z_b = z_r.next()
            S.op("pe", lambda e: e.matmul(z_t[:, cs], lhsT=T2[0:64, 2 + c * 128:2 + (c + 1) * 128], rhs=T1[0:64, qcols],
                                          start=True, stop=True), [b_k[c // 32], bq], [z_b])
            e_t, e_b = E_r.next()
            spb_t, spb_b = SPb_r.next()
            S.op("act", lambda e: e.activation(out=e_t[:, cs], in_=z_t[:, cs], func=AF.Exp), [z_b], [e_b])
            S.op("act", lambda e: e.activation(out=spb_t[:, cs], in_=e_t[:, cs], func=AF.Ln, bias=1.0, scale=1.0), [e_b], [spb_b])
            if g["diag"]:
                S.op("pool", lambda e: e.tensor_tensor(out=spb_t[:, lo:lo + 128], in0=spb_t[:, lo:lo + 128], in1=M_b[:], op=ALU.mult),
                     [spb_b, b_zero], [spb_b])
            g.update(z=(z_t, z_b), spb=(spb_t, spb_b), lbs_in=state["lbs"])
            if g["first"]:
                S.op("pe", lambda e: e.matmul(lb_t[:, 0:512], lhsT=zero_b[:], rhs=T1[:, 0:512],
                                              start=True, stop=False), [b_zero, b_q[0], b_v[0]], [lb_b])
            if not g["last"]:
                S.op("pe", lambda e: e.matmul(lb_t[:, cs], lhsT=ones_b, rhs=spb_t[:, cs], start=False, stop=False), [spb_b, b_cb], [lb_b])
                lbs = lb_r.next()
                S.op("dve", lambda e: e.tensor_copy(out=lbs[0][:], in_=lb_t[:, 0:512]), [lb_b], [lbs[1]])
                state["lbs"] = lbs

        def stage2(g):
            m, c, r = g["m"], g["c"], g["r"]
            lo = r * 128
            cs = slice(lo, 512)
            z_t, z_b = g["z"]
            spb_t, spb_b = g["spb"]
            lbs = g["lbs_in"]
            if g["first"]:
                state["ob"] = psC.next()
                ob_t, ob_b = state["ob"]
                S.op("pe", lambda e: e.matmul(ob_t[0:64, 0:512], lhsT=zero_b[:, 0:64], rhs=T1[:, 0:512], start=True, stop=False),
                     [b_zero, b_q[0], b_v[0]], [ob_b])
            ob_t, ob_b = state["ob"]
            x_t, x_b = psB.next()
            has_later = not g["first"]
            S.op("pe", lambda e: e.matmul(x_t[:, cs], lhsT=U_b, rhs=spb_t[:, cs], start=True, stop=not has_later), [spb_b, b_cb], [x_b])
            if has_later:
                llo = lo + 128 if g["diag"] else lo
                S.op("pe", lambda e: e.matmul(x_t[:, llo:512], lhsT=ident_b, rhs=lbs[0][:, llo:512], start=False, stop=True),
                     [lbs[1], b_cb], [x_b])
            t1_t, t1_b = t1_r.next()
            S.op("dve", lambda e: e.tensor_tensor(out=t1_t[:, cs], in0=z_t[:, cs], in1=spb_t[:, cs], op=ALU.subtract), [z_b, spb_b], [t1_b])
            S.op("dve", lambda e: e.tensor_tensor(out=t1_t[:, cs], in0=t1_t[:, cs], in1=x_t[:, cs], op=ALU.subtract), [t1_b, x_b], [t1_b])
            a_t, a_b = att_r.next()
            S.op("act", lambda e: e.activation(out=a_t[:, cs], in_=t1_t[:, cs], func=AF.Exp), [t1_b], [a_b])
            if g["diag"]:
                S.op("pool", lambda e: e.tensor_tensor(out=a_t[:, lo:lo + 128], in0=a_t[:, lo:lo + 128], in1=M_b[:], op=ALU.mult),
                     [a_b, b_zero], [a_b])
            S.op("pe", lambda e: e.matmul(ob_t[0:64, cs], lhsT=vt[:, c, :], rhs=a_t[:, cs], start=False, stop=g["last"]),
                 [b_vt[c // 8], a_b], [ob_b])
            if g["last"]:
                ost = ost_r.next()
                S.op("dve", lambda e: e.tensor_copy(out=ost[0][:], in_=ob_t[0:64, 0:512]), [ob_b], [ost[1]])
                t0 = m * 512
                S.dma("pool", O2[t0 // NT, 64:128, t0 % NT:t0 % NT + 512], ost[0][:], reads=[ost[1]], adds=[o2buf])

        prev = None
        for g in groups + [None]:
            if g is not None:
                stage1(g)
            if prev is not None:
                stage2(prev)
            prev = g
    return o2buf


def ssd_piece(C, R, Rdt, O2, P, cf, b_cf, cb, b_cb, psA, psB, psC, psL, psM, o2buf):
    S = C.S
    ident_f = cf[:, 0:128]; tri_f = cf[:, 128:256]; ones_f = cf[:, 256:384]; trirow0_f = cf[:, 128:384]
    neg_f = cf[:, 640:768]; sel127_f = cf[:, 768:896]
    ident_b = cb[:, 0:128]
    xpre = C.sb([128, 3 + SEQ], BF16, "xpre"); b_xp = [Buf() for _ in range(4)]
    bcpre = C.sb([128, 3 + SEQ], BF16, "bcpre"); b_bc = [Buf() for _ in range(4)]
    b_pad = Buf()
    S.op("pool", lambda e: e.memset(xpre[:, 0:3], 0.0), [], [b_pad])
    S.op("pool", lambda e: e.memset(bcpre[:, 0:3], 0.0), [], [b_pad])
    for j in range(4):
        S.dma("sp", xpre[:, 3 + j * NT:3 + (j + 1) * NT], R[j, 256:384, :], writes=[b_xp[j]])
        S.dma("sp", bcpre[:, 3 + j * NT:3 + (j + 1) * NT], R[j, 384:512, :], writes=[b_bc[j]])
    prm = C.sb([128, 16], F32, "prm"); b_prm = Buf()
    S.dma("sp", prm[:, 0:4], P["xw"], adds=[b_prm])
    S.dma("sp", prm[:, 4:5], P["xb"], adds=[b_prm])
    S.dma("sp", prm[:, 5:9], P["bcw"], adds=[b_prm])
    S.dma("sp", prm[:, 9:10], P["bcb"], adds=[b_prm])
    S.dma("sp", prm[:, 10:16], P["hp"], adds=[b_prm])
    cbt = C.sb([64, 1], F32, "cbt")
    S.dma("sp", cbt[:], P["cbias"], adds=[b_prm])
    rows = C.sb([1, 192], F32, "rows"); b_rows = Buf()
    S.dma("sp", rows[:, 0:128], P["xb_row"], adds=[b_rows])
    S.dma("sp", rows[:, 128:192], P["bb_row"], adds=[b_rows])
    rows_b = C.sb([1, 320], BF16, "rows_b"); b_rowsb = Buf()
    S.op("dve", lambda e: e.tensor_copy(out=rows_b[:, 0:192], in_=rows[:, :]), [b_rows], [b_rowsb])
    S.op("dve", lambda e: e.memset(rows_b[:, 192:320], 1.0), [], [b_rowsb])
    cm = C.sb([128, 4, 256], BF16, "cm"); b_cm = Buf()
    for k in range(4):
        S.op("dve", lambda e: e.tensor_scalar(out=cm[:, k, 0:128], in0=ident_f, scalar1=prm[:, k:k + 1], scalar2=None, op0=ALU.mult),
             [b_cf, b_prm], [b_cm])
        S.op("dve", lambda e: e.tensor_scalar(out=cm[:, k, 128:256], in0=ident_f, scalar1=prm[:, 5 + k:6 + k], scalar2=None, op0=ALU.mult),
             [b_cf, b_prm], [b_cm])
    aneg = C.sb([128, 2], F32, "aneg"); b_aneg = Buf()
    S.op("act", lambda e: e.activation(out=aneg[:], in_=prm[:, 12:14], func=AF.Exp), [b_prm], [b_aneg])
    S.op("dve", lambda e: e.tensor_scalar(out=aneg[:], in0=aneg[:], scalar1=-1.0, scalar2=None, op0=ALU.mult), [b_aneg], [b_aneg])

    dtr = Rot([(C.sb([2, 1024], F32, "dtr"), Buf()) for _ in range(2)])
    dtv = C.sb([128, NBLK, 2], F32, "dtv"); b_dt = Buf()
    av = C.sb([128, NBLK, 2], F32, "av"); b_a = Buf()
    pt, pb = psA.next()
    for pc in range(16):
        d_t, d_b = dtr.next()
        j = (pc * 1024) // NT
        o = (pc * 1024) % NT
        S.dma("sp", d_t[:], Rdt[j, :, o:o + 1024], writes=[d_b])
        for k in range(8):
            blk = pc * 8 + k
            S.op("pe", lambda e: e.matmul(pt[:, blk * 2:blk * 2 + 2], lhsT=d_t[:, k * 128:(k + 1) * 128], rhs=ident_f[0:2, 0:2],
                                          start=True, stop=True), [d_b, b_cf], [pb])
    dtf = dtv[:].rearrange("p a b -> p (a b)")
    S.op("dve", lambda e: e.tensor_tensor(out=dtv[:], in0=pt[:, 0:256].rearrange("p (a b) -> p a b", b=2),
                                          in1=prm[:, 10:12].unsqueeze(1).to_broadcast([128, NBLK, 2]), op=ALU.add), [pb, b_prm], [b_dt])
    S.op("act", lambda e: e.activation(out=dtf, in_=dtf, func=AF.Exp), [b_dt], [b_dt])
    S.op("act", lambda e: e.activation(out=dtf, in_=dtf, func=AF.Ln, bias=1.0, scale=1.0), [b_dt], [b_dt])
    S.op("dve", lambda e: e.tensor_tensor(out=av[:], in0=dtv[:], in1=aneg[:].unsqueeze(1).to_broadcast([128, NBLK, 2]), op=ALU.mult),
         [b_dt, b_aneg], [b_a])
    acs = C.sb([128, NBLK, 2], F32, "acs"); b_acs = Buf()
    nacs = C.sb([128, NBLK, 2], F32, "nacs")
    eacs = C.sb([128, NBLK, 2], F32, "eacs")
    wdec = C.sb([128, NBLK, 2], F32, "wdec")
    acl = C.sb([128, NCH, 2], F32, "acl")
    dch = C.sb([128, NCH, 2], F32, "dch")
    pt, pb = psA.next()
    for c in range(NCH):
        b0, b1 = 2 * c, 2 * c + 1
        S.op("pe", lambda e: e.matmul(pt[:, b0 * 2:b0 * 2 + 2], lhsT=tri_f, rhs=av[:, b0, :], start=True, stop=True), [b_a, b_cf], [pb])
        S.op("pe", lambda e: e.matmul(pt[:, b1 * 2:b1 * 2 + 2], lhsT=tri_f, rhs=av[:, b1, :], start=True, stop=False), [b_a, b_cf], [pb])
        S.op("pe", lambda e: e.matmul(pt[:, b1 * 2:b1 * 2 + 2], lhsT=ones_f, rhs=av[:, b0, :], start=False, stop=True), [b_a, b_cf], [pb])
    S.op("dve", lambda e: e.tensor_copy(out=acs[:].rearrange("p a b -> p (a b)"), in_=pt[:, 0:256]), [pb], [b_acs])
    S.op("dve", lambda e: e.tensor_scalar(out=nacs[:].rearrange("p a b -> p (a b)"), in0=acs[:].rearrange("p a b -> p (a b)"),
                                          scalar1=-1.0, scalar2=None, op0=ALU.mult), [b_acs], [b_acs])
    S.op("act", lambda e: e.activation(out=eacs[:].rearrange("p a b -> p (a b)"), in_=acs[:].rearrange("p a b -> p (a b)"), func=AF.Exp),
         [b_acs], [b_acs])
    pt2, pb2 = psB.next()
    acs_last = acs[:].rearrange("p (c t) h -> p c t h", t=2)[:, :, 1, :]
    S.op("dve", lambda e: e.tensor_copy(out=acl[:], in_=acs_last), [b_acs], [b_acs])
    S.op("pe", lambda e: e.matmul(pt2[:, 0:128], lhsT=sel127_f, rhs=acl[:].rearrange("p a b -> p (a b)"), start=True, stop=True),
         [b_acs, b_cf], [pb2])
    S.op("dve", lambda e: e.tensor_copy(out=acl[:].rearrange("p a b -> p (a b)"), in_=pt2[:, 0:128]), [pb2], [b_acs])
    S.op("act", lambda e: e.activation(out=dch[:].rearrange("p a b -> p (a b)"), in_=acl[:].rearrange("p a b -> p (a b)"), func=AF.Exp),
         [b_acs], [b_acs])
    wd4 = wdec[:].rearrange("p (c t) h -> p c t h", t=2)
    S.op("dve", lambda e: e.tensor_tensor(out=wd4, in0=acl[:].unsqueeze(2).to_broadcast([128, NCH, 2, 2]),
                                          in1=acs[:].rearrange("p (c t) h -> p c t h", t=2), op=ALU.subtract), [b_acs], [b_acs])
    S.op("act", lambda e: e.activation(out=wdec[:].rearrange("p a b -> p (a b)"), in_=wdec[:].rearrange("p a b -> p (a b)"), func=AF.Exp),
         [b_acs], [b_acs])

    prev32 = C.sb([64, 128], F32, "prev32"); b_p32 = Buf()
    prevb = C.sb([64, 128], BF16, "prevb"); b_pb = Buf()
    S.op("dve", lambda e: e.memset(prev32[:], 0.0), [], [b_p32])
    S.op("dve", lambda e: e.memset(prevb[:], 0.0), [], [b_pb])
    xs_r = Rot([(C.sb([128, 2, 128], F32, "xs"), Buf()) for _ in range(2)])
    xdt_r = Rot([(C.sb([128, 2, 128], BF16, "xdt"), Buf()) for _ in range(2)])
    xw_r = Rot([(C.sb([128, 2, 128], BF16, "xw"), Buf()) for _ in range(2)])
    btok_r = Rot([(C.sb([128, 2, 64], BF16, "btok"), Buf()) for _ in range(2)])
    bct_r = Rot([(C.sb([64, 2, 256], BF16, "bct"), Buf()) for _ in range(2)])
    at_r = Rot([(C.sb([128, 2, 256], F32, "aT"), Buf()) for _ in range(2)])
    lt_r = Rot([(C.sb([128, 384], F32, "LT"), Buf()) for _ in range(2)])
    st_r = Rot([(C.sb([128, 384], BF16, "ST"), Buf()) for _ in range(2)])
    yo_r = Rot([(C.sb([128, 2, 128], F32, "yo"), Buf()) for _ in range(2)])
    yt_r = Rot([(C.sb([128, 2, 128], BF16, "yt"), Buf()) for _ in range(2)])
    yst_r = Rot([(C.sb([128, 256], BF16, "yst"), Buf()) for _ in range(2)])
    for c in range(NCH):
        b0 = 2 * c
        tok0 = c * 256
        j = tok0 // NT
        rx = [b_xp[j], b_pad] + ([b_xp[j - 1]] if (j > 0 and tok0 % NT == 0) else [])
        rbc = [b_bc[j], b_pad] + ([b_bc[j - 1]] if (j > 0 and tok0 % NT == 0) else [])
        xs_t, xs_b = xs_r.next()
        px, pxb = psA.next()
        for t in range(2):
            for k in range(4):
                o = tok0 + t * 128 + k
                S.op("pe", lambda e: e.matmul(px[:, t * 128:(t + 1) * 128], lhsT=xpre[:, o:o + 128], rhs=cm[:, k, 0:128],
                                              start=(k == 0), stop=False), rx + [b_cm], [pxb])
            S.op("pe", lambda e: e.matmul(px[:, t * 128:(t + 1) * 128], lhsT=rows_b[:, 192:320], rhs=rows_b[:, 0:128],
                                          start=False, stop=True), [b_rowsb], [pxb])
        S.op("act", lambda e: e.activation(out=xs_t[:].rearrange("p a b -> p (a b)"), in_=px[:, 0:256], func=AF.Silu), [pxb], [xs_b])
        bt_t, bt_b = btok_r.next()
        pbt, pbtb = psB.next()
        for t in range(2):
            for k in range(4):
                o = tok0 + t * 128 + k
                S.op("pe", lambda e: e.matmul(pbt[:, t * 64:(t + 1) * 64], lhsT=bcpre[:, o:o + 128], rhs=cm[:, k, 128:192],
                                              start=(k == 0), stop=False), rbc + [b_cm], [pbtb])
            S.op("pe", lambda e: e.matmul(pbt[:, t * 64:(t + 1) * 64], lhsT=rows_b[:, 192:320], rhs=rows_b[:, 128:192],
                                          start=False, stop=True), [b_rowsb], [pbtb])
        S.op("act", lambda e: e.activation(out=bt_t[:].rearrange("p a b -> p (a b)"), in_=pbt[:, 0:128], func=AF.Silu), [pbtb], [bt_b])
        bct_t, bct_b = bct_r.next()
        pbc, pbcb = psC.next()
        for which in range(2):
            for k in range(4):
                o = tok0 + k
                S.op("pe", lambda e: e.matmul(pbc[0:64, which * 256:(which + 1) * 256], lhsT=cm[:, k, 128 + 64 * which:192 + 64 * which],
                                              rhs=bcpre[:, o:o + 256], start=(k == 0), stop=(k == 3)), rbc + [b_cm], [pbcb])
        S.op("act", lambda e: e.activation(out=bct_t[:, 0, :], in_=pbc[0:64, 0:256], func=AF.Silu, bias=prm[0:64, 9:10], scale=1.0),
             [pbcb, b_prm], [bct_b])
        S.op("act", lambda e: e.activation(out=bct_t[:, 1, :], in_=pbc[0:64, 256:512], func=AF.Silu, bias=cbt[:, 0:1], scale=1.0),
             [pbcb, b_prm], [bct_b])
        xdt_t, xdt_b = xdt_r.next()
        xw_t, xw_b = xw_r.next()
        for t in range(2):
            for h in range(2):
                S.op("dve", lambda e: e.tensor_scalar(out=xdt_t[:, t, h * 64:(h + 1) * 64], in0=xs_t[:, t, h * 64:(h + 1) * 64],
                                                      scalar1=dtv[:, b0 + t, h:h + 1], scalar2=None, op0=ALU.mult), [xs_b, b_dt], [xdt_b])
                S.op("dve", lambda e: e.tensor_scalar(out=xw_t[:, t, h * 64:(h + 1) * 64], in0=xs_t[:, t, h * 64:(h + 1) * 64],
                                                      scalar1=dtv[:, b0 + t, h:h + 1], scalar2=wdec[:, b0 + t, h:h + 1],
                                                      op0=ALU.mult, op1=ALU.mult), [xs_b, b_dt, b_acs], [xw_b])
        pcb, pcbb = psA.next()
        S.op("pe", lambda e: e.matmul(pcb[:, 0:256], lhsT=bct_t[:, 0, 0:128], rhs=bct_t[:, 1, 0:256], start=True, stop=True), [bct_b], [pcbb])
        S.op("pe", lambda e: e.matmul(pcb[:, 256:384], lhsT=bct_t[:, 0, 128:256], rhs=bct_t[:, 1, 128:256], start=True, stop=True), [bct_b], [pcbb])
        pyo, pyob = psB.next()
        for t in range(2):
            S.op("pe", lambda e: e.matmul(pyo[:, t * 128:(t + 1) * 128], lhsT=bct_t[:, 1, t * 128:(t + 1) * 128], rhs=prevb[:],
                                          start=True, stop=True), [bct_b, b_pb], [pyob])
        yo_t, yo_b = yo_r.next()
        for t in range(2):
            for h in range(2):
                S.op("act", lambda e: e.activation(out=yo_t[:, t, h * 64:(h + 1) * 64], in_=pyo[:, t * 128 + h * 64:t * 128 + (h + 1) * 64],
                                                   func=AF.Copy, scale=eacs[:, b0 + t, h:h + 1]), [pyob, b_acs], [yo_b])
        py, pyb = psC.next()
        for h in range(2):
            at_t, at_b = at_r.next()
            S.op("dve", lambda e: e.tensor_scalar(out=at_t[:, 0, :], in0=trirow0_f, scalar1=av[:, b0, h:h + 1], scalar2=None, op0=ALU.mult),
                 [b_cf, b_a], [at_b])
            S.op("dve", lambda e: e.tensor_scalar(out=at_t[:, 1, 0:128], in0=tri_f, scalar1=av[:, b0 + 1, h:h + 1], scalar2=None, op0=ALU.mult),
                 [b_cf, b_a], [at_b])
            pr, prb = psM
            S.op("pe", lambda e: e.matmul(pr[:, 0:256], lhsT=ones_f, rhs=at_t[:, 0, :], start=True, stop=False), [at_b, b_cf], [prb])
            S.op("pe", lambda e: e.matmul(pr[:, 128:256], lhsT=ones_f, rhs=at_t[:, 1, 0:128], start=False, stop=False), [at_b, b_cf], [prb])
            S.op("pe", lambda e: e.matmul(pr[:, 0:128], lhsT=ident_f, rhs=neg_f, start=False, stop=True), [b_cf], [prb])
            S.op("pe", lambda e: e.matmul(pr[:, 256:384], lhsT=ones_f, rhs=at_t[:, 0, 128:256], start=True, stop=False), [at_b, b_cf], [prb])
            S.op("pe", lambda e: e.matmul(pr[:, 256:384], lhsT=ones_f, rhs=at_t[:, 1, 0:128], start=False, stop=False), [at_b, b_cf], [prb])
            S.op("pe", lambda e: e.matmul(pr[:, 256:384], lhsT=ident_f, rhs=neg_f, start=False, stop=True), [b_cf], [prb])
            lt_t, lt_b = lt_r.next()
            S.op("act", lambda e: e.activation(out=lt_t[:, 0:256], in_=pr[:, 0:256], func=AF.Exp, bias=nacs[:, b0, h:h + 1], scale=1.0),
                 [prb, b_acs], [lt_b])
            S.op("act", lambda e: e.activation(out=lt_t[:, 256:384], in_=pr[:, 256:384], func=AF.Exp, bias=nacs[:, b0 + 1, h:h + 1], scale=1.0),
                 [prb, b_acs], [lt_b])
            st_t, st_b = st_r.next()
            S.op("dve", lambda e: e.tensor_tensor(out=st_t[:, 0:384], in0=pcb[:, 0:384], in1=lt_t[:, 0:384], op=ALU.mult), [pcbb, lt_b], [st_b])
            hs = slice(h * 64, (h + 1) * 64)
            S.op("pe", lambda e: e.matmul(py[:, h * 64:(h + 1) * 64], lhsT=st_t[:, 0:128], rhs=xdt_t[:, 0, hs], start=True, stop=True),
                 [st_b, xdt_b], [pyb])
            S.op("pe", lambda e: e.matmul(py[:, 128 + h * 64:128 + (h + 1) * 64], lhsT=st_t[:, 128:256], rhs=xdt_t[:, 0, hs], start=True, stop=False),
                 [st_b, xdt_b], [pyb])
            S.op("pe", lambda e: e.matmul(py[:, 128 + h * 64:128 + (h + 1) * 64], lhsT=st_t[:, 256:384], rhs=xdt_t[:, 1, hs], start=False, stop=True),
                 [st_b, xdt_b], [pyb])
        yt_t, yt_b = yt_r.next()
        S.op("dve", lambda e: e.tensor_tensor(out=yo_t[:].rearrange("p a b -> p (a b)"), in0=py[:, 0:256], in1=yo_t[:].rearrange("p a b -> p (a b)"),
                                              op=ALU.add), [pyb, yo_b], [yo_b])
        for h in range(2):
            S.op("dve", lambda e: e.scalar_tensor_tensor(out=yt_t[:, :, h * 64:(h + 1) * 64], in0=xs_t[:, :, h * 64:(h + 1) * 64],
                                                         scalar=prm[:, 14 + h:15 + h], in1=yo_t[:, :, h * 64:(h + 1) * 64],
                                                         op0=ALU.mult, op1=ALU.add), [xs_b, yo_b, b_prm], [yt_b])
        pT, pTb = psA.next()
        for t in range(2):
            S.op("pe", lambda e: e.matmul(pT[:, t * 128:(t + 1) * 128], lhsT=yt_t[:, t, :], rhs=ident_b, start=True, stop=True),
                 [yt_b, b_cb], [pTb])
        ys_t, ys_b = yst_r.next()
        S.op("act", lambda e: e.copy(out=ys_t[:], in_=pT[:, 0:256]), [pTb], [ys_b])
        S.dma("pool", O2[j, 128:256, tok0 % NT:tok0 % NT + 256], ys_t[:], reads=[ys_b], adds=[o2buf])
        pst, pstb = psB.next()
        for t in range(2):
            S.op("pe", lambda e: e.matmul(pst[0:64, 0:128], lhsT=bt_t[:, t, :], rhs=xw_t[:, t, :], start=(t == 0), stop=(t == 1)),
                 [bt_b, xw_b], [pstb])
        for h in range(2):
            S.op("dve", lambda e: e.scalar_tensor_tensor(out=prev32[:, h * 64:(h + 1) * 64], in0=prev32[:, h * 64:(h + 1) * 64],
                                                         scalar=dch[0:64, c, h:h + 1], in1=pst[0:64, h * 64:(h + 1) * 64],
                                                         op0=ALU.mult, op1=ALU.add), [b_p32, pstb, b_acs], [b_p32])
        S.op("dve", lambda e: e.tensor_copy(out=prevb[:], in_=prev32[:]), [b_p32], [b_pb])


def host_mix_params(inp, l, g):
    G = g // 2
    cw = inp["ssm_conv_w"][l]
    cbv = inp["ssm_conv_b"][l]
    f = lambda a: np.ascontiguousarray(a, dtype=np.float32)
    xsl = slice(g * 128, (g + 1) * 128)
    bsl = slice(512 + G * 64, 512 + (G + 1) * 64)
    csl = slice(640 + G * 64, 640 + (G + 1) * 64)
    hp = np.concatenate([inp["ssm_dt_bias"][l][2 * g:2 * g + 2], inp["ssm_a_log"][l][2 * g:2 * g + 2], inp["ssm_d"][l][2 * g:2 * g + 2]])
    return {
        "consts": host_consts(),
        "cwA": f(inp["sc_conv_w"][l][:, g * 64:(g + 1) * 64].T),
        "xw": f(cw[:, xsl].T),
        "xb": f(cbv[xsl][:, None]),
        "bcw": f(np.concatenate([cw[:, bsl], cw[:, csl]], axis=1).T),
        "bcb": f(np.concatenate([cbv[bsl], cbv[csl]])[:, None]),
        "cbias": f(cbv[csl][:, None]),
        "xb_row": f(cbv[xsl][None, :]),
        "bb_row": f(cbv[bsl][None, :]),
        "hp": f(np.broadcast_to(hp[None, :], (128, 6))),
    }


def tokb_weight_specs(W):
    return [
        ("winb", W["w_in"], 8, [[WIN_GROUPS[0]], [WIN_GROUPS[3]]] + [[g] for g in WIN_GROUPS[6:]]),
        ("wsc", W["w_sc_out"], 2, [[(0, 1024)]]),
        ("wsb", W["w_sb_out"], 2, [[(0, 1024)]]),
        ("wssm", W["w_ssm_out"], 4, [[(0, 1024)]]),
        ("wo", W["w_o"], 8, [[(0, 512)], [(512, 512)]]),
        ("wf1", W["w_ffn_in"], 8, [[(256 * j, 256), (FFH + 256 * j, 256)] for j in range(11)]),
        ("wf2", W["w_ffn_out"], NFC, [[(256 * j, 256)] for j in range(4)]),
    ]


def tok_b(C, T, xT_in, xT_out, Pin, wg, nw, ws):
    S = C.S
    xv = xT_in.rearrange("(kc p) t -> p kc t", p=128)
    ov = xT_out.rearrange("(kc p) t -> p kc t", p=128)
    outbuf = Buf()
    wsc = C.sb([128, 2, 1024], BF16, "wsc"); wsb = C.sb([128, 2, 1024], BF16, "wsb"); wssm = C.sb([128, 4, 1024], BF16, "wssm")
    b_wres = Buf()
    S.dma("sp", wsc[:], wg["wsc"][0][0], reads=[wg["wsc"][0][1]], adds=[b_wres])
    S.dma("sp", wsb[:], wg["wsb"][0][0], reads=[wg["wsb"][0][1]], adds=[b_wres])
    S.dma("sp", wssm[:], wg["wssm"][0][0], reads=[wg["wssm"][0][1]], adds=[b_wres])
    nwt = C.sb([128, 4], F32, "nwt")
    S.dma("sp", nwt[:], nw, adds=[b_wres])
    ya_in = C.sb([128, 2, TT], BF16, "ya_in"); b_yain = Buf()
    yb = C.sb([128, 2, TT], BF16, "yb"); b_yb = Buf()
    yc_in = C.sb([128, 4, TT], BF16, "yc_in"); b_ycin = Buf()
    ya = C.sb([128, 2, TT], BF16, "ya"); b_ya = Buf()
    gated = C.sb([128, 4, TT], F32, "gated"); b_gated = Buf()
    yc = C.sb([128, 4, TT], BF16, "yc"); b_yc = Buf()
    merged = C.sb([128, 8, TT], BF16, "merged"); b_merged = Buf()
    mo = C.sb([128, 8, TT], F32, "mo"); b_mo = Buf()
    act_a = C.sb([128, NFC, TT], BF16, "act_a"); b_acta = Buf()
    sig = Rot([(C.sb([128, TT], F32, "sig"), Buf()) for _ in range(3)])
    winb = wg["winb"]

    def post_norm_residual(gpcol):
        rms_stats(C, T, mo, b_mo, 8, T.ones_d)
        for m in range(8):
            tt, tb = T.rtmp.next()
            S.op("dve", lambda e: e.scalar_tensor_tensor(out=tt[:], in0=mo[:, m, :], scalar=T.vec[:, gpcol + m:gpcol + m + 1], in1=T.rstd[:],
                                                         op0=ALU.mult, op1=ALU.mult), [b_mo, T.b_vec, T.b_rstd], [tb])
            S.op("pool", lambda e: e.tensor_tensor(out=T.xt[:, m, :], in0=T.xt[:, m, :], in1=tt[:], op=ALU.add), [tb, T.b_xt], [T.b_xt])

    for ti in range(NTT):
        t0 = ti * TT
        S.dma("sp", T.xt[:], xv[:, :, t0:t0 + TT], writes=[T.b_xt])
        S.dma("sp", ya_in[0:64, 0, :], Pin[0, 0:64, t0:t0 + TT], writes=[b_yain])
        S.dma("sp", ya_in[64:128, 0, :], Pin[1, 0:64, t0:t0 + TT], adds=[b_yain])
        S.dma("sp", ya_in[0:64, 1, :], Pin[2, 0:64, t0:t0 + TT], adds=[b_yain])
        S.dma("sp", ya_in[64:128, 1, :], Pin[3, 0:64, t0:t0 + TT], adds=[b_yain])
        S.dma("sp", yb[0:64, 0, :], Pin[0, 64:128, t0:t0 + TT], writes=[b_yb])
        S.dma("sp", yb[64:128, 0, :], Pin[1, 64:128, t0:t0 + TT], adds=[b_yb])
        S.dma("sp", yb[0:64, 1, :], Pin[2, 64:128, t0:t0 + TT], adds=[b_yb])
        S.dma("sp", yb[64:128, 1, :], Pin[3, 64:128, t0:t0 + TT], adds=[b_yb])
        S.dma("sp", yc_in[:, 0, :], Pin[0, 128:256, t0:t0 + TT], writes=[b_ycin])
        for g in range(1, 4):
            S.dma("sp", yc_in[:, g, :], Pin[g, 128:256, t0:t0 + TT], adds=[b_ycin])
        norm_mod(C, T, 0, 8)
        w0, b0 = ws.load(winb[0])
        w3, b3 = ws.load(winb[1])
        for k in range(2):
            pp, ppb = proj_chunk(C, T, w0, b0, k * 128)
            S.op("dve", lambda e: e.tensor_tensor(out=ya[:, k, :], in0=pp[:], in1=ya_in[:, k, :], op=ALU.mult), [ppb, b_yain], [b_ya])
        for k in range(4):
            pp, ppb = proj_chunk(C, T, w3, b3, k * 128)
            st, sb_ = sig.next()
            S.op("act", lambda e: e.activation(out=st[:], in_=pp[:], func=AF.Silu), [ppb], [sb_])
            S.op("dve", lambda e: e.tensor_tensor(out=gated[:, k, :], in0=st[:], in1=yc_in[:, k, :], op=ALU.mult), [sb_, b_ycin], [b_gated])
        S.op("act", lambda e: e.activation(out=T.sq[:, 0:4, :], in_=gated[:], func=AF.Square), [b_gated], [T.b_sq])
        for grp in range(2):
            pt, pb = T.psum.next()
            for kk in range(2):
                S.op("pe", lambda e: e.matmul(pt[:], lhsT=T.ones_g[:], rhs=T.sq[:, 2 * grp + kk, :], start=(kk == 0), stop=(kk == 1)),
                     [T.b_sq, T.b_const], [pb])
            S.op("act", lambda e: e.activation(out=T.rstd[:], in_=pt[:], func=AF.Sqrt, bias=T.eps_t[:, 0:1], scale=1.0), [pb, T.b_const], [T.b_rstd])
            S.op("dve", lambda e: e.reciprocal(out=T.rstd[:], in_=T.rstd[:]), [T.b_rstd], [T.b_rstd])
            for kk in range(2):
                k = 2 * grp + kk
                S.op("dve", lambda e: e.scalar_tensor_tensor(out=yc[:, k, :], in0=gated[:, k, :], scalar=nwt[:, k:k + 1], in1=T.rstd[:],
                                                             op0=ALU.mult, op1=ALU.mult), [b_gated, b_wres, T.b_rstd], [b_yc])
        for half in range(2):
            wga, bga = ws.load(winb[2 + half])
            wgb, bgb = ws.load(winb[4 + half])
            wgc, bgc = ws.load(winb[6 + half])
            for mm in range(4):
                m = half * 4 + mm
                ms = slice(m * 128, (m + 1) * 128)
                acc = None
                for (wgt, bgt, ysrc, ybuf, nk, wres) in ((wga, bga, ya, b_ya, 2, wsc), (wgb, bgb, yb, b_yb, 2, wsb), (wgc, bgc, yc, b_yc, 4, wssm)):
                    pg, pgb = proj_chunk(C, T, wgt, bgt, mm * 128)
                    st, sb_ = sig.next()
                    S.op("act", lambda e: e.activation(out=st[:], in_=pg[:], func=AF.Sigmoid), [pgb], [sb_])
                    pbr, pbrb = T.psum.next()
                    for k in range(nk):
                        S.op("pe", lambda e: e.matmul(pbr[:], lhsT=wres[:, k, ms], rhs=ysrc[:, k, :], start=(k == 0), stop=(k == nk - 1)),
                             [b_wres, ybuf], [pbrb])
                    S.op("dve", lambda e: e.tensor_tensor(out=st[:], in0=st[:], in1=pbr[:], op=ALU.mult), [sb_, pbrb], [sb_])
                    if acc is None:
                        acc = (st, sb_)
                    else:
                        S.op("pool", lambda e: e.tensor_tensor(out=acc[0][:], in0=acc[0][:], in1=st[:], op=ALU.add), [acc[1], sb_], [acc[1]])
                S.op("pool", lambda e: e.tensor_copy(out=merged[:, m, :], in_=acc[0][:]), [acc[1]], [b_merged])
        for half in range(2):
            wo_v, wo_b = ws.load(wg["wo"][half])
            for mm in range(4):
                m = half * 4 + mm
                pt, pb = T.psum.next()
                for k in range(8):
                    S.op("pe", lambda e: e.matmul(pt[:], lhsT=wo_v[:, k, mm * 128:(mm + 1) * 128], rhs=merged[:, k, :], start=(k == 0), stop=(k == 7)),
                         [wo_b, b_merged], [pb])
                S.op("act", lambda e: e.copy(out=mo[:, m, :], in_=pt[:]), [pb], [b_mo])
        post_norm_residual(16)
        norm_mod(C, T, 24, 32)
        for j2 in range(11):
            wf, wfb = ws.load(wg["wf1"][j2])
            for jj in range(2):
                j = 2 * j2 + jj
                pgt, pgtb = proj_chunk(C, T, wf, wfb, jj * 128)
                pup, pupb = proj_chunk(C, T, wf, wfb, 256 + jj * 128)
                st, sb_ = sig.next()
                S.op("act", lambda e: e.activation(out=st[:], in_=pgt[:], func=AF.Silu), [pgtb], [sb_])
                S.op("dve", lambda e: e.tensor_tensor(out=act_a[:, j, :], in0=st[:], in1=pup[:], op=ALU.mult), [sb_, pupb], [b_acta])
        for mq in range(4):
            w2, w2b = ws.load(wg["wf2"][mq])
            for mm in range(2):
                m = mq * 2 + mm
                pt, pb = T.psum.next()
                for j in range(NFC):
                    S.op("pe", lambda e: e.matmul(pt[:], lhsT=w2[:, j, mm * 128:(mm + 1) * 128], rhs=act_a[:, j, :], start=(j == 0), stop=(j == NFC - 1)),
                         [w2b, b_acta], [pb])
                S.op("act", lambda e: e.copy(out=mo[:, m, :], in_=pt[:]), [pb], [b_mo])
        post_norm_residual(40)
        S.dma("pool", ov[:, :, t0:t0 + TT], T.xt[:], reads=[T.b_xt], adds=[outbuf])
    return outbuf


def host_tokb_params(inp, l):
    f = lambda a: np.ascontiguousarray(a, dtype=np.float32)
    return {"w_sc_out": f(inp["w_sc_out"][l]), "w_sb_out": f(inp["w_sb_out"][l]), "w_ssm_out": f(inp["w_ssm_out"][l]),
            "w_o": f(inp["w_o"][l]), "w_ffn_in": f(inp["w_ffn_in"][l]), "w_ffn_out": f(inp["w_ffn_out"][l]),
            "nw": f(inp["ssm_norm_w"][l].reshape(4, 128).T)}


def _di(nc, name, shape, dt=F32):
    return nc.dram_tensor(name, list(shape), dt, kind="ExternalInput").ap()


def _do(nc, name, shape, dt=F32):
    return nc.dram_tensor(name, list(shape), dt, kind="ExternalOutput").ap()


def _mod_aps(nc, sfx=""):
    return {"cT": _di(nc, "cT" + sfx, [128, 8]), "mod_w": _di(nc, "mod_w" + sfx, [1024, 6144]),
            "mod_b6": _di(nc, "mod_b6" + sfx, [128, 48]), "g4": _di(nc, "g4" + sfx, [128, 32])}


def _mix_aps(nc, sfx=""):
    d = lambda n, s: _di(nc, n + sfx, s)
    return {"consts": d("consts", [128, 896]), "cwA": d("cwA", [64, 3]), "xw": d("xw", [128, 4]), "xb": d("xb", [128, 1]),
            "bcw": d("bcw", [128, 4]), "bcb": d("bcb", [128, 1]), "cbias": d("cbias", [64, 1]),
            "xb_row": d("xb_row", [1, 128]), "bb_row": d("bb_row", [1, 64]), "hp": d("hp", [128, 6])}


def _tokb_w_aps(nc, sfx=""):
    d = lambda n, s: _di(nc, n + sfx, s)
    return {"w_in": d("w_in", [1024, 5896]), "w_sc_out": d("w_sc_out", [256, 1024]), "w_sb_out": d("w_sb_out", [256, 1024]),
            "w_ssm_out": d("w_ssm_out", [512, 1024]), "w_o": d("w_o", [1024, 1024]), "w_ffn_in": d("w_ffn_in", [1024, 2 * FFH]),
            "w_ffn_out": d("w_ffn_out", [FFH, 1024])}


def build_tok_a():
    nc = bass.Bass("TRN2", target_bir_lowering=False)
    with contextlib.ExitStack() as es:
        C = Ctx(nc, es)
        xT = _di(nc, "xT", [1024, NT])
        P = _mod_aps(nc)
        w_in = _di(nc, "w_in", [1024, 5896])
        send = _do(nc, "send", [4, 512, NT], BF16)
        send_dt = _do(nc, "send_dt", [4, 2, NT])
        wgs = prep_weights(C, [("win", w_in, 8, [[g] for g in WIN_GROUPS[:6]])])
        T = tok_setup(C, P)
        compute_mod(C, T, P)
        ws = WStream(C, nbuf=3)
        tok_a(C, T, xT, wgs["win"], send, send_dt, ws)
        C.S.finish()
    return nc


def build_mix():
    nc = bass.Bass("TRN2", target_bir_lowering=False)
    with contextlib.ExitStack() as es:
        C = Ctx(nc, es)
        R = _di(nc, "R", [4, 512, NT], BF16)
        Rdt = _di(nc, "Rdt", [4, 2, NT])
        O2 = _do(nc, "O2", [4, 256, NT], BF16)
        P = _mix_aps(nc)
        mix_piece(C, R, Rdt, O2, P)
        C.S.finish()
    return nc


def build_tok_b():
    nc = bass.Bass("TRN2", target_bir_lowering=False)
    with contextlib.ExitStack() as es:
        C = Ctx(nc, es)
        xT = _di(nc, "xT", [1024, NT])
        P = _mod_aps(nc)
        W = _tokb_w_aps(nc)
        nw = _di(nc, "nw", [128, 4])
        Pin = _di(nc, "Pin", [4, 256, NT], BF16)
        xo = _do(nc, "xo", [1024, NT])
        wgs = prep_weights(C, tokb_weight_specs(W))
        T = tok_setup(C, P)
        compute_mod(C, T, P)
        ws = WStream(C, nbuf=3)
        tok_b(C, T, xT, xo, Pin, wgs, nw, ws)
        C.S.finish()
    return nc


def host_mod_params(inp, l, b):
    f = lambda a: np.ascontiguousarray(a, dtype=np.float32)
    return {
        "cT": f(inp["c"][b].reshape(8, 128).T),
        "mod_w": f(inp["mod_w"][l]),
        "mod_b6": f(inp["mod_b"][l].reshape(48, 128).T),
        "g4": f(np.stack([inp["g_pre_mix"][l], inp["g_post_mix"][l], inp["g_pre_ffn"][l], inp["g_post_ffn"][l]]).reshape(32, 128).T),
    }


def kernel_unfused(**inp):
    inp = {k: np.asarray(v) for k, v in inp.items()}
    x = inp["x"]
    cores = list(range(8))
    xT = [np.ascontiguousarray(x[c // 4, (c % 4) * NT:(c % 4 + 1) * NT, :].T) for c in cores]
    nc_a, nc_m, nc_b = build_tok_a(), build_mix(), build_tok_b()
    for l in range(DEPTH):
        w_in = np.ascontiguousarray(inp["w_in"][l], dtype=np.float32)
        maps = []
        for c in cores:
            m = host_mod_params(inp, l, c // 4)
            m["xT"] = xT[c]
            m["w_in"] = w_in
            maps.append(m)
        ra = run_bass_kernel_spmd(nc_a, maps, core_ids=cores).results
        maps = []
        for c in cores:
            b, g = c // 4, c % 4
            m = host_mix_params(inp, l, g)
            m["R"] = np.stack([np.asarray(ra[4 * b + j]["send"])[g] for j in range(4)])
            m["Rdt"] = np.stack([np.asarray(ra[4 * b + j]["send_dt"])[g] for j in range(4)])
            maps.append(m)
        rm = run_bass_kernel_spmd(nc_m, maps, core_ids=cores).results
        del ra
        maps = []
        tb = host_tokb_params(inp, l)
        for c in cores:
            b, g = c // 4, c % 4
            m = host_mod_params(inp, l, b)
            m.update(tb)
            m["xT"] = xT[c]
            m["w_in"] = w_in
            m["Pin"] = np.stack([np.asarray(rm[4 * b + j]["O2"])[g] for j in range(4)])
            maps.append(m)
        rb = run_bass_kernel_spmd(nc_b, maps, core_ids=cores).results
        del rm
        xT = [np.ascontiguousarray(np.asarray(rb[c]["xo"])) for c in cores]
    out = np.empty((NB, SEQ, D), np.float32)
    for c in cores:
        out[c // 4, (c % 4) * NT:(c % 4 + 1) * NT, :] = xT[c].T
    return out


def kernel(**inp):
    return kernel_unfused(**inp)
```

```python
import contextlib
import numpy as np
import ml_dtypes
import concourse.bass as bass
import concourse.mybir as mybir
from concourse.bass_utils import run_bass_kernel_spmd

F32 = mybir.dt.float32
BF16 = mybir.dt.bfloat16
AF = mybir.ActivationFunctionType
ALU = mybir.AluOpType

D = 1024
SEQ = 16384
NB = 2
DEPTH = 2
NT = 4096
TT = 512
NTT = NT // TT
EPS = 1e-6
FFH = 2816
NFC = FFH // 128
SAME_SYNC = True
NDS = 24
NDS_POOL = 4


class Buf:
    __slots__ = ("name", "w", "r")

    def __init__(self, name=""):
        self.name = name
        self.w = {}
        self.r = {}


class Sched:
    def __init__(self, nc, es):
        self.nc = nc
        self.engs = {"pe": nc.tensor, "act": nc.scalar, "dve": nc.vector, "pool": nc.gpsimd, "sp": nc.sync}
        self.sems = []
        self.esem = {}
        self.ecnt = {}
        for k in ("pe", "act", "dve", "pool"):
            self.esem[k] = self._newsem(es, "e_" + k)
            self.ecnt[k] = 0
        self.dsem = {}
        self.dcnt = {}
        self.dnext = {}
        self.nds = {"sp": NDS, "pool": NDS_POOL, "act": 8}
        for q in ("sp", "pool", "act"):
            self.dsem[q] = [self._newsem(es, f"d_{q}{i}") for i in range(self.nds[q])]
            self.dcnt[q] = [0] * self.nds[q]
            self.dnext[q] = 0
        self.known = {k: {} for k in self.engs}
        self.nwaits = 0
        self.nins = 0

    def _newsem(self, es, name):
        self.sems.append(es.enter_context(self.nc.semaphore(name)))
        return len(self.sems) - 1

    def _wait(self, ek, si, val):
        k = self.known[ek]
        if k.get(si, 0) >= val:
            return
        self.engs[ek].wait_ge(self.sems[si], val)
        k[si] = val
        self.nwaits += 1

    def _deps(self, ek, reads, writes):
        need = {}
        for b in reads:
            for si, v in b.w.items():
                if need.get(si, 0) < v:
                    need[si] = v
        for b in writes:
            for si, v in b.w.items():
                if need.get(si, 0) < v:
                    need[si] = v
            for si, v in b.r.items():
                if need.get(si, 0) < v:
                    need[si] = v
        own = self.esem.get(ek)
        for si, v in need.items():
            if si == own and (ek == "pe" or not SAME_SYNC):
                continue
            self._wait(ek, si, v)

    def _mark(self, si, val, reads, writes):
        for b in reads:
            if b.r.get(si, 0) < val:
                b.r[si] = val
        for b in writes:
            b.w = {si: val}
            b.r = {}

    def op(self, ek, fn, reads=(), writes=()):
        self._deps(ek, reads, writes)
        ins = fn(self.engs[ek])
        self.ecnt[ek] += 1
        si = self.esem[ek]
        ins.then_inc(self.sems[si], 1)
        self._mark(si, self.ecnt[ek], reads, writes)
        self.nins += 1

    def dma(self, q, out, in_, reads=(), writes=(), adds=()):
        self._deps(q, reads, writes)
        i = self.dnext[q]
        self.dnext[q] = (i + 1) % self.nds[q]
        si = self.dsem[q][i]
        if self.dcnt[q][i] > 0:
            self._wait(q, si, self.dcnt[q][i])
        self.engs[q].dma_start(out=out, in_=in_).then_inc(self.sems[si], 16)
        self.dcnt[q][i] += 16
        self._mark(si, self.dcnt[q][i], reads, writes)
        for b in adds:
            b.w[si] = self.dcnt[q][i]
        self.nins += 1

    def barrier(self):
        for ek in self.engs:
            for q in self.dsem:
                for i, si in enumerate(self.dsem[q]):
                    if self.dcnt[q][i] > 0:
                        self._wait(ek, si, self.dcnt[q][i])
            for k, si in self.esem.items():
                if self.ecnt[k] > 0 and k != ek:
                    self._wait(ek, si, self.ecnt[k])

    def finish(self):
        for q in self.dsem:
            for i, si in enumerate(self.dsem[q]):
                if self.dcnt[q][i] > 0:
                    self._wait("sp", si, self.dcnt[q][i])
        for k, si in self.esem.items():
            if self.ecnt[k] > 0:
                self._wait("sp", si, self.ecnt[k])


class Ctx:
    def __init__(self, nc, es):
        self.nc = nc
        self.es = es
        self.S = Sched(nc, es)
        self.n = 0
        self.cur_es = None

    def sb(self, shape, dt, name=None, es=None):
        self.n += 1
        t = (es or self.cur_es or self.es).enter_context(self.nc.sbuf_tensor(f"{name or 't'}_{self.n}", list(shape), dt))
        return t

    @contextlib.contextmanager
    def scope(self):
        prev = self.cur_es
        with contextlib.ExitStack() as es:
            self.cur_es = es
            try:
                yield es
            finally:
                self.S.barrier()
                self.cur_es = prev

    def ps(self, shape=(128, 512), dt=F32, name=None):
        self.n += 1
        return self.es.enter_context(self.nc.psum_tensor(f"{name or 'p'}_{self.n}", list(shape), dt))

    def dram(self, name, shape, dt, kind="Internal"):
        return self.nc.dram_tensor(name, list(shape), dt, kind=kind).ap()


class Rot:
    def __init__(self, items):
        self.items = items
        self.i = 0

    def next(self):
        it = self.items[self.i]
        self.i = (self.i + 1) % len(self.items)
        return it


WIN_GROUPS = [(0, 512), (512, 512), (1024, 512), (1536, 512), (2048, 512), (2560, 264)] + \
             [(2824 + 512 * i, 512) for i in range(6)]


def prep_weights(C, specs):
    S = C.S
    nc = C.nc
    out = {}
    with contextlib.ExitStack() as es:
        st32 = [(C.sb([128, 5632], F32, "st32", es), Buf()) for _ in range(2)]
        st16 = [(C.sb([128, 5632], BF16, "st16", es), Buf()) for _ in range(2)]
        r32 = Rot(st32)
        r16 = Rot(st16)
        engs = ["pool", "dve", "act"]
        ei = 0
        for key, src, KC, groups in specs:
            srcv = src.rearrange("(kc p) n -> p kc n", p=128)
            lst = []
            for gi, pieces in enumerate(groups):
                GW = sum(p[1] for p in pieces)
                dst = C.dram(f"wb_{key}_{gi}", [128, KC, GW], BF16)
                dbuf = Buf()
                t32, b32 = r32.next()
                t16, b16 = r16.next()
                v32 = t32[:, 0:KC * GW].rearrange("p (k n) -> p k n", k=KC)
                v16 = t16[:, 0:KC * GW].rearrange("p (k n) -> p k n", k=KC)
                o = 0
                for (c0, ncol) in pieces:
                    S.dma("sp", v32[:, :, o:o + ncol], srcv[:, :, c0:c0 + ncol], writes=[b32])
                    o += ncol
                ek = engs[ei % 3]
                ei += 1
                if ek == "act":
                    S.op("act", lambda e: e.copy(out=t16[:, 0:KC * GW], in_=t32[:, 0:KC * GW]), [b32], [b16])
                else:
                    S.op(ek, lambda e: e.tensor_copy(out=t16[:, 0:KC * GW], in_=t32[:, 0:KC * GW]), [b32], [b16])
                S.dma("act", dst, v16, reads=[b16], writes=[dbuf])
                lst.append((dst, dbuf))
            out[key] = lst
        S.barrier()
    return out


class WStream:
    def __init__(self, C, nbuf=3, elems=5632):
        self.C = C
        self.bufs = Rot([(C.sb([128, elems], BF16, "wst"), Buf()) for _ in range(nbuf)])

    def load(self, grp):
        dst, dbuf = grp
        KC, GW = dst.shape[1], dst.shape[2]
        t, b = self.bufs.next()
        v = t[:, 0:KC * GW].rearrange("p (k n) -> p k n", k=KC)
        self.C.S.dma("sp", v, dst, reads=[dbuf], writes=[b])
        return v, b


class TokState:
    pass


def tok_setup(C, l_params):
    S = C.S
    T = TokState()
    T.ones_d = C.sb([128, 128], BF16, "ones_d")
    T.ones_g = C.sb([128, 128], BF16, "ones_g")
    T.b_const = Buf()
    S.op("pool", lambda e: e.memset(T.ones_d[:], 1.0 / 1024.0), [], [T.b_const])
    S.op("pool", lambda e: e.memset(T.ones_g[:], 1.0 / 256.0), [], [T.b_const])
    T.eps_t = C.sb([128, 1], F32, "eps_t")
    S.op("pool", lambda e: e.memset(T.eps_t[:], EPS), [], [T.b_const])
    T.xt = C.sb([128, 8, TT], F32, "xt"); T.b_xt = Buf()
    T.hT = C.sb([128, 8, TT], BF16, "hT"); T.b_hT = Buf()
    T.sq = C.sb([128, 8, TT], BF16, "sq"); T.b_sq = Buf()
    T.rstd = C.sb([128, TT], F32, "rstd"); T.b_rstd = Buf()
    T.tmp = [(C.sb([128, TT], F32, "tmp"), Buf()) for _ in range(4)]
    T.rtmp = Rot(T.tmp)
    T.psum = Rot([(C.ps(), Buf()) for _ in range(8)])
    return T


def compute_mod(C, T, P, es_scope=None):
    S = C.S
    nc = C.nc
    T.vec = C.sb([128, 48], F32, "vec"); T.b_vec = Buf()
    with contextlib.ExitStack() as es:
        cT = C.sb([128, 8], F32, "cT", es); b_c = Buf()
        sc = C.sb([128, 8], F32, "sc", es); b_sc = Buf()
        mb = C.sb([128, 48], F32, "mb", es); b_mb = Buf()
        g4 = C.sb([128, 32], F32, "g4", es); b_g4 = Buf()
        modv = C.sb([128, 48], F32, "modv", es); b_modv = Buf()
        wbufs = Rot([(C.sb([128, 8, 512], F32, "mw", es), Buf()) for _ in range(2)])
        S.dma("sp", cT[:], P["cT"], writes=[b_c])
        S.dma("sp", mb[:], P["mod_b6"], writes=[b_mb])
        S.dma("sp", g4[:], P["g4"], writes=[b_g4])
        S.op("act", lambda e: e.activation(out=sc[:], in_=cT[:], func=AF.Silu), [b_c], [b_sc])
        mw = P["mod_w"].rearrange("(kc p) n -> p kc n", p=128)
        pt, pb = T.psum.next()
        for gi in range(12):
            wt, wb = wbufs.next()
            S.dma("sp", wt[:], mw[:, :, gi * 512:(gi + 1) * 512], writes=[wb])
            for fc in range(4):
                col = gi * 4 + fc
                for k in range(8):
                    S.op("pe", lambda e, k=k, fc=fc, col=col: e.matmul(
                        pt[:, col:col + 1], lhsT=wt[:, k, fc * 128:(fc + 1) * 128], rhs=sc[:, k:k + 1],
                        start=(k == 0), stop=(k == 7)), [wb, b_sc], [pb])
        S.op("dve", lambda e: e.tensor_tensor(out=modv[:], in0=pt[:, 0:48], in1=mb[:], op=ALU.add), [pb, b_mb], [b_modv])
        v = T.vec
        S.op("dve", lambda e: e.scalar_tensor_tensor(out=v[:, 0:8], in0=modv[:, 8:16], scalar=1.0, in1=g4[:, 0:8],
                                                     op0=ALU.add, op1=ALU.mult), [b_modv, b_g4], [T.b_vec])
        S.op("dve", lambda e: e.tensor_copy(out=v[:, 8:16], in_=modv[:, 0:8]), [b_modv], [T.b_vec])
        S.op("dve", lambda e: e.tensor_tensor(out=v[:, 16:24], in0=modv[:, 16:24], in1=g4[:, 8:16], op=ALU.mult), [b_modv, b_g4], [T.b_vec])
        S.op("dve", lambda e: e.scalar_tensor_tensor(out=v[:, 24:32], in0=modv[:, 32:40], scalar=1.0, in1=g4[:, 16:24],
                                                     op0=ALU.add, op1=ALU.mult), [b_modv, b_g4], [T.b_vec])
        S.op("dve", lambda e: e.tensor_copy(out=v[:, 32:40], in_=modv[:, 24:32]), [b_modv], [T.b_vec])
        S.op("dve", lambda e: e.tensor_tensor(out=v[:, 40:48], in0=modv[:, 40:48], in1=g4[:, 24:32], op=ALU.mult), [b_modv, b_g4], [T.b_vec])
        S.barrier()
    return T


def rms_stats(C, T, src_t, src_b, nch, ones_t):
    S = C.S
    S.op("act", lambda e: e.activation(out=T.sq[:, 0:nch, :], in_=src_t[:, 0:nch, :], func=AF.Square), [src_b], [T.b_sq])
    pt, pb = T.psum.next()
    for k in range(nch):
        S.op("pe", lambda e, k=k: e.matmul(pt[:], lhsT=ones_t[:], rhs=T.sq[:, k, :], start=(k == 0), stop=(k == nch - 1)),
             [T.b_sq, T.b_const], [pb])
    S.op("act", lambda e: e.activation(out=T.rstd[:], in_=pt[:], func=AF.Sqrt, bias=T.eps_t[:, 0:1], scale=1.0), [pb, T.b_const], [T.b_rstd])
    S.op("dve", lambda e: e.reciprocal(out=T.rstd[:], in_=T.rstd[:]), [T.b_rstd], [T.b_rstd])


def norm_mod(C, T, gcol, scol):
    S = C.S
    rms_stats(C, T, T.xt, T.b_xt, 8, T.ones_d)
    for k in range(8):
        tt, tb = T.rtmp.next()
        S.op("dve", lambda e, k=k, tt=tt: e.scalar_tensor_tensor(
            out=tt[:], in0=T.xt[:, k, :], scalar=T.vec[:, gcol + k:gcol + k + 1], in1=T.rstd[:],
            op0=ALU.mult, op1=ALU.mult), [T.b_xt, T.b_vec, T.b_rstd], [tb])
        S.op("act", lambda e, k=k, tt=tt: e.activation(
            out=T.hT[:, k, :], in_=tt[:], func=AF.Identity, bias=T.vec[:, scol + k:scol + k + 1], scale=1.0),
            [tb, T.b_vec], [T.b_hT])


def proj_chunk(C, T, wv, wb, c0, M=128):
    S = C.S
    pt, pb = T.psum.next()
    for k in range(8):
        S.op("pe", lambda e, k=k: e.matmul(pt[0:M, :], lhsT=wv[:, k, c0:c0 + M], rhs=T.hT[:, k, :],
                                           start=(k == 0), stop=(k == 7)), [wb, T.b_hT], [pb])
    return pt, pb


def tok_a(C, T, xT_in, wg, send, send_dt, ws):
    S = C.S
    xv = xT_in.rearrange("(kc p) t -> p kc t", p=128)
    stage = Rot([(C.sb([128, TT], BF16, "stg"), Buf()) for _ in range(4)])
    dts = Rot([(C.sb([8, TT], F32, "dts"), Buf()) for _ in range(2)])
    sb_send = Buf()
    for ti in range(NTT):
        t0 = ti * TT
        S.dma("sp", T.xt[:], xv[:, :, t0:t0 + TT], writes=[T.b_xt])
        norm_mod(C, T, 0, 8)

        def store_halves(st, sb_, row0, dest_of_half):
            for hf in range(2):
                S.dma("act", send[dest_of_half[hf], row0:row0 + 64, t0:t0 + TT], st[hf * 64:(hf + 1) * 64, :],
                      reads=[sb_], adds=[sb_send])

        w0, b0 = ws.load(wg[0])
        w1, b1 = ws.load(wg[1])
        for i in range(2):
            pc, pcb = proj_chunk(C, T, w0, b0, 256 + i * 128)
            tt, tb = T.rtmp.next()
            S.op("act", lambda e, tt=tt, pc=pc: e.copy(out=tt[:], in_=pc[:]), [pcb], [tb])
            px, pxb = proj_chunk(C, T, w1, b1, i * 128)
            st, sb_ = stage.next()
            S.op("dve", lambda e, tt=tt, px=px, st=st: e.tensor_tensor(out=st[:], in0=px[:], in1=tt[:], op=ALU.mult),
                 [pxb, tb], [sb_])
            store_halves(st, sb_, 0, (2 * i, 2 * i + 1))
        w2, b2 = ws.load(wg[2])
        for i in range(2):
            pq, pqb = proj_chunk(C, T, w1, b1, 256 + i * 128)
            st, sb_ = stage.next()
            S.op("act", lambda e, pq=pq, st=st: e.activation(out=st[:], in_=pq[:], func=AF.Copy, scale=0.125), [pqb], [sb_])
            store_halves(st, sb_, 64, (2 * i, 2 * i + 1))
        w4, b4 = ws.load(wg[4])
        for j in range(2):
            for i in range(2):
                pk, pkb = proj_chunk(C, T, w2, b2, j * 256 + i * 128)
                st, sb_ = stage.next()
                S.op("act", lambda e, pk=pk, st=st: e.copy(out=st[:], in_=pk[:]), [pkb], [sb_])
                store_halves(st, sb_, 128 + 64 * j, (2 * i, 2 * i + 1))
        w5, b5 = ws.load(wg[5])
        for i in range(4):
            px, pxb = proj_chunk(C, T, w4, b4, i * 128)
            st, sb_ = stage.next()
            S.op("act", lambda e, px=px, st=st: e.copy(out=st[:], in_=px[:]), [pxb], [sb_])
            S.dma("act", send[i, 256:384, t0:t0 + TT], st[:], reads=[sb_], adds=[sb_send])
        for j in range(2):
            pb_, pbb = proj_chunk(C, T, w5, b5, j * 128)
            st, sb_ = stage.next()
            S.op("act", lambda e, pb_=pb_, st=st: e.copy(out=st[:], in_=pb_[:]), [pbb], [sb_])
            for gi in range(2):
                for dd in range(2):
                    S.dma("act", send[2 * gi + dd, 384 + 64 * j:448 + 64 * j, t0:t0 + TT], st[gi * 64:(gi + 1) * 64, :],
                          reads=[sb_], adds=[sb_send])
        pd, pdb = proj_chunk(C, T, w5, b5, 256, M=8)
        dt_t, dt_b = dts.next()
        S.op("act", lambda e, pd=pd, dt_t=dt_t: e.copy(out=dt_t[:], in_=pd[0:8, :]), [pdb], [dt_b])
        for g in range(4):
            S.dma("act", send_dt[g, :, t0:t0 + TT], dt_t[2 * g:2 * g + 2, :], reads=[dt_b], adds=[sb_send])
    return sb_send


NBLK = SEQ // 128
NCH = SEQ // 256


def host_consts():
    c = np.zeros((128, 896), np.float32)
    i = np.arange(128)
    c[:, 0:128] = np.eye(128)
    c[:, 128:256] = (i[:, None] <= i[None, :])
    c[:, 256:384] = 1.0
    c[:, 384:512] = (i[:, None] > i[None, :])
    c[:, 512:640] = (i[:, None] < i[None, :])
    c[:, 640:768] = np.where(i[None, :] < i[:, None], -30000.0, 0.0)
    c[127, 768:896] = 1.0
    return c


def mix_piece(C, R, Rdt, O2, P, do_conv=True, do_attn=True, do_ssd=True, nq=NBLK):
    S = C.S
    nc = C.nc
    cf = C.sb([128, 896], F32, "cf"); b_cf = Buf()
    cb = C.sb([128, 640], BF16, "cb"); b_cb = Buf()
    S.dma("sp", cf[:], P["consts"], writes=[b_cf])
    S.op("dve", lambda e: e.tensor_copy(out=cb[:], in_=cf[:, 0:640]), [b_cf], [b_cb])
    ident_b = cb[:, 0:128]; tri_b = cb[:, 128:256]; ones_b = cb[:, 256:384]; U_b = cb[:, 384:512]
    M_f = cf[:, 512:640]
    o2buf = Buf()
    psA = Rot([(C.ps(), Buf()) for _ in range(2)])
    psB = Rot([(C.ps(), Buf()) for _ in range(2)])
    psC = Rot([(C.ps(), Buf()) for _ in range(2)])
    psL = (C.ps(), Buf())
    psM = (C.ps(), Buf())

    T1 = C.sb([128, SEQ], BF16, "T1")
    T2 = C.sb([128, 2 + SEQ], BF16, "T2")
    b_q = [Buf() for _ in range(4)]; b_v = [Buf() for _ in range(4)]
    b_k = [Buf() for _ in range(4)]; b_u = [Buf() for _ in range(4)]
    b_upad = Buf()
    S.op("pool", lambda e: e.memset(T2[64:128, 0:2], 0.0), [], [b_upad])
    for j in range(4):
        S.dma("sp", T1[0:64, j * NT:(j + 1) * NT], R[j, 64:128, :], writes=[b_q[j]])
        S.dma("sp", T2[0:64, 2 + j * NT:2 + (j + 1) * NT], R[j, 128:192, :], writes=[b_k[j]])
        S.dma("sp", T1[64:128, j * NT:(j + 1) * NT], R[j, 192:256, :], writes=[b_v[j]])
        S.dma("sp", T2[64:128, 2 + j * NT:2 + (j + 1) * NT], R[j, 0:64, :], writes=[b_u[j]])

    if do_conv:
      with C.scope():
        cw = C.sb([128, 3], F32, "cw"); b_cw = Buf()
        S.dma("sp", cw[64:128, :], P["cwA"], writes=[b_cw])
        tA = [(C.sb([128, 2048], F32, "tA"), Buf()) for _ in range(1)]
        oA = Rot([(C.sb([128, 2048], BF16, "oA"), Buf()) for _ in range(2)])
        for ch in range(8):
            t0 = ch * 2048
            j = t0 // NT
            rb = [b_u[j], b_cw, b_upad] + ([b_u[j - 1]] if (j > 0 and t0 % NT == 0) else [])
            ta, tb = tA[0]
            ot, ob = oA.next()
            S.op("dve", lambda e: e.tensor_scalar(out=ta[64:128, :], in0=T2[64:128, 2 + t0:2 + t0 + 2048], scalar1=cw[64:128, 2:3],
                                                   scalar2=None, op0=ALU.mult), rb, [tb])
            S.op("dve", lambda e: e.scalar_tensor_tensor(out=ta[64:128, :], in0=T2[64:128, 1 + t0:1 + t0 + 2048], scalar=cw[64:128, 1:2],
                                                          in1=ta[64:128, :], op0=ALU.mult, op1=ALU.add), rb + [tb], [tb])
            S.op("dve", lambda e: e.scalar_tensor_tensor(out=ot[64:128, :], in0=T2[64:128, t0:t0 + 2048], scalar=cw[64:128, 0:1],
                                                          in1=ta[64:128, :], op0=ALU.mult, op1=ALU.add), rb + [tb], [ob])
            S.dma("act", O2[j, 0:64, t0 % NT:t0 % NT + 2048], ot[64:128, :], reads=[ob], adds=[o2buf])

    if do_ssd:
      with C.scope():
        ssd_piece(C, R, Rdt, O2, P, cf, b_cf, cb, b_cb, psA, psB, psC, psL, psM, o2buf)

    if do_attn:
        vt = C.sb([128, NBLK, 64], BF16, "vt"); b_vt = [Buf() for _ in range(16)]
        for bb in range(16):
            pt, pb = psA.next()
            for k in range(8):
                blk = bb * 8 + k
                S.op("pe", lambda e: e.matmul(pt[:, k * 64:(k + 1) * 64], lhsT=T1[64:128, blk * 128:(blk + 1) * 128],
                                              rhs=ident_b[64:128, 64:128], start=True, stop=True), [b_v[blk // 32], b_cb], [pb])
            S.op("dve", lambda e: e.tensor_copy(out=vt[:, bb * 8:(bb + 1) * 8, :].rearrange("p a b -> p (a b)"), in_=pt[:]),
                 [pb], [b_vt[bb]])
        NPIPE = 3
        zero_b = C.sb([128, 128], BF16, "zero_b"); b_zero = Buf()
        S.op("pool", lambda e: e.memset(zero_b[:], 0.0), [], [b_zero])
        M_b = C.sb([128, 128], BF16, "M_b")
        S.op("dve", lambda e: e.tensor_copy(out=M_b[:], in_=M_f), [b_cf], [b_zero])
        E_r = Rot([(C.sb([128, 512], F32, "E"), Buf()) for _ in range(2)])
        SPb_r = Rot([(C.sb([128, 512], BF16, "SPb"), Buf()) for _ in range(NPIPE)])
        t1_r = Rot([(C.sb([128, 512], F32, "t1"), Buf()) for _ in range(2)])
        att_r = Rot([(C.sb([128, 512], BF16, "att"), Buf()) for _ in range(2)])
        lb_r = Rot([(C.sb([128, 512], BF16, "lbsb"), Buf()) for _ in range(NPIPE + 1)])
        ost_r = Rot([(C.sb([64, 512], BF16, "ost"), Buf()) for _ in range(2)])
        z_r = Rot(psA.items + [psM])
        lb_t, lb_b = psL
        nquad = (nq + 3) // 4
        groups = []
        for m in range(nquad):
            for c in reversed(range(4 * m + 4)):
                r = max(0, c - 4 * m)
                groups.append(dict(m=m, c=c, r=r, diag=(c >= 4 * m), first=(c == 4 * m + 3), last=(c == 0)))
        state = {"lbs": None, "ob": None}

        def stage1(g):
            m, c, r = g["m"], g["c"], g["r"]
            lo = r * 128
            cs = slice(lo, 512)
            qcols = slice((4 * m + r) * 128, (4 * m + 4) * 128)
            bq = b_q[(4 * m) // 32]
            z_t, z_b = z_r.next()
            S.op("pe", lambda e: e.matmul(z_t[:, cs], lhsT=T2[0:64, 2 + c * 128:2 + (c + 1) * 128], rhs=T1[0:64, qcols],
                                          start=True, stop=True), [b_k[c // 32], bq], [z_b])
            e_t, e_b = E_r.next()
            spb_t, spb_b = SPb_r.next()
            S.op("act", lambda e: e.activation(out=e_t[:, cs], in_=z_t[:, cs], func=AF.Exp), [z_b], [e_b])
            S.op("act", lambda e: e.activation(out=spb_t[:, cs], in_=e_t[:, cs], func=AF.Ln, bias=1.0, scale=1.0), [e_b], [spb_b])
            if g["diag"]:
                S.op("pool", lambda e: e.tensor_tensor(out=spb_t[:, lo:lo + 128], in0=spb_t[:, lo:lo + 128], in1=M_b[:], op=ALU.mult),
                     [spb_b, b_zero], [spb_b])
            g.update(z=(z_t, z_b), spb=(spb_t, spb_b), lbs_in=state["lbs"])
            if g["first"]:
                S.op("pe", lambda e: e.matmul(lb_t[:, 0:512], lhsT=zero_b[:], rhs=T1[:, 0:512],
                                              start=True, stop=False), [b_zero, b_q[0], b_v[0]], [lb_b])
            if not g["last"]:
                S.op("pe", lambda e: e.matmul(lb_t[:, cs], lhsT=ones_b, rhs=spb_t[:, cs], start=False, stop=False), [spb_b, b_cb], [lb_b])
                lbs = lb_r.next()
                S.op("dve", lambda e: e.tensor_copy(out=lbs[0][:], in_=lb_t[:, 0:512]), [lb_b], [lbs[1]])
                state["lbs"] = lbs

        def stage2(g):
            m, c, r = g["m"], g["c"], g["r"]
            lo = r * 128
            cs = slice(lo, 512)
            z_t, z_b = g["z"]
            spb_t, spb_b = g["spb"]
            lbs = g["lbs_in"]
            if g["first"]:
                state["ob"] = psC.next()
                ob_t, ob_b = state["ob"]
                S.op("pe", lambda e: e.matmul(ob_t[0:64, 0:512], lhsT=zero_b[:, 0:64], rhs=T1[:, 0:512], start=True, stop=False),
                     [b_zero, b_q[0], b_v[0]], [ob_b])
            ob_t, ob_b = state["ob"]
            x_t, x_b = psB.next()
            has_later = not g["first"]
            S.op("pe", lambda e: e.matmul(x_t[:, cs], lhsT=U_b, rhs=spb_t[:, cs], start=True, stop=not has_later), [spb_b, b_cb], [x_b])
            if has_later:
                llo = lo + 128 if g["diag"] else lo
                S.op("pe", lambda e: e.matmul(x_t[:, llo:512], lhsT=ident_b, rhs=lbs[0][:, llo:512], start=False, stop=True),
                     [lbs[1], b_cb], [x_b])
            t1_t, t1_b = t1_r.next()
            S.op("dve", lambda e: e.tensor_tensor(out=t1_t[:, cs], in0=z_t[:, cs], in1=spb_t[:, cs], op=ALU.subtract), [z_b, spb_b], [t1_b])
            S.op("dve", lambda e: e.tensor_tensor(out=t1_t[:, cs], in0=t1_t[:, cs], in1=x_t[:, cs], op=ALU.subtract), [t1_b, x_b], [t1_b])
            a_t, a_b = att_r.next()
            S.op("act", lambda e: e.activation(out=a_t[:, cs], in_=t1_t[:, cs], func=AF.Exp), [t1_b], [a_b])
            if g["diag"]:
                S.op("pool", lambda e: e.tensor_tensor(out=a_t[:, lo:lo + 128], in0=a_t[:, lo:lo + 128], in1=M_b[:], op=ALU.mult),
                     [a_b, b_zero], [a_b])
            S.op("pe", lambda e: e.matmul(ob_t[0:64, cs], lhsT=vt[:, c, :], rhs=a_t[:, cs], start=False, stop=g["last"]),
                 [b_vt[c // 8], a_b], [ob_b])
            if g["last"]:
                ost = ost_r.next()
                S.op("dve", lambda e: e.tensor_copy(out=ost[0][:], in_=ob_t[0:64, 0:512]), [ob_b], [ost[1]])
                t0 = m * 512
                S.dma("act", O2[t0 // NT, 64:128, t0 % NT:t0 % NT + 512], ost[0][:], reads=[ost[1]], adds=[o2buf])

        prev = None
        for g in groups + [None]:
            if g is not None:
                stage1(g)
            if prev is not None:
                stage2(prev)
            prev = g
    return o2buf


def ssd_piece(C, R, Rdt, O2, P, cf, b_cf, cb, b_cb, psA, psB, psC, psL, psM, o2buf):
    S = C.S
    ident_f = cf[:, 0:128]; tri_f = cf[:, 128:256]; ones_f = cf[:, 256:384]; trirow0_f = cf[:, 128:384]
    neg_f = cf[:, 640:768]; sel127_f = cf[:, 768:896]
    ident_b = cb[:, 0:128]
    xpre = C.sb([128, 3 + SEQ], BF16, "xpre"); b_xp = [Buf() for _ in range(4)]
    bcpre = C.sb([128, 3 + SEQ], BF16, "bcpre"); b_bc = [Buf() for _ in range(4)]
    b_pad = Buf()
    S.op("pool", lambda e: e.memset(xpre[:, 0:3], 0.0), [], [b_pad])
    S.op("pool", lambda e: e.memset(bcpre[:, 0:3], 0.0), [], [b_pad])
    for j in range(4):
        S.dma("sp", xpre[:, 3 + j * NT:3 + (j + 1) * NT], R[j, 256:384, :], writes=[b_xp[j]])
        S.dma("sp", bcpre[:, 3 + j * NT:3 + (j + 1) * NT], R[j, 384:512, :], writes=[b_bc[j]])
    prm = C.sb([128, 16], F32, "prm"); b_prm = Buf()
    S.dma("sp", prm[:, 0:4], P["xw"], adds=[b_prm])
    S.dma("sp", prm[:, 4:5], P["xb"], adds=[b_prm])
    S.dma("sp", prm[:, 5:9], P["bcw"], adds=[b_prm])
    S.dma("sp", prm[:, 9:10], P["bcb"], adds=[b_prm])
    S.dma("sp", prm[:, 10:16], P["hp"], adds=[b_prm])
    cbt = C.sb([64, 1], F32, "cbt")
    S.dma("sp", cbt[:], P["cbias"], adds=[b_prm])
    rows = C.sb([1, 192], F32, "rows"); b_rows = Buf()
    S.dma("sp", rows[:, 0:128], P["xb_row"], adds=[b_rows])
    S.dma("sp", rows[:, 128:192], P["bb_row"], adds=[b_rows])
    rows_b = C.sb([1, 320], BF16, "rows_b"); b_rowsb = Buf()
    S.op("dve", lambda e: e.tensor_copy(out=rows_b[:, 0:192], in_=rows[:, :]), [b_rows], [b_rowsb])
    S.op("dve", lambda e: e.memset(rows_b[:, 192:320], 1.0), [], [b_rowsb])
    cm = C.sb([128, 4, 256], BF16, "cm"); b_cm = Buf()
    for k in range(4):
        S.op("dve", lambda e: e.tensor_scalar(out=cm[:, k, 0:128], in0=ident_f, scalar1=prm[:, k:k + 1], scalar2=None, op0=ALU.mult),
             [b_cf, b_prm], [b_cm])
        S.op("dve", lambda e: e.tensor_scalar(out=cm[:, k, 128:256], in0=ident_f, scalar1=prm[:, 5 + k:6 + k], scalar2=None, op0=ALU.mult),
             [b_cf, b_prm], [b_cm])
    aneg = C.sb([128, 2], F32, "aneg"); b_aneg = Buf()
    S.op("act", lambda e: e.activation(out=aneg[:], in_=prm[:, 12:14], func=AF.Exp), [b_prm], [b_aneg])
    S.op("dve", lambda e: e.tensor_scalar(out=aneg[:], in0=aneg[:], scalar1=-1.0, scalar2=None, op0=ALU.mult), [b_aneg], [b_aneg])

    dtr = Rot([(C.sb([2, 1024], F32, "dtr"), Buf()) for _ in range(2)])
    dtv = C.sb([128, NBLK, 2], F32, "dtv"); b_dt = Buf()
    av = C.sb([128, NBLK, 2], F32, "av"); b_a = Buf()
    pt, pb = psA.next()
    for pc in range(16):
        d_t, d_b = dtr.next()
        j = (pc * 1024) // NT
        o = (pc * 1024) % NT
        S.dma("sp", d_t[:], Rdt[j, :, o:o + 1024], writes=[d_b])
        for k in range(8):
            blk = pc * 8 + k
            S.op("pe", lambda e: e.matmul(pt[:, blk * 2:blk * 2 + 2], lhsT=d_t[:, k * 128:(k + 1) * 128], rhs=ident_f[0:2, 0:2],
                                          start=True, stop=True), [d_b, b_cf], [pb])
    dtf = dtv[:].rearrange("p a b -> p (a b)")
    S.op("dve", lambda e: e.tensor_tensor(out=dtv[:], in0=pt[:, 0:256].rearrange("p (a b) -> p a b", b=2),
                                          in1=prm[:, 10:12].unsqueeze(1).to_broadcast([128, NBLK, 2]), op=ALU.add), [pb, b_prm], [b_dt])
    S.op("act", lambda e: e.activation(out=dtf, in_=dtf, func=AF.Exp), [b_dt], [b_dt])
    S.op("act", lambda e: e.activation(out=dtf, in_=dtf, func=AF.Ln, bias=1.0, scale=1.0), [b_dt], [b_dt])
    S.op("dve", lambda e: e.tensor_tensor(out=av[:], in0=dtv[:], in1=aneg[:].unsqueeze(1).to_broadcast([128, NBLK, 2]), op=ALU.mult),
         [b_dt, b_aneg], [b_a])
    acs = C.sb([128, NBLK, 2], F32, "acs"); b_acs = Buf()
    nacs = C.sb([128, NBLK, 2], F32, "nacs")
    eacs = C.sb([128, NBLK, 2], F32, "eacs")
    wdec = C.sb([128, NBLK, 2], F32, "wdec")
    acl = C.sb([128, NCH, 2], F32, "acl")
    dch = C.sb([128, NCH, 2], F32, "dch")
    pt, pb = psA.next()
    for c in range(NCH):
        b0, b1 = 2 * c, 2 * c + 1
        S.op("pe", lambda e: e.matmul(pt[:, b0 * 2:b0 * 2 + 2], lhsT=tri_f, rhs=av[:, b0, :], start=True, stop=True), [b_a, b_cf], [pb])
        S.op("pe", lambda e: e.matmul(pt[:, b1 * 2:b1 * 2 + 2], lhsT=tri_f, rhs=av[:, b1, :], start=True, stop=False), [b_a, b_cf], [pb])
        S.op("pe", lambda e: e.matmul(pt[:, b1 * 2:b1 * 2 + 2], lhsT=ones_f, rhs=av[:, b0, :], start=False, stop=True), [b_a, b_cf], [pb])
    S.op("dve", lambda e: e.tensor_copy(out=acs[:].rearrange("p a b -> p (a b)"), in_=pt[:, 0:256]), [pb], [b_acs])
    S.op("dve", lambda e: e.tensor_scalar(out=nacs[:].rearrange("p a b -> p (a b)"), in0=acs[:].rearrange("p a b -> p (a b)"),
                                          scalar1=-1.0, scalar2=None, op0=ALU.mult), [b_acs], [b_acs])
    S.op("act", lambda e: e.activation(out=eacs[:].rearrange("p a b -> p (a b)"), in_=acs[:].rearrange("p a b -> p (a b)"), func=AF.Exp),
         [b_acs], [b_acs])
    pt2, pb2 = psB.next()
    acs_last = acs[:].rearrange("p (c t) h -> p c t h", t=2)[:, :, 1, :]
    S.op("dve", lambda e: e.tensor_copy(out=acl[:], in_=acs_last), [b_acs], [b_acs])
    S.op("pe", lambda e: e.matmul(pt2[:, 0:128], lhsT=sel127_f, rhs=acl[:].rearrange("p a b -> p (a b)"), start=True, stop=True),
         [b_acs, b_cf], [pb2])
    S.op("dve", lambda e: e.tensor_copy(out=acl[:].rearrange("p a b -> p (a b)"), in_=pt2[:, 0:128]), [pb2], [b_acs])
    S.op("act", lambda e: e.activation(out=dch[:].rearrange("p a b -> p (a b)"), in_=acl[:].rearrange("p a b -> p (a b)"), func=AF.Exp),
         [b_acs], [b_acs])
    wd4 = wdec[:].rearrange("p (c t) h -> p c t h", t=2)
    S.op("dve", lambda e: e.tensor_tensor(out=wd4, in0=acl[:].unsqueeze(2).to_broadcast([128, NCH, 2, 2]),
                                          in1=acs[:].rearrange("p (c t) h -> p c t h", t=2), op=ALU.subtract), [b_acs], [b_acs])
    S.op("act", lambda e: e.activation(out=wdec[:].rearrange("p a b -> p (a b)"), in_=wdec[:].rearrange("p a b -> p (a b)"), func=AF.Exp),
         [b_acs], [b_acs])

    prev32 = C.sb([64, 128], F32, "prev32"); b_p32 = Buf()
    prevb = C.sb([64, 128], BF16, "prevb"); b_pb = Buf()
    S.op("dve", lambda e: e.memset(prev32[:], 0.0), [], [b_p32])
    S.op("dve", lambda e: e.memset(prevb[:], 0.0), [], [b_pb])
    xs_r = Rot([(C.sb([128, 2, 128], F32, "xs"), Buf()) for _ in range(2)])
    xdt_r = Rot([(C.sb([128, 2, 128], BF16, "xdt"), Buf()) for _ in range(2)])
    xw_r = Rot([(C.sb([128, 2, 128], BF16, "xw"), Buf()) for _ in range(2)])
    btok_r = Rot([(C.sb([128, 2, 64], BF16, "btok"), Buf()) for _ in range(2)])
    bct_r = Rot([(C.sb([64, 2, 256], BF16, "bct"), Buf()) for _ in range(2)])
    at_r = Rot([(C.sb([128, 2, 256], F32, "aT"), Buf()) for _ in range(2)])
    lt_r = Rot([(C.sb([128, 384], F32, "LT"), Buf()) for _ in range(2)])
    st_r = Rot([(C.sb([128, 384], BF16, "ST"), Buf()) for _ in range(2)])
    yo_r = Rot([(C.sb([128, 2, 128], F32, "yo"), Buf()) for _ in range(2)])
    yt_r = Rot([(C.sb([128, 2, 128], BF16, "yt"), Buf()) for _ in range(2)])
    yst_r = Rot([(C.sb([128, 256], BF16, "yst"), Buf()) for _ in range(2)])
    for c in range(NCH):
        b0 = 2 * c
        tok0 = c * 256
        j = tok0 // NT
        rx = [b_xp[j], b_pad] + ([b_xp[j - 1]] if (j > 0 and tok0 % NT == 0) else [])
        rbc = [b_bc[j], b_pad] + ([b_bc[j - 1]] if (j > 0 and tok0 % NT == 0) else [])
        xs_t, xs_b = xs_r.next()
        px, pxb = psA.next()
        for t in range(2):
            for k in range(4):
                o = tok0 + t * 128 + k
                S.op("pe", lambda e: e.matmul(px[:, t * 128:(t + 1) * 128], lhsT=xpre[:, o:o + 128], rhs=cm[:, k, 0:128],
                                              start=(k == 0), stop=False), rx + [b_cm], [pxb])
            S.op("pe", lambda e: e.matmul(px[:, t * 128:(t + 1) * 128], lhsT=rows_b[:, 192:320], rhs=rows_b[:, 0:128],
                                          start=False, stop=True), [b_rowsb], [pxb])
        S.op("act", lambda e: e.activation(out=xs_t[:].rearrange("p a b -> p (a b)"), in_=px[:, 0:256], func=AF.Silu), [pxb], [xs_b])
        bt_t, bt_b = btok_r.next()
        pbt, pbtb = psB.next()
        for t in range(2):
            for k in range(4):
                o = tok0 + t * 128 + k
                S.op("pe", lambda e: e.matmul(pbt[:, t * 64:(t + 1) * 64], lhsT=bcpre[:, o:o + 128], rhs=cm[:, k, 128:192],
                                              start=(k == 0), stop=False), rbc + [b_cm], [pbtb])
            S.op("pe", lambda e: e.matmul(pbt[:, t * 64:(t + 1) * 64], lhsT=rows_b[:, 192:320], rhs=rows_b[:, 128:192],
                                          start=False, stop=True), [b_rowsb], [pbtb])
        S.op("act", lambda e: e.activation(out=bt_t[:].rearrange("p a b -> p (a b)"), in_=pbt[:, 0:128], func=AF.Silu), [pbtb], [bt_b])
        bct_t, bct_b = bct_r.next()
        pbc, pbcb = psC.next()
        for which in range(2):
            for k in range(4):
                o = tok0 + k
                S.op("pe", lambda e: e.matmul(pbc[0:64, which * 256:(which + 1) * 256], lhsT=cm[:, k, 128 + 64 * which:192 + 64 * which],
                                              rhs=bcpre[:, o:o + 256], start=(k == 0), stop=(k == 3)), rbc + [b_cm], [pbcb])
        S.op("act", lambda e: e.activation(out=bct_t[:, 0, :], in_=pbc[0:64, 0:256], func=AF.Silu, bias=prm[0:64, 9:10], scale=1.0),
             [pbcb, b_prm], [bct_b])
        S.op("act", lambda e: e.activation(out=bct_t[:, 1, :], in_=pbc[0:64, 256:512], func=AF.Silu, bias=cbt[:, 0:1], scale=1.0),
             [pbcb, b_prm], [bct_b])
        xdt_t, xdt_b = xdt_r.next()
        xw_t, xw_b = xw_r.next()
        for t in range(2):
            for h in range(2):
                S.op("dve", lambda e: e.tensor_scalar(out=xdt_t[:, t, h * 64:(h + 1) * 64], in0=xs_t[:, t, h * 64:(h + 1) * 64],
                                                      scalar1=dtv[:, b0 + t, h:h + 1], scalar2=None, op0=ALU.mult), [xs_b, b_dt], [xdt_b])
                S.op("dve", lambda e: e.tensor_scalar(out=xw_t[:, t, h * 64:(h + 1) * 64], in0=xs_t[:, t, h * 64:(h + 1) * 64],
                                                      scalar1=dtv[:, b0 + t, h:h + 1], scalar2=wdec[:, b0 + t, h:h + 1],
                                                      op0=ALU.mult, op1=ALU.mult), [xs_b, b_dt, b_acs], [xw_b])
        pcb, pcbb = psA.next()
        S.op("pe", lambda e: e.matmul(pcb[:, 0:256], lhsT=bct_t[:, 0, 0:128], rhs=bct_t[:, 1, 0:256], start=True, stop=True), [bct_b], [pcbb])
        S.op("pe", lambda e: e.matmul(pcb[:, 256:384], lhsT=bct_t[:, 0, 128:256], rhs=bct_t[:, 1, 128:256], start=True, stop=True), [bct_b], [pcbb])
        pyo, pyob = psB.next()
        for t in range(2):
            S.op("pe", lambda e: e.matmul(pyo[:, t * 128:(t + 1) * 128], lhsT=bct_t[:, 1, t * 128:(t + 1) * 128], rhs=prevb[:],
                                          start=True, stop=True), [bct_b, b_pb], [pyob])
        yo_t, yo_b = yo_r.next()
        for t in range(2):
            for h in range(2):
                S.op("act", lambda e: e.activation(out=yo_t[:, t, h * 64:(h + 1) * 64], in_=pyo[:, t * 128 + h * 64:t * 128 + (h + 1) * 64],
                                                   func=AF.Copy, scale=eacs[:, b0 + t, h:h + 1]), [pyob, b_acs], [yo_b])
        py, pyb = psC.next()
        for h in range(2):
            at_t, at_b = at_r.next()
            S.op("dve", lambda e: e.tensor_scalar(out=at_t[:, 0, :], in0=trirow0_f, scalar1=av[:, b0, h:h + 1], scalar2=None, op0=ALU.mult),
                 [b_cf, b_a], [at_b])
            S.op("dve", lambda e: e.tensor_scalar(out=at_t[:, 1, 0:128], in0=tri_f, scalar1=av[:, b0 + 1, h:h + 1], scalar2=None, op0=ALU.mult),
                 [b_cf, b_a], [at_b])
            pr, prb = psM
            S.op("pe", lambda e: e.matmul(pr[:, 0:256], lhsT=ones_f, rhs=at_t[:, 0, :], start=True, stop=False), [at_b, b_cf], [prb])
            S.op("pe", lambda e: e.matmul(pr[:, 128:256], lhsT=ones_f, rhs=at_t[:, 1, 0:128], start=False, stop=False), [at_b, b_cf], [prb])
            S.op("pe", lambda e: e.matmul(pr[:, 0:128], lhsT=ident_f, rhs=neg_f, start=False, stop=True), [b_cf], [prb])
            S.op("pe", lambda e: e.matmul(pr[:, 256:384], lhsT=ones_f, rhs=at_t[:, 0, 128:256], start=True, stop=False), [at_b, b_cf], [prb])
            S.op("pe", lambda e: e.matmul(pr[:, 256:384], lhsT=ones_f, rhs=at_t[:, 1, 0:128], start=False, stop=False), [at_b, b_cf], [prb])
            S.op("pe", lambda e: e.matmul(pr[:, 256:384], lhsT=ident_f, rhs=neg_f, start=False, stop=True), [b_cf], [prb])
            lt_t, lt_b = lt_r.next()
            S.op("act", lambda e: e.activation(out=lt_t[:, 0:256], in_=pr[:, 0:256], func=AF.Exp, bias=nacs[:, b0, h:h + 1], scale=1.0),
                 [prb, b_acs], [lt_b])
            S.op("act", lambda e: e.activation(out=lt_t[:, 256:384], in_=pr[:, 256:384], func=AF.Exp, bias=nacs[:, b0 + 1, h:h + 1], scale=1.0),
                 [prb, b_acs], [lt_b])
            st_t, st_b = st_r.next()
            S.op("dve", lambda e: e.tensor_tensor(out=st_t[:, 0:384], in0=pcb[:, 0:384], in1=lt_t[:, 0:384], op=ALU.mult), [pcbb, lt_b], [st_b])
            hs = slice(h * 64, (h + 1) * 64)
            S.op("pe", lambda e: e.matmul(py[:, h * 64:(h + 1) * 64], lhsT=st_t[:, 0:128], rhs=xdt_t[:, 0, hs], start=True, stop=True),
                 [st_b, xdt_b], [pyb])
            S.op("pe", lambda e: e.matmul(py[:, 128 + h * 64:128 + (h + 1) * 64], lhsT=st_t[:, 128:256], rhs=xdt_t[:, 0, hs], start=True, stop=False),
                 [st_b, xdt_b], [pyb])
            S.op("pe", lambda e: e.matmul(py[:, 128 + h * 64:128 + (h + 1) * 64], lhsT=st_t[:, 256:384], rhs=xdt_t[:, 1, hs], start=False, stop=True),
                 [st_b, xdt_b], [pyb])
        yt_t, yt_b = yt_r.next()
        S.op("dve", lambda e: e.tensor_tensor(out=yo_t[:].rearrange("p a b -> p (a b)"), in0=py[:, 0:256], in1=yo_t[:].rearrange("p a b -> p (a b)"),
                                              op=ALU.add), [pyb, yo_b], [yo_b])
        for h in range(2):
            S.op("dve", lambda e: e.scalar_tensor_tensor(out=yt_t[:, :, h * 64:(h + 1) * 64], in0=xs_t[:, :, h * 64:(h + 1) * 64],
                                                         scalar=prm[:, 14 + h:15 + h], in1=yo_t[:, :, h * 64:(h + 1) * 64],
                                                         op0=ALU.mult, op1=ALU.add), [xs_b, yo_b, b_prm], [yt_b])
        pT, pTb = psA.next()
        for t in range(2):
            S.op("pe", lambda e: e.matmul(pT[:, t * 128:(t + 1) * 128], lhsT=yt_t[:, t, :], rhs=ident_b, start=True, stop=True),
                 [yt_b, b_cb], [pTb])
        ys_t, ys_b = yst_r.next()
        S.op("act", lambda e: e.copy(out=ys_t[:], in_=pT[:, 0:256]), [pTb], [ys_b])
        S.dma("act", O2[j, 128:256, tok0 % NT:tok0 % NT + 256], ys_t[:], reads=[ys_b], adds=[o2buf])
        pst, pstb = psB.next()
        for t in range(2):
            S.op("pe", lambda e: e.matmul(pst[0:64, 0:128], lhsT=bt_t[:, t, :], rhs=xw_t[:, t, :], start=(t == 0), stop=(t == 1)),
                 [bt_b, xw_b], [pstb])
        for h in range(2):
            S.op("dve", lambda e: e.scalar_tensor_tensor(out=prev32[:, h * 64:(h + 1) * 64], in0=prev32[:, h * 64:(h + 1) * 64],
                                                         scalar=dch[0:64, c, h:h + 1], in1=pst[0:64, h * 64:(h + 1) * 64],
                                                         op0=ALU.mult, op1=ALU.add), [b_p32, pstb, b_acs], [b_p32])
        S.op("dve", lambda e: e.tensor_copy(out=prevb[:], in_=prev32[:]), [b_p32], [b_pb])


def host_mix_params(inp, l, g):
    G = g // 2
    cw = inp["ssm_conv_w"][l]
    cbv = inp["ssm_conv_b"][l]
    f = lambda a: np.ascontiguousarray(a, dtype=np.float32)
    xsl = slice(g * 128, (g + 1) * 128)
    bsl = slice(512 + G * 64, 512 + (G + 1) * 64)
    csl = slice(640 + G * 64, 640 + (G + 1) * 64)
    hp = np.concatenate([inp["ssm_dt_bias"][l][2 * g:2 * g + 2], inp["ssm_a_log"][l][2 * g:2 * g + 2], inp["ssm_d"][l][2 * g:2 * g + 2]])
    return {
        "consts": host_consts(),
        "cwA": f(inp["sc_conv_w"][l][:, g * 64:(g + 1) * 64].T),
        "xw": f(cw[:, xsl].T),
        "xb": f(cbv[xsl][:, None]),
        "bcw": f(np.concatenate([cw[:, bsl], cw[:, csl]], axis=1).T),
        "bcb": f(np.concatenate([cbv[bsl], cbv[csl]])[:, None]),
        "cbias": f(cbv[csl][:, None]),
        "xb_row": f(cbv[xsl][None, :]),
        "bb_row": f(cbv[bsl][None, :]),
        "hp": f(np.broadcast_to(hp[None, :], (128, 6))),
    }


def tokb_weight_specs(W):
    return [
        ("winb", W["w_in"], 8, [[WIN_GROUPS[0]], [WIN_GROUPS[3]]] + [[g] for g in WIN_GROUPS[6:]]),
        ("wsc", W["w_sc_out"], 2, [[(0, 1024)]]),
        ("wsb", W["w_sb_out"], 2, [[(0, 1024)]]),
        ("wssm", W["w_ssm_out"], 4, [[(0, 1024)]]),
        ("wo", W["w_o"], 8, [[(0, 512)], [(512, 512)]]),
        ("wf1", W["w_ffn_in"], 8, [[(256 * j, 256), (FFH + 256 * j, 256)] for j in range(11)]),
        ("wf2", W["w_ffn_out"], NFC, [[(256 * j, 256)] for j in range(4)]),
    ]


def tok_b(C, T, xT_in, xT_out, Pin, wg, nw, ws):
    S = C.S
    xv = xT_in.rearrange("(kc p) t -> p kc t", p=128)
    ov = xT_out.rearrange("(kc p) t -> p kc t", p=128)
    outbuf = Buf()
    wsc = C.sb([128, 2, 1024], BF16, "wsc"); wsb = C.sb([128, 2, 1024], BF16, "wsb"); wssm = C.sb([128, 4, 1024], BF16, "wssm")
    b_wres = Buf()
    S.dma("sp", wsc[:], wg["wsc"][0][0], reads=[wg["wsc"][0][1]], adds=[b_wres])
    S.dma("sp", wsb[:], wg["wsb"][0][0], reads=[wg["wsb"][0][1]], adds=[b_wres])
    S.dma("sp", wssm[:], wg["wssm"][0][0], reads=[wg["wssm"][0][1]], adds=[b_wres])
    nwt = C.sb([128, 4], F32, "nwt")
    S.dma("sp", nwt[:], nw, adds=[b_wres])
    ya_in = C.sb([128, 2, TT], BF16, "ya_in"); b_yain = Buf()
    yb = C.sb([128, 2, TT], BF16, "yb"); b_yb = Buf()
    yc_in = C.sb([128, 4, TT], BF16, "yc_in"); b_ycin = Buf()
    ya = C.sb([128, 2, TT], BF16, "ya"); b_ya = Buf()
    gated = C.sb([128, 4, TT], F32, "gated"); b_gated = Buf()
    yc = C.sb([128, 4, TT], BF16, "yc"); b_yc = Buf()
    merged = C.sb([128, 8, TT], BF16, "merged"); b_merged = Buf()
    mo = C.sb([128, 8, TT], F32, "mo"); b_mo = Buf()
    act_a = C.sb([128, NFC, TT], BF16, "act_a"); b_acta = Buf()
    sig = Rot([(C.sb([128, TT], F32, "sig"), Buf()) for _ in range(3)])
    winb = wg["winb"]

    def post_norm_residual(gpcol):
        rms_stats(C, T, mo, b_mo, 8, T.ones_d)
        for m in range(8):
            tt, tb = T.rtmp.next()
            S.op("dve", lambda e: e.scalar_tensor_tensor(out=tt[:], in0=mo[:, m, :], scalar=T.vec[:, gpcol + m:gpcol + m + 1], in1=T.rstd[:],
                                                         op0=ALU.mult, op1=ALU.mult), [b_mo, T.b_vec, T.b_rstd], [tb])
            S.op("pool", lambda e: e.tensor_tensor(out=T.xt[:, m, :], in0=T.xt[:, m, :], in1=tt[:], op=ALU.add), [tb, T.b_xt], [T.b_xt])

    for ti in range(NTT):
        t0 = ti * TT
        S.dma("sp", T.xt[:], xv[:, :, t0:t0 + TT], writes=[T.b_xt])
        S.dma("sp", ya_in[0:64, 0, :], Pin[0, 0:64, t0:t0 + TT], writes=[b_yain])
        S.dma("sp", ya_in[64:128, 0, :], Pin[1, 0:64, t0:t0 + TT], adds=[b_yain])
        S.dma("sp", ya_in[0:64, 1, :], Pin[2, 0:64, t0:t0 + TT], adds=[b_yain])
        S.dma("sp", ya_in[64:128, 1, :], Pin[3, 0:64, t0:t0 + TT], adds=[b_yain])
        S.dma("sp", yb[0:64, 0, :], Pin[0, 64:128, t0:t0 + TT], writes=[b_yb])
        S.dma("sp", yb[64:128, 0, :], Pin[1, 64:128, t0:t0 + TT], adds=[b_yb])
        S.dma("sp", yb[0:64, 1, :], Pin[2, 64:128, t0:t0 + TT], adds=[b_yb])
        S.dma("sp", yb[64:128, 1, :], Pin[3, 64:128, t0:t0 + TT], adds=[b_yb])
        S.dma("sp", yc_in[:, 0, :], Pin[0, 128:256, t0:t0 + TT], writes=[b_ycin])
        for g in range(1, 4):
            S.dma("sp", yc_in[:, g, :], Pin[g, 128:256, t0:t0 + TT], adds=[b_ycin])
        norm_mod(C, T, 0, 8)
        w0, b0 = ws.load(winb[0])
        w3, b3 = ws.load(winb[1])
        for k in range(2):
            pp, ppb = proj_chunk(C, T, w0, b0, k * 128)
            S.op("dve", lambda e: e.tensor_tensor(out=ya[:, k, :], in0=pp[:], in1=ya_in[:, k, :], op=ALU.mult), [ppb, b_yain], [b_ya])
        for k in range(4):
            pp, ppb = proj_chunk(C, T, w3, b3, k * 128)
            st, sb_ = sig.next()
            S.op("act", lambda e: e.activation(out=st[:], in_=pp[:], func=AF.Silu), [ppb], [sb_])
            S.op("dve", lambda e: e.tensor_tensor(out=gated[:, k, :], in0=st[:], in1=yc_in[:, k, :], op=ALU.mult), [sb_, b_ycin], [b_gated])
        S.op("act", lambda e: e.activation(out=T.sq[:, 0:4, :], in_=gated[:], func=AF.Square), [b_gated], [T.b_sq])
        for grp in range(2):
            pt, pb = T.psum.next()
            for kk in range(2):
                S.op("pe", lambda e: e.matmul(pt[:], lhsT=T.ones_g[:], rhs=T.sq[:, 2 * grp + kk, :], start=(kk == 0), stop=(kk == 1)),
                     [T.b_sq, T.b_const], [pb])
            S.op("act", lambda e: e.activation(out=T.rstd[:], in_=pt[:], func=AF.Sqrt, bias=T.eps_t[:, 0:1], scale=1.0), [pb, T.b_const], [T.b_rstd])
            S.op("dve", lambda e: e.reciprocal(out=T.rstd[:], in_=T.rstd[:]), [T.b_rstd], [T.b_rstd])
            for kk in range(2):
                k = 2 * grp + kk
                S.op("dve", lambda e: e.scalar_tensor_tensor(out=yc[:, k, :], in0=gated[:, k, :], scalar=nwt[:, k:k + 1], in1=T.rstd[:],
                                                             op0=ALU.mult, op1=ALU.mult), [b_gated, b_wres, T.b_rstd], [b_yc])
        for half in range(2):
            wga, bga = ws.load(winb[2 + half])
            wgb, bgb = ws.load(winb[4 + half])
            wgc, bgc = ws.load(winb[6 + half])
            for mm in range(4):
                m = half * 4 + mm
                ms = slice(m * 128, (m + 1) * 128)
                acc = None
                for (wgt, bgt, ysrc, ybuf, nk, wres) in ((wga, bga, ya, b_ya, 2, wsc), (wgb, bgb, yb, b_yb, 2, wsb), (wgc, bgc, yc, b_yc, 4, wssm)):
                    pg, pgb = proj_chunk(C, T, wgt, bgt, mm * 128)
                    st, sb_ = sig.next()
                    S.op("act", lambda e: e.activation(out=st[:], in_=pg[:], func=AF.Sigmoid), [pgb], [sb_])
                    pbr, pbrb = T.psum.next()
                    for k in range(nk):
                        S.op("pe", lambda e: e.matmul(pbr[:], lhsT=wres[:, k, ms], rhs=ysrc[:, k, :], start=(k == 0), stop=(k == nk - 1)),
                             [b_wres, ybuf], [pbrb])
                    S.op("dve", lambda e: e.tensor_tensor(out=st[:], in0=st[:], in1=pbr[:], op=ALU.mult), [sb_, pbrb], [sb_])
                    if acc is None:
                        acc = (st, sb_)
                    else:
                        S.op("pool", lambda e: e.tensor_tensor(out=acc[0][:], in0=acc[0][:], in1=st[:], op=ALU.add), [acc[1], sb_], [acc[1]])
                S.op("pool", lambda e: e.tensor_copy(out=merged[:, m, :], in_=acc[0][:]), [acc[1]], [b_merged])
        for half in range(2):
            wo_v, wo_b = ws.load(wg["wo"][half])
            for mm in range(4):
                m = half * 4 + mm
                pt, pb = T.psum.next()
                for k in range(8):
                    S.op("pe", lambda e: e.matmul(pt[:], lhsT=wo_v[:, k, mm * 128:(mm + 1) * 128], rhs=merged[:, k, :], start=(k == 0), stop=(k == 7)),
                         [wo_b, b_merged], [pb])
                S.op("act", lambda e: e.copy(out=mo[:, m, :], in_=pt[:]), [pb], [b_mo])
        post_norm_residual(16)
        norm_mod(C, T, 24, 32)
        for j2 in range(11):
            wf, wfb = ws.load(wg["wf1"][j2])
            for jj in range(2):
                j = 2 * j2 + jj
                pgt, pgtb = proj_chunk(C, T, wf, wfb, jj * 128)
                pup, pupb = proj_chunk(C, T, wf, wfb, 256 + jj * 128)
                st, sb_ = sig.next()
                S.op("act", lambda e: e.activation(out=st[:], in_=pgt[:], func=AF.Silu), [pgtb], [sb_])
                S.op("dve", lambda e: e.tensor_tensor(out=act_a[:, j, :], in0=st[:], in1=pup[:], op=ALU.mult), [sb_, pupb], [b_acta])
        for mq in range(4):
            w2, w2b = ws.load(wg["wf2"][mq])
            for mm in range(2):
                m = mq * 2 + mm
                pt, pb = T.psum.next()
                for j in range(NFC):
                    S.op("pe", lambda e: e.matmul(pt[:], lhsT=w2[:, j, mm * 128:(mm + 1) * 128], rhs=act_a[:, j, :], start=(j == 0), stop=(j == NFC - 1)),
                         [w2b, b_acta], [pb])
                S.op("act", lambda e: e.copy(out=mo[:, m, :], in_=pt[:]), [pb], [b_mo])
        post_norm_residual(40)
        S.dma("act", ov[:, :, t0:t0 + TT], T.xt[:], reads=[T.b_xt], adds=[outbuf])
    return outbuf


def host_tokb_params(inp, l):
    f = lambda a: np.ascontiguousarray(a, dtype=np.float32)
    return {"w_sc_out": f(inp["w_sc_out"][l]), "w_sb_out": f(inp["w_sb_out"][l]), "w_ssm_out": f(inp["w_ssm_out"][l]),
            "w_o": f(inp["w_o"][l]), "w_ffn_in": f(inp["w_ffn_in"][l]), "w_ffn_out": f(inp["w_ffn_out"][l]),
            "nw": f(inp["ssm_norm_w"][l].reshape(4, 128).T)}


def _di(nc, name, shape, dt=F32):
    return nc.dram_tensor(name, list(shape), dt, kind="ExternalInput").ap()


def _do(nc, name, shape, dt=F32):
    return nc.dram_tensor(name, list(shape), dt, kind="ExternalOutput").ap()


def _mod_aps(nc, sfx=""):
    return {"cT": _di(nc, "cT" + sfx, [128, 8]), "mod_w": _di(nc, "mod_w" + sfx, [1024, 6144]),
            "mod_b6": _di(nc, "mod_b6" + sfx, [128, 48]), "g4": _di(nc, "g4" + sfx, [128, 32])}


def _mix_aps(nc, sfx=""):
    d = lambda n, s: _di(nc, n + sfx, s)
    return {"consts": d("consts", [128, 896]), "cwA": d("cwA", [64, 3]), "xw": d("xw", [128, 4]), "xb": d("xb", [128, 1]),
            "bcw": d("bcw", [128, 4]), "bcb": d("bcb", [128, 1]), "cbias": d("cbias", [64, 1]),
            "xb_row": d("xb_row", [1, 128]), "bb_row": d("bb_row", [1, 64]), "hp": d("hp", [128, 6])}


def _tokb_w_aps(nc, sfx=""):
    d = lambda n, s: _di(nc, n + sfx, s)
    return {"w_in": d("w_in", [1024, 5896]), "w_sc_out": d("w_sc_out", [256, 1024]), "w_sb_out": d("w_sb_out", [256, 1024]),
            "w_ssm_out": d("w_ssm_out", [512, 1024]), "w_o": d("w_o", [1024, 1024]), "w_ffn_in": d("w_ffn_in", [1024, 2 * FFH]),
            "w_ffn_out": d("w_ffn_out", [FFH, 1024])}


def build_tok_a():
    nc = bass.Bass("TRN2", target_bir_lowering=False)
    with contextlib.ExitStack() as es:
        C = Ctx(nc, es)
        xT = _di(nc, "xT", [1024, NT])
        P = _mod_aps(nc)
        w_in = _di(nc, "w_in", [1024, 5896])
        send = _do(nc, "send", [4, 512, NT], BF16)
        send_dt = _do(nc, "send_dt", [4, 2, NT])
        wgs = prep_weights(C, [("win", w_in, 8, [[g] for g in WIN_GROUPS[:6]])])
        T = tok_setup(C, P)
        compute_mod(C, T, P)
        ws = WStream(C, nbuf=3)
        tok_a(C, T, xT, wgs["win"], send, send_dt, ws)
        C.S.finish()
    return nc


def build_mix():
    nc = bass.Bass("TRN2", target_bir_lowering=False)
    with contextlib.ExitStack() as es:
        C = Ctx(nc, es)
        R = _di(nc, "R", [4, 512, NT], BF16)
        Rdt = _di(nc, "Rdt", [4, 2, NT])
        O2 = _do(nc, "O2", [4, 256, NT], BF16)
        P = _mix_aps(nc)
        mix_piece(C, R, Rdt, O2, P)
        C.S.finish()
    return nc


def build_tok_b():
    nc = bass.Bass("TRN2", target_bir_lowering=False)
    with contextlib.ExitStack() as es:
        C = Ctx(nc, es)
        xT = _di(nc, "xT", [1024, NT])
        P = _mod_aps(nc)
        W = _tokb_w_aps(nc)
        nw = _di(nc, "nw", [128, 4])
        Pin = _di(nc, "Pin", [4, 256, NT], BF16)
        xo = _do(nc, "xo", [1024, NT])
        wgs = prep_weights(C, tokb_weight_specs(W))
        T = tok_setup(C, P)
        compute_mod(C, T, P)
        ws = WStream(C, nbuf=3)
        tok_b(C, T, xT, xo, Pin, wgs, nw, ws)
        C.S.finish()
    return nc


def host_mod_params(inp, l, b):
    f = lambda a: np.ascontiguousarray(a, dtype=np.float32)
    return {
        "cT": f(inp["c"][b].reshape(8, 128).T),
        "mod_w": f(inp["mod_w"][l]),
        "mod_b6": f(inp["mod_b"][l].reshape(48, 128).T),
        "g4": f(np.stack([inp["g_pre_mix"][l], inp["g_post_mix"][l], inp["g_pre_ffn"][l], inp["g_post_ffn"][l]]).reshape(32, 128).T),
    }


def kernel_unfused(**inp):
    inp = {k: np.asarray(v) for k, v in inp.items()}
    x = inp["x"]
    cores = list(range(8))
    xT = [np.ascontiguousarray(x[c // 4, (c % 4) * NT:(c % 4 + 1) * NT, :].T) for c in cores]
    nc_a, nc_m, nc_b = build_tok_a(), build_mix(), build_tok_b()
    for l in range(DEPTH):
        w_in = np.ascontiguousarray(inp["w_in"][l], dtype=np.float32)
        maps = []
        for c in cores:
            m = host_mod_params(inp, l, c // 4)
            m["xT"] = xT[c]
            m["w_in"] = w_in
            maps.append(m)
        ra = run_bass_kernel_spmd(nc_a, maps, core_ids=cores).results
        maps = []
        for c in cores:
            b, g = c // 4, c % 4
            m = host_mix_params(inp, l, g)
            m["R"] = np.stack([np.asarray(ra[4 * b + j]["send"])[g] for j in range(4)])
            m["Rdt"] = np.stack([np.asarray(ra[4 * b + j]["send_dt"])[g] for j in range(4)])
            maps.append(m)
        rm = run_bass_kernel_spmd(nc_m, maps, core_ids=cores).results
        del ra
        maps = []
        tb = host_tokb_params(inp, l)
        for c in cores:
            b, g = c // 4, c % 4
            m = host_mod_params(inp, l, b)
            m.update(tb)
            m["xT"] = xT[c]
            m["w_in"] = w_in
            m["Pin"] = np.stack([np.asarray(rm[4 * b + j]["O2"])[g] for j in range(4)])
            maps.append(m)
        rb = run_bass_kernel_spmd(nc_b, maps, core_ids=cores).results
        del rm
        xT = [np.ascontiguousarray(np.asarray(rb[c]["xo"])) for c in cores]
    out = np.empty((NB, SEQ, D), np.float32)
    for c in cores:
        out[c // 4, (c % 4) * NT:(c % 4 + 1) * NT, :] = xT[c].T
    return out


def kernel(**inp):
    return kernel_unfused(**inp)
```

```python
import contextlib
import numpy as np
import ml_dtypes
import concourse.bass as bass
import concourse.mybir as mybir
from concourse.bass_utils import run_bass_kernel_spmd

F32 = mybir.dt.float32
BF16 = mybir.dt.bfloat16
AF = mybir.ActivationFunctionType
ALU = mybir.AluOpType

D = 1024
SEQ = 16384
NB = 2
DEPTH = 2
NT = 4096
TT = 512
NTT = NT // TT
EPS = 1e-6
FFH = 2816
NFC = FFH // 128
NDUMMY = 2
SAME_SYNC = True
NDS = 24
NDS_POOL = 4


class Buf:
    __slots__ = ("name", "w", "r")

    def __init__(self, name=""):
        self.name = name
        self.w = {}
        self.r = {}


class Sched:
    def __init__(self, nc, es):
        self.nc = nc
        self.engs = {"pe": nc.tensor, "act": nc.scalar, "dve": nc.vector, "pool": nc.gpsimd, "sp": nc.sync}
        self.sems = []
        self.esem = {}
        self.ecnt = {}
        for k in ("pe", "act", "dve", "pool"):
            self.esem[k] = self._newsem(es, "e_" + k)
            self.ecnt[k] = 0
        self.dsem = {}
        self.dcnt = {}
        self.dnext = {}
        self.nds = {"sp": NDS, "pool": NDS_POOL, "act": 8}
        for q in ("sp", "pool", "act"):
            self.dsem[q] = [self._newsem(es, f"d_{q}{i}") for i in range(self.nds[q])]
            self.dcnt[q] = [0] * self.nds[q]
            self.dnext[q] = 0
        self.known = {k: {} for k in self.engs}
        self.nwaits = 0
        self.nins = 0

    def _newsem(self, es, name):
        self.sems.append(es.enter_context(self.nc.semaphore(name)))
        return len(self.sems) - 1

    def _wait(self, ek, si, val):
        k = self.known[ek]
        if k.get(si, 0) >= val:
            return
        self.engs[ek].wait_ge(self.sems[si], val)
        k[si] = val
        self.nwaits += 1

    def _deps(self, ek, reads, writes):
        need = {}
        for b in reads:
            for si, v in b.w.items():
                if need.get(si, 0) < v:
                    need[si] = v
        for b in writes:
            for si, v in b.w.items():
                if need.get(si, 0) < v:
                    need[si] = v
            for si, v in b.r.items():
                if need.get(si, 0) < v:
                    need[si] = v
        own = self.esem.get(ek)
        for si, v in need.items():
            if si == own and (ek == "pe" or not SAME_SYNC):
                continue
            self._wait(ek, si, v)

    def _mark(self, si, val, reads, writes):
        for b in reads:
            if b.r.get(si, 0) < val:
                b.r[si] = val
        for b in writes:
            b.w = {si: val}
            b.r = {}

    def op(self, ek, fn, reads=(), writes=()):
        self._deps(ek, reads, writes)
        ins = fn(self.engs[ek])
        self.ecnt[ek] += 1
        si = self.esem[ek]
        ins.then_inc(self.sems[si], 1)
        self._mark(si, self.ecnt[ek], reads, writes)
        self.nins += 1

    def dma(self, q, out, in_, reads=(), writes=(), adds=()):
        self._deps(q, reads, writes)
        i = self.dnext[q]
        self.dnext[q] = (i + 1) % self.nds[q]
        si = self.dsem[q][i]
        if self.dcnt[q][i] > 0:
            self._wait(q, si, self.dcnt[q][i])
        self.engs[q].dma_start(out=out, in_=in_).then_inc(self.sems[si], 16)
        self.dcnt[q][i] += 16
        self._mark(si, self.dcnt[q][i], reads, writes)
        for b in adds:
            b.w[si] = self.dcnt[q][i]
        self.nins += 1

    def barrier(self):
        for ek in self.engs:
            for q in self.dsem:
                for i, si in enumerate(self.dsem[q]):
                    if self.dcnt[q][i] > 0:
                        self._wait(ek, si, self.dcnt[q][i])
            for k, si in self.esem.items():
                if self.ecnt[k] > 0 and k != ek:
                    self._wait(ek, si, self.ecnt[k])

    def finish(self):
        for q in self.dsem:
            for i, si in enumerate(self.dsem[q]):
                if self.dcnt[q][i] > 0:
                    self._wait("sp", si, self.dcnt[q][i])
        for k, si in self.esem.items():
            if self.ecnt[k] > 0:
                self._wait("sp", si, self.ecnt[k])


class Ctx:
    def __init__(self, nc, es):
        self.nc = nc
        self.es = es
        self.S = Sched(nc, es)
        self.n = 0
        self.cur_es = None

    def sb(self, shape, dt, name=None, es=None):
        self.n += 1
        t = (es or self.cur_es or self.es).enter_context(self.nc.sbuf_tensor(f"{name or 't'}_{self.n}", list(shape), dt))
        return t

    @contextlib.contextmanager
    def scope(self):
        prev = self.cur_es
        with contextlib.ExitStack() as es:
            self.cur_es = es
            try:
                yield es
            finally:
                self.S.barrier()
                self.cur_es = prev

    def ps(self, shape=(128, 512), dt=F32, name=None):
        self.n += 1
        return self.es.enter_context(self.nc.psum_tensor(f"{name or 'p'}_{self.n}", list(shape), dt))

    def dram(self, name, shape, dt, kind="Internal"):
        return self.nc.dram_tensor(name, list(shape), dt, kind=kind).ap()


class Rot:
    def __init__(self, items):
        self.items = items
        self.i = 0

    def next(self):
        it = self.items[self.i]
        self.i = (self.i + 1) % len(self.items)
        return it


WIN_GROUPS = [(0, 512), (512, 512), (1024, 512), (1536, 512), (2048, 512), (2560, 264)] + \
             [(2824 + 512 * i, 512) for i in range(6)]


def prep_weights(C, specs):
    S = C.S
    nc = C.nc
    out = {}
    with contextlib.ExitStack() as es:
        st32 = [(C.sb([128, 5632], F32, "st32", es), Buf()) for _ in range(2)]
        st16 = [(C.sb([128, 5632], BF16, "st16", es), Buf()) for _ in range(2)]
        r32 = Rot(st32)
        r16 = Rot(st16)
        engs = ["pool", "dve", "act"]
        ei = 0
        for key, src, KC, groups in specs:
            srcv = src.rearrange("(kc p) n -> p kc n", p=128)
            lst = []
            for gi, pieces in enumerate(groups):
                GW = sum(p[1] for p in pieces)
                dst = C.dram(f"wb_{key}_{gi}", [128, KC, GW], BF16)
                dbuf = Buf()
                t32, b32 = r32.next()
                t16, b16 = r16.next()
                v32 = t32[:, 0:KC * GW].rearrange("p (k n) -> p k n", k=KC)
                v16 = t16[:, 0:KC * GW].rearrange("p (k n) -> p k n", k=KC)
                o = 0
                for (c0, ncol) in pieces:
                    S.dma("sp", v32[:, :, o:o + ncol], srcv[:, :, c0:c0 + ncol], writes=[b32])
                    o += ncol
                ek = engs[ei % 3]
                ei += 1
                if ek == "act":
                    S.op("act", lambda e: e.copy(out=t16[:, 0:KC * GW], in_=t32[:, 0:KC * GW]), [b32], [b16])
                else:
                    S.op(ek, lambda e: e.tensor_copy(out=t16[:, 0:KC * GW], in_=t32[:, 0:KC * GW]), [b32], [b16])
                S.dma("act", dst, v16, reads=[b16], writes=[dbuf])
                lst.append((dst, dbuf))
            out[key] = lst
        S.barrier()
    return out


class WStream:
    def __init__(self, C, nbuf=3, elems=5632):
        self.C = C
        self.bufs = Rot([(C.sb([128, elems], BF16, "wst"), Buf()) for _ in range(nbuf)])

    def load(self, grp):
        dst, dbuf = grp
        KC, GW = dst.shape[1], dst.shape[2]
        t, b = self.bufs.next()
        v = t[:, 0:KC * GW].rearrange("p (k n) -> p k n", k=KC)
        self.C.S.dma("sp", v, dst, reads=[dbuf], writes=[b])
        return v, b


class TokState:
    pass


def tok_setup(C, l_params):
    S = C.S
    T = TokState()
    T.ones_d = C.sb([128, 128], BF16, "ones_d")
    T.ones_g = C.sb([128, 128], BF16, "ones_g")
    T.b_const = Buf()
    S.op("pool", lambda e: e.memset(T.ones_d[:], 1.0 / 1024.0), [], [T.b_const])
    S.op("pool", lambda e: e.memset(T.ones_g[:], 1.0 / 256.0), [], [T.b_const])
    T.eps_t = C.sb([128, 1], F32, "eps_t")
    S.op("pool", lambda e: e.memset(T.eps_t[:], EPS), [], [T.b_const])
    T.xt = C.sb([128, 8, TT], F32, "xt"); T.b_xt = Buf()
    T.hT = C.sb([128, 8, TT], BF16, "hT"); T.b_hT = Buf()
    T.sq = C.sb([128, 8, TT], BF16, "sq"); T.b_sq = Buf()
    T.rstd = C.sb([128, TT], F32, "rstd"); T.b_rstd = Buf()
    T.tmp = [(C.sb([128, TT], F32, "tmp"), Buf()) for _ in range(4)]
    T.rtmp = Rot(T.tmp)
    T.psum = Rot([(C.ps(), Buf()) for _ in range(8)])
    return T


def compute_mod(C, T, P, es_scope=None):
    S = C.S
    nc = C.nc
    T.vec = C.sb([128, 48], F32, "vec"); T.b_vec = Buf()
    with contextlib.ExitStack() as es:
        cT = C.sb([128, 8], F32, "cT", es); b_c = Buf()
        sc = C.sb([128, 8], F32, "sc", es); b_sc = Buf()
        mb = C.sb([128, 48], F32, "mb", es); b_mb = Buf()
        g4 = C.sb([128, 32], F32, "g4", es); b_g4 = Buf()
        modv = C.sb([128, 48], F32, "modv", es); b_modv = Buf()
        wbufs = Rot([(C.sb([128, 8, 512], F32, "mw", es), Buf()) for _ in range(2)])
        S.dma("sp", cT[:], P["cT"], writes=[b_c])
        S.dma("sp", mb[:], P["mod_b6"], writes=[b_mb])
        S.dma("sp", g4[:], P["g4"], writes=[b_g4])
        S.op("act", lambda e: e.activation(out=sc[:], in_=cT[:], func=AF.Silu), [b_c], [b_sc])
        mw = P["mod_w"].rearrange("(kc p) n -> p kc n", p=128)
        pt, pb = T.psum.next()
        for gi in range(12):
            wt, wb = wbufs.next()
            S.dma("sp", wt[:], mw[:, :, gi * 512:(gi + 1) * 512], writes=[wb])
            for fc in range(4):
                col = gi * 4 + fc
                for k in range(8):
                    S.op("pe", lambda e, k=k, fc=fc, col=col: e.matmul(
                        pt[:, col:col + 1], lhsT=wt[:, k, fc * 128:(fc + 1) * 128], rhs=sc[:, k:k + 1],
                        start=(k == 0), stop=(k == 7)), [wb, b_sc], [pb])
        S.op("dve", lambda e: e.tensor_tensor(out=modv[:], in0=pt[:, 0:48], in1=mb[:], op=ALU.add), [pb, b_mb], [b_modv])
        v = T.vec
        S.op("dve", lambda e: e.scalar_tensor_tensor(out=v[:, 0:8], in0=modv[:, 8:16], scalar=1.0, in1=g4[:, 0:8],
                                                     op0=ALU.add, op1=ALU.mult), [b_modv, b_g4], [T.b_vec])
        S.op("dve", lambda e: e.tensor_copy(out=v[:, 8:16], in_=modv[:, 0:8]), [b_modv], [T.b_vec])
        S.op("dve", lambda e: e.tensor_tensor(out=v[:, 16:24], in0=modv[:, 16:24], in1=g4[:, 8:16], op=ALU.mult), [b_modv, b_g4], [T.b_vec])
        S.op("dve", lambda e: e.scalar_tensor_tensor(out=v[:, 24:32], in0=modv[:, 32:40], scalar=1.0, in1=g4[:, 16:24],
                                                     op0=ALU.add, op1=ALU.mult), [b_modv, b_g4], [T.b_vec])
        S.op("dve", lambda e: e.tensor_copy(out=v[:, 32:40], in_=modv[:, 24:32]), [b_modv], [T.b_vec])
        S.op("dve", lambda e: e.tensor_tensor(out=v[:, 40:48], in0=modv[:, 40:48], in1=g4[:, 24:32], op=ALU.mult), [b_modv, b_g4], [T.b_vec])
        S.barrier()
    return T


def rms_stats(C, T, src_t, src_b, nch, ones_t):
    S = C.S
    S.op("act", lambda e: e.activation(out=T.sq[:, 0:nch, :], in_=src_t[:, 0:nch, :], func=AF.Square), [src_b], [T.b_sq])
    pt, pb = T.psum.next()
    for k in range(nch):
        S.op("pe", lambda e, k=k: e.matmul(pt[:], lhsT=ones_t[:], rhs=T.sq[:, k, :], start=(k == 0), stop=(k == nch - 1)),
             [T.b_sq, T.b_const], [pb])
    S.op("act", lambda e: e.activation(out=T.rstd[:], in_=pt[:], func=AF.Sqrt, bias=T.eps_t[:, 0:1], scale=1.0), [pb, T.b_const], [T.b_rstd])
    S.op("dve", lambda e: e.reciprocal(out=T.rstd[:], in_=T.rstd[:]), [T.b_rstd], [T.b_rstd])


def norm_mod(C, T, gcol, scol):
    S = C.S
    rms_stats(C, T, T.xt, T.b_xt, 8, T.ones_d)
    for k in range(8):
        tt, tb = T.rtmp.next()
        S.op("dve", lambda e, k=k, tt=tt: e.scalar_tensor_tensor(
            out=tt[:], in0=T.xt[:, k, :], scalar=T.vec[:, gcol + k:gcol + k + 1], in1=T.rstd[:],
            op0=ALU.mult, op1=ALU.mult), [T.b_xt, T.b_vec, T.b_rstd], [tb])
        S.op("act", lambda e, k=k, tt=tt: e.activation(
            out=T.hT[:, k, :], in_=tt[:], func=AF.Identity, bias=T.vec[:, scol + k:scol + k + 1], scale=1.0),
            [tb, T.b_vec], [T.b_hT])


def proj_chunk(C, T, wv, wb, c0, M=128):
    S = C.S
    pt, pb = T.psum.next()
    for k in range(8):
        S.op("pe", lambda e, k=k: e.matmul(pt[0:M, :], lhsT=wv[:, k, c0:c0 + M], rhs=T.hT[:, k, :],
                                           start=(k == 0), stop=(k == 7)), [wb, T.b_hT], [pb])
    return pt, pb


def tok_a(C, T, xT_in, wg, send, send_dt, ws):
    S = C.S
    xv = xT_in.rearrange("(kc p) t -> p kc t", p=128)
    stage = Rot([(C.sb([128, TT], BF16, "stg"), Buf()) for _ in range(4)])
    dts = Rot([(C.sb([8, TT], F32, "dts"), Buf()) for _ in range(2)])
    sb_send = Buf()
    for ti in range(NTT):
        t0 = ti * TT
        S.dma("sp", T.xt[:], xv[:, :, t0:t0 + TT], writes=[T.b_xt])
        norm_mod(C, T, 0, 8)

        def store_halves(st, sb_, row0, dest_of_half):
            for hf in range(2):
                S.dma("act", send[dest_of_half[hf], row0:row0 + 64, t0:t0 + TT], st[hf * 64:(hf + 1) * 64, :],
                      reads=[sb_], adds=[sb_send])

        w0, b0 = ws.load(wg[0])
        w1, b1 = ws.load(wg[1])
        for i in range(2):
            pc, pcb = proj_chunk(C, T, w0, b0, 256 + i * 128)
            tt, tb = T.rtmp.next()
            S.op("act", lambda e, tt=tt, pc=pc: e.copy(out=tt[:], in_=pc[:]), [pcb], [tb])
            px, pxb = proj_chunk(C, T, w1, b1, i * 128)
            st, sb_ = stage.next()
            S.op("dve", lambda e, tt=tt, px=px, st=st: e.tensor_tensor(out=st[:], in0=px[:], in1=tt[:], op=ALU.mult),
                 [pxb, tb], [sb_])
            store_halves(st, sb_, 0, (2 * i, 2 * i + 1))
        w2, b2 = ws.load(wg[2])
        for i in range(2):
            pq, pqb = proj_chunk(C, T, w1, b1, 256 + i * 128)
            st, sb_ = stage.next()
            S.op("act", lambda e, pq=pq, st=st: e.activation(out=st[:], in_=pq[:], func=AF.Copy, scale=0.125), [pqb], [sb_])
            store_halves(st, sb_, 64, (2 * i, 2 * i + 1))
        w4, b4 = ws.load(wg[4])
        for j in range(2):
            for i in range(2):
                pk, pkb = proj_chunk(C, T, w2, b2, j * 256 + i * 128)
                st, sb_ = stage.next()
                S.op("act", lambda e, pk=pk, st=st: e.copy(out=st[:], in_=pk[:]), [pkb], [sb_])
                store_halves(st, sb_, 128 + 64 * j, (2 * i, 2 * i + 1))
        w5, b5 = ws.load(wg[5])
        for i in range(4):
            px, pxb = proj_chunk(C, T, w4, b4, i * 128)
            st, sb_ = stage.next()
            S.op("act", lambda e, px=px, st=st: e.copy(out=st[:], in_=px[:]), [pxb], [sb_])
            S.dma("act", send[i, 256:384, t0:t0 + TT], st[:], reads=[sb_], adds=[sb_send])
        for j in range(2):
            pb_, pbb = proj_chunk(C, T, w5, b5, j * 128)
            st, sb_ = stage.next()
            S.op("act", lambda e, pb_=pb_, st=st: e.copy(out=st[:], in_=pb_[:]), [pbb], [sb_])
            for gi in range(2):
                for dd in range(2):
                    S.dma("act", send[2 * gi + dd, 384 + 64 * j:448 + 64 * j, t0:t0 + TT], st[gi * 64:(gi + 1) * 64, :],
                          reads=[sb_], adds=[sb_send])
        pd, pdb = proj_chunk(C, T, w5, b5, 256, M=8)
        dt_t, dt_b = dts.next()
        S.op("act", lambda e, pd=pd, dt_t=dt_t: e.copy(out=dt_t[:], in_=pd[0:8, :]), [pdb], [dt_b])
        for g in range(4):
            S.dma("act", send_dt[g, :, t0:t0 + TT], dt_t[2 * g:2 * g + 2, :], reads=[dt_b], adds=[sb_send])
    return sb_send


NBLK = SEQ // 128
NCH = SEQ // 256


def host_consts():
    c = np.zeros((128, 896), np.float32)
    i = np.arange(128)
    c[:, 0:128] = np.eye(128)
    c[:, 128:256] = (i[:, None] <= i[None, :])
    c[:, 256:384] = 1.0
    c[:, 384:512] = (i[:, None] > i[None, :])
    c[:, 512:640] = (i[:, None] < i[None, :])
    c[:, 640:768] = np.where(i[None, :] < i[:, None], -30000.0, 0.0)
    c[127, 768:896] = 1.0
    return c


def mix_piece(C, R, Rdt, O2, P, do_conv=True, do_attn=True, do_ssd=True, nq=NBLK):
    S = C.S
    nc = C.nc
    cf = C.sb([128, 896], F32, "cf"); b_cf = Buf()
    cb = C.sb([128, 640], BF16, "cb"); b_cb = Buf()
    S.dma("sp", cf[:], P["consts"], writes=[b_cf])
    S.op("dve", lambda e: e.tensor_copy(out=cb[:], in_=cf[:, 0:640]), [b_cf], [b_cb])
    ident_b = cb[:, 0:128]; tri_b = cb[:, 128:256]; ones_b = cb[:, 256:384]; U_b = cb[:, 384:512]
    M_f = cf[:, 512:640]
    o2buf = Buf()
    psA = Rot([(C.ps(), Buf()) for _ in range(2)])
    psB = Rot([(C.ps(), Buf()) for _ in range(2)])
    psC = Rot([(C.ps(), Buf()) for _ in range(2)])
    psL = (C.ps(), Buf())
    psM = (C.ps(), Buf())

    T1 = C.sb([128, SEQ], BF16, "T1")
    T2 = C.sb([128, 2 + SEQ], BF16, "T2")
    b_q = [Buf() for _ in range(4)]; b_v = [Buf() for _ in range(4)]
    b_k = [Buf() for _ in range(4)]; b_u = [Buf() for _ in range(4)]
    b_upad = Buf()
    S.op("pool", lambda e: e.memset(T2[64:128, 0:2], 0.0), [], [b_upad])
    for j in range(4):
        S.dma("sp", T1[0:64, j * NT:(j + 1) * NT], R[j, 64:128, :], writes=[b_q[j]])
        S.dma("sp", T2[0:64, 2 + j * NT:2 + (j + 1) * NT], R[j, 128:192, :], writes=[b_k[j]])
        S.dma("sp", T1[64:128, j * NT:(j + 1) * NT], R[j, 192:256, :], writes=[b_v[j]])
        S.dma("sp", T2[64:128, 2 + j * NT:2 + (j + 1) * NT], R[j, 0:64, :], writes=[b_u[j]])

    if do_conv:
      with C.scope():
        cw = C.sb([128, 3], F32, "cw"); b_cw = Buf()
        S.dma("sp", cw[64:128, :], P["cwA"], writes=[b_cw])
        tA = [(C.sb([128, 2048], F32, "tA"), Buf()) for _ in range(1)]
        oA = Rot([(C.sb([128, 2048], BF16, "oA"), Buf()) for _ in range(2)])
        for ch in range(8):
            t0 = ch * 2048
            j = t0 // NT
            rb = [b_u[j], b_cw, b_upad] + ([b_u[j - 1]] if (j > 0 and t0 % NT == 0) else [])
            ta, tb = tA[0]
            ot, ob = oA.next()
            S.op("dve", lambda e: e.tensor_scalar(out=ta[64:128, :], in0=T2[64:128, 2 + t0:2 + t0 + 2048], scalar1=cw[64:128, 2:3],
                                                   scalar2=None, op0=ALU.mult), rb, [tb])
            S.op("dve", lambda e: e.scalar_tensor_tensor(out=ta[64:128, :], in0=T2[64:128, 1 + t0:1 + t0 + 2048], scalar=cw[64:128, 1:2],
                                                          in1=ta[64:128, :], op0=ALU.mult, op1=ALU.add), rb + [tb], [tb])
            S.op("dve", lambda e: e.scalar_tensor_tensor(out=ot[64:128, :], in0=T2[64:128, t0:t0 + 2048], scalar=cw[64:128, 0:1],
                                                          in1=ta[64:128, :], op0=ALU.mult, op1=ALU.add), rb + [tb], [ob])
            S.dma("act", O2[j, 0:64, t0 % NT:t0 % NT + 2048], ot[64:128, :], reads=[ob], adds=[o2buf])

    if do_ssd:
      with C.scope():
        ssd_piece(C, R, Rdt, O2, P, cf, b_cf, cb, b_cb, psA, psB, psC, psL, psM, o2buf)

    if do_attn:
        vt = C.sb([128, NBLK, 64], BF16, "vt"); b_vt = [Buf() for _ in range(16)]
        for bb in range(16):
            pt, pb = psA.next()
            for k in range(8):
                blk = bb * 8 + k
                S.op("pe", lambda e: e.matmul(pt[:, k * 64:(k + 1) * 64], lhsT=T1[64:128, blk * 128:(blk + 1) * 128],
                                              rhs=ident_b[64:128, 64:128], start=True, stop=True), [b_v[blk // 32], b_cb], [pb])
            S.op("dve", lambda e: e.tensor_copy(out=vt[:, bb * 8:(bb + 1) * 8, :].rearrange("p a b -> p (a b)"), in_=pt[:]),
                 [pb], [b_vt[bb]])
        NPIPE = 3
        zero_b = C.sb([128, 128], BF16, "zero_b"); b_zero = Buf()
        S.op("pool", lambda e: e.memset(zero_b[:], 0.0), [], [b_zero])
        M_b = C.sb([128, 128], BF16, "M_b")
        S.op("dve", lambda e: e.tensor_copy(out=M_b[:], in_=M_f), [b_cf], [b_zero])
        E_r = Rot([(C.sb([128, 512], F32, "E"), Buf()) for _ in range(2)])
        SPb_r = Rot([(C.sb([128, 512], BF16, "SPb"), Buf()) for _ in range(NPIPE)])
        t1_r = Rot([(C.sb([128, 512], F32, "t1"), Buf()) for _ in range(2)])
        att_r = Rot([(C.sb([128, 512], BF16, "att"), Buf()) for _ in range(3)])
        lb_r = Rot([(C.sb([128, 512], BF16, "lbsb"), Buf()) for _ in range(NPIPE + 1)])
        ost_r = Rot([(C.sb([64, 512], BF16, "ost"), Buf()) for _ in range(2)])
        z_r = Rot(psA.items + [psM] + psB.items)
        lb_t, lb_b = psL
        nu_b = C.sb([128, 128], BF16, "nu_b")
        nident_b = C.sb([128, 128], BF16, "nident_b")
        S.op("dve", lambda e: e.tensor_scalar(out=nu_b[:], in0=cf[:, 384:512], scalar1=cf[:, 0:128], scalar2=None, op0=ALU.add) if False else
             e.tensor_tensor(out=nu_b[:], in0=cf[:, 384:512], in1=cf[:, 0:128], op=ALU.add), [b_cf], [b_zero])
        S.op("dve", lambda e: e.tensor_scalar(out=nu_b[:], in0=nu_b[:], scalar1=-1.0, scalar2=None, op0=ALU.mult), [b_zero], [b_zero])
        S.op("dve", lambda e: e.tensor_scalar(out=nident_b[:], in0=cf[:, 0:128], scalar1=-1.0, scalar2=None, op0=ALU.mult), [b_cf], [b_zero])
        nquad = (nq + 3) // 4
        groups = []
        for m in range(nquad):
            for c in reversed(range(4 * m + 4)):
                r = max(0, c - 4 * m)
                groups.append(dict(m=m, c=c, r=r, diag=(c >= 4 * m), first=(c == 4 * m + 3), last=(c == 0)))
        state = {"lbs": None, "ob": None}

        def geom(g):
            lo = g["r"] * 128
            return lo, slice(lo, 512)

        def stageA(g):
            m, c, r = g["m"], g["c"], g["r"]
            lo, cs = geom(g)
            qcols = slice((4 * m + r) * 128, (4 * m + 4) * 128)
            z_t, z_b = z_r.next()
            S.op("pe", lambda e: e.matmul(z_t[:, cs], lhsT=T2[0:64, 2 + c * 128:2 + (c + 1) * 128], rhs=T1[0:64, qcols],
                                          start=True, stop=False), [b_k[c // 32], b_q[(4 * m) // 32]], [z_b])
            g["z"] = (z_t, z_b)

        def stageB(g):
            lo, cs = geom(g)
            z_t, z_b = g["z"]
            e_t, e_b = E_r.next()
            spb_t, spb_b = SPb_r.next()
            S.op("act", lambda e: e.activation(out=e_t[:, cs], in_=z_t[:, cs], func=AF.Exp), [z_b], [e_b])
            S.op("act", lambda e: e.activation(out=spb_t[:, cs], in_=e_t[:, cs], func=AF.Ln, bias=1.0, scale=1.0), [e_b], [spb_b])
            if g["diag"]:
                S.op("pool", lambda e: e.tensor_tensor(out=spb_t[:, lo:lo + 128], in0=spb_t[:, lo:lo + 128], in1=M_b[:], op=ALU.mult),
                     [spb_b, b_zero], [spb_b])
            g["spb"] = (spb_t, spb_b)

        def stageC(g):
            lo, cs = geom(g)
            z_t, z_b = g["z"]
            spb_t, spb_b = g["spb"]
            lbs = state["lbs"]
            has_later = not g["first"]
            S.op("pe", lambda e: e.matmul(z_t[:, cs], lhsT=nu_b[:], rhs=spb_t[:, cs], start=False, stop=not has_later), [spb_b, b_zero], [z_b])
            if has_later:
                llo = lo + 128 if g["diag"] else lo
                S.op("pe", lambda e: e.matmul(z_t[:, llo:512], lhsT=nident_b[:], rhs=lbs[0][:, llo:512], start=False, stop=True),
                     [lbs[1], b_zero], [z_b])
            if g["first"]:
                S.op("pe", lambda e: e.matmul(lb_t[:, 0:512], lhsT=zero_b[:], rhs=T1[:, 0:512], start=True, stop=False),
                     [b_zero, b_q[0], b_v[0]], [lb_b])
            if not g["last"]:
                S.op("pe", lambda e: e.matmul(lb_t[:, cs], lhsT=ones_b, rhs=spb_t[:, cs], start=False, stop=False), [spb_b, b_cb], [lb_b])
                nl = lb_r.next()
                S.op("dve", lambda e: e.tensor_copy(out=nl[0][:], in_=lb_t[:, 0:512]), [lb_b], [nl[1]])
                state["lbs"] = nl

        def stageE(g):
            lo, cs = geom(g)
            z_t, z_b = g["z"]
            a_t, a_b = att_r.next()
            S.op("act", lambda e: e.activation(out=a_t[:, cs], in_=z_t[:, cs], func=AF.Exp), [z_b], [a_b])
            if g["diag"]:
                S.op("pool", lambda e: e.tensor_tensor(out=a_t[:, lo:lo + 128], in0=a_t[:, lo:lo + 128], in1=M_b[:], op=ALU.mult),
                     [a_b, b_zero], [a_b])
            g["att"] = (a_t, a_b)

        def stageF(g):
            m, c = g["m"], g["c"]
            lo, cs = geom(g)
            a_t, a_b = g["att"]
            if g["first"]:
                state["ob"] = psC.next()
                ob_t, ob_b = state["ob"]
                S.op("pe", lambda e: e.matmul(ob_t[0:64, 0:512], lhsT=zero_b[:, 0:64], rhs=T1[:, 0:512], start=True, stop=False),
                     [b_zero, b_q[0], b_v[0]], [ob_b])
            ob_t, ob_b = state["ob"]
            S.op("pe", lambda e: e.matmul(ob_t[0:64, cs], lhsT=vt[:, c, :], rhs=a_t[:, cs], start=False, stop=g["last"]),
                 [b_vt[c // 8], a_b], [ob_b])
            if g["last"]:
                ost = ost_r.next()
                S.op("dve", lambda e: e.tensor_copy(out=ost[0][:], in_=ob_t[0:64, 0:512]), [ob_b], [ost[1]])
                t0 = m * 512
                S.dma("act", O2[t0 // NT, 64:128, t0 % NT:t0 % NT + 512], ost[0][:], reads=[ost[1]], adds=[o2buf])
            for key in ("z", "spb", "att"):
                g.pop(key, None)

        NG = len(groups)
        for k in range(-3, NG + 1):
            if 0 <= k + 3 < NG:
                stageA(groups[k + 3])
            if 0 <= k + 2 < NG:
                stageB(groups[k + 2])
            if 0 <= k + 1 < NG:
                stageC(groups[k + 1])
            if 0 <= k < NG:
                stageE(groups[k])
            if 0 <= k - 1 < NG:
                stageF(groups[k - 1])
    return o2buf


def ssd_piece(C, R, Rdt, O2, P, cf, b_cf, cb, b_cb, psA, psB, psC, psL, psM, o2buf):
    S = C.S
    ident_f = cf[:, 0:128]; tri_f = cf[:, 128:256]; ones_f = cf[:, 256:384]; trirow0_f = cf[:, 128:384]
    neg_f = cf[:, 640:768]; sel127_f = cf[:, 768:896]
    ident_b = cb[:, 0:128]
    xpre = C.sb([128, 3 + SEQ], BF16, "xpre"); b_xp = [Buf() for _ in range(4)]
    bcpre = C.sb([128, 3 + SEQ], BF16, "bcpre"); b_bc = [Buf() for _ in range(4)]
    b_pad = Buf()
    S.op("pool", lambda e: e.memset(xpre[:, 0:3], 0.0), [], [b_pad])
    S.op("pool", lambda e: e.memset(bcpre[:, 0:3], 0.0), [], [b_pad])
    for j in range(4):
        S.dma("sp", xpre[:, 3 + j * NT:3 + (j + 1) * NT], R[j, 256:384, :], writes=[b_xp[j]])
        S.dma("sp", bcpre[:, 3 + j * NT:3 + (j + 1) * NT], R[j, 384:512, :], writes=[b_bc[j]])
    prm = C.sb([128, 16], F32, "prm"); b_prm = Buf()
    S.dma("sp", prm[:, 0:4], P["xw"], adds=[b_prm])
    S.dma("sp", prm[:, 4:5], P["xb"], adds=[b_prm])
    S.dma("sp", prm[:, 5:9], P["bcw"], adds=[b_prm])
    S.dma("sp", prm[:, 9:10], P["bcb"], adds=[b_prm])
    S.dma("sp", prm[:, 10:16], P["hp"], adds=[b_prm])
    cbt = C.sb([64, 1], F32, "cbt")
    S.dma("sp", cbt[:], P["cbias"], adds=[b_prm])
    rows = C.sb([1, 192], F32, "rows"); b_rows = Buf()
    S.dma("sp", rows[:, 0:128], P["xb_row"], adds=[b_rows])
    S.dma("sp", rows[:, 128:192], P["bb_row"], adds=[b_rows])
    rows_b = C.sb([1, 320], BF16, "rows_b"); b_rowsb = Buf()
    S.op("dve", lambda e: e.tensor_copy(out=rows_b[:, 0:192], in_=rows[:, :]), [b_rows], [b_rowsb])
    S.op("dve", lambda e: e.memset(rows_b[:, 192:320], 1.0), [], [b_rowsb])
    cm = C.sb([128, 4, 256], BF16, "cm"); b_cm = Buf()
    for k in range(4):
        S.op("dve", lambda e: e.tensor_scalar(out=cm[:, k, 0:128], in0=ident_f, scalar1=prm[:, k:k + 1], scalar2=None, op0=ALU.mult),
             [b_cf, b_prm], [b_cm])
        S.op("dve", lambda e: e.tensor_scalar(out=cm[:, k, 128:256], in0=ident_f, scalar1=prm[:, 5 + k:6 + k], scalar2=None, op0=ALU.mult),
             [b_cf, b_prm], [b_cm])
    aneg = C.sb([128, 2], F32, "aneg"); b_aneg = Buf()
    S.op("act", lambda e: e.activation(out=aneg[:], in_=prm[:, 12:14], func=AF.Exp), [b_prm], [b_aneg])
    S.op("dve", lambda e: e.tensor_scalar(out=aneg[:], in0=aneg[:], scalar1=-1.0, scalar2=None, op0=ALU.mult), [b_aneg], [b_aneg])

    dtr = Rot([(C.sb([2, 1024], F32, "dtr"), Buf()) for _ in range(2)])
    dtv = C.sb([128, NBLK, 2], F32, "dtv"); b_dt = Buf()
    av = C.sb([128, NBLK, 2], F32, "av"); b_a = Buf()
    pt, pb = psA.next()
    for pc in range(16):
        d_t, d_b = dtr.next()
        j = (pc * 1024) // NT
        o = (pc * 1024) % NT
        S.dma("sp", d_t[:], Rdt[j, :, o:o + 1024], writes=[d_b])
        for k in range(8):
            blk = pc * 8 + k
            S.op("pe", lambda e: e.matmul(pt[:, blk * 2:blk * 2 + 2], lhsT=d_t[:, k * 128:(k + 1) * 128], rhs=ident_f[0:2, 0:2],
                                          start=True, stop=True), [d_b, b_cf], [pb])
    dtf = dtv[:].rearrange("p a b -> p (a b)")
    S.op("dve", lambda e: e.tensor_tensor(out=dtv[:], in0=pt[:, 0:256].rearrange("p (a b) -> p a b", b=2),
                                          in1=prm[:, 10:12].unsqueeze(1).to_broadcast([128, NBLK, 2]), op=ALU.add), [pb, b_prm], [b_dt])
    S.op("act", lambda e: e.activation(out=dtf, in_=dtf, func=AF.Exp), [b_dt], [b_dt])
    S.op("act", lambda e: e.activation(out=dtf, in_=dtf, func=AF.Ln, bias=1.0, scale=1.0), [b_dt], [b_dt])
    S.op("dve", lambda e: e.tensor_tensor(out=av[:], in0=dtv[:], in1=aneg[:].unsqueeze(1).to_broadcast([128, NBLK, 2]), op=ALU.mult),
         [b_dt, b_aneg], [b_a])
    acs = C.sb([128, NBLK, 2], F32, "acs"); b_acs = Buf()
    nacs = C.sb([128, NBLK, 2], F32, "nacs")
    eacs = C.sb([128, NBLK, 2], F32, "eacs")
    wdec = C.sb([128, NBLK, 2], F32, "wdec")
    acl = C.sb([128, NCH, 2], F32, "acl")
    dch = C.sb([128, NCH, 2], F32, "dch")
    pt, pb = psA.next()
    for c in range(NCH):
        b0, b1 = 2 * c, 2 * c + 1
        S.op("pe", lambda e: e.matmul(pt[:, b0 * 2:b0 * 2 + 2], lhsT=tri_f, rhs=av[:, b0, :], start=True, stop=True), [b_a, b_cf], [pb])
        S.op("pe", lambda e: e.matmul(pt[:, b1 * 2:b1 * 2 + 2], lhsT=tri_f, rhs=av[:, b1, :], start=True, stop=False), [b_a, b_cf], [pb])
        S.op("pe", lambda e: e.matmul(pt[:, b1 * 2:b1 * 2 + 2], lhsT=ones_f, rhs=av[:, b0, :], start=False, stop=True), [b_a, b_cf], [pb])
    S.op("dve", lambda e: e.tensor_copy(out=acs[:].rearrange("p a b -> p (a b)"), in_=pt[:, 0:256]), [pb], [b_acs])
    S.op("dve", lambda e: e.tensor_scalar(out=nacs[:].rearrange("p a b -> p (a b)"), in0=acs[:].rearrange("p a b -> p (a b)"),
                                          scalar1=-1.0, scalar2=None, op0=ALU.mult), [b_acs], [b_acs])
    S.op("act", lambda e: e.activation(out=eacs[:].rearrange("p a b -> p (a b)"), in_=acs[:].rearrange("p a b -> p (a b)"), func=AF.Exp),
         [b_acs], [b_acs])
    pt2, pb2 = psB.next()
    acs_last = acs[:].rearrange("p (c t) h -> p c t h", t=2)[:, :, 1, :]
    S.op("dve", lambda e: e.tensor_copy(out=acl[:], in_=acs_last), [b_acs], [b_acs])
    S.op("pe", lambda e: e.matmul(pt2[:, 0:128], lhsT=sel127_f, rhs=acl[:].rearrange("p a b -> p (a b)"), start=True, stop=True),
         [b_acs, b_cf], [pb2])
    S.op("dve", lambda e: e.tensor_copy(out=acl[:].rearrange("p a b -> p (a b)"), in_=pt2[:, 0:128]), [pb2], [b_acs])
    S.op("act", lambda e: e.activation(out=dch[:].rearrange("p a b -> p (a b)"), in_=acl[:].rearrange("p a b -> p (a b)"), func=AF.Exp),
         [b_acs], [b_acs])
    wd4 = wdec[:].rearrange("p (c t) h -> p c t h", t=2)
    S.op("dve", lambda e: e.tensor_tensor(out=wd4, in0=acl[:].unsqueeze(2).to_broadcast([128, NCH, 2, 2]),
                                          in1=acs[:].rearrange("p (c t) h -> p c t h", t=2), op=ALU.subtract), [b_acs], [b_acs])
    S.op("act", lambda e: e.activation(out=wdec[:].rearrange("p a b -> p (a b)"), in_=wdec[:].rearrange("p a b -> p (a b)"), func=AF.Exp),
         [b_acs], [b_acs])

    prev32 = C.sb([64, 128], F32, "prev32"); b_p32 = Buf()
    prevb = C.sb([64, 128], BF16, "prevb"); b_pb = Buf()
    S.op("dve", lambda e: e.memset(prev32[:], 0.0), [], [b_p32])
    S.op("dve", lambda e: e.memset(prevb[:], 0.0), [], [b_pb])
    xs_r = Rot([(C.sb([128, 2, 128], F32, "xs"), Buf()) for _ in range(2)])
    xdt_r = Rot([(C.sb([128, 2, 128], BF16, "xdt"), Buf()) for _ in range(2)])
    xw_r = Rot([(C.sb([128, 2, 128], BF16, "xw"), Buf()) for _ in range(2)])
    btok_r = Rot([(C.sb([128, 2, 64], BF16, "btok"), Buf()) for _ in range(2)])
    bct_r = Rot([(C.sb([64, 2, 256], BF16, "bct"), Buf()) for _ in range(2)])
    at_r = Rot([(C.sb([128, 2, 256], F32, "aT"), Buf()) for _ in range(2)])
    lt_r = Rot([(C.sb([128, 384], F32, "LT"), Buf()) for _ in range(2)])
    st_r = Rot([(C.sb([128, 384], BF16, "ST"), Buf()) for _ in range(2)])
    yo_r = Rot([(C.sb([128, 2, 128], F32, "yo"), Buf()) for _ in range(2)])
    yt_r = Rot([(C.sb([128, 2, 128], BF16, "yt"), Buf()) for _ in range(2)])
    yst_r = Rot([(C.sb([128, 256], BF16, "yst"), Buf()) for _ in range(2)])
    for c in range(NCH):
        b0 = 2 * c
        tok0 = c * 256
        j = tok0 // NT
        rx = [b_xp[j], b_pad] + ([b_xp[j - 1]] if (j > 0 and tok0 % NT == 0) else [])
        rbc = [b_bc[j], b_pad] + ([b_bc[j - 1]] if (j > 0 and tok0 % NT == 0) else [])
        xs_t, xs_b = xs_r.next()
        px, pxb = psA.next()
        for t in range(2):
            for k in range(4):
                o = tok0 + t * 128 + k
                S.op("pe", lambda e: e.matmul(px[:, t * 128:(t + 1) * 128], lhsT=xpre[:, o:o + 128], rhs=cm[:, k, 0:128],
                                              start=(k == 0), stop=False), rx + [b_cm], [pxb])
            S.op("pe", lambda e: e.matmul(px[:, t * 128:(t + 1) * 128], lhsT=rows_b[:, 192:320], rhs=rows_b[:, 0:128],
                                          start=False, stop=True), [b_rowsb], [pxb])
        S.op("act", lambda e: e.activation(out=xs_t[:].rearrange("p a b -> p (a b)"), in_=px[:, 0:256], func=AF.Silu), [pxb], [xs_b])
        bt_t, bt_b = btok_r.next()
        pbt, pbtb = psB.next()
        for t in range(2):
            for k in range(4):
                o = tok0 + t * 128 + k
                S.op("pe", lambda e: e.matmul(pbt[:, t * 64:(t + 1) * 64], lhsT=bcpre[:, o:o + 128], rhs=cm[:, k, 128:192],
                                              start=(k == 0), stop=False), rbc + [b_cm], [pbtb])
            S.op("pe", lambda e: e.matmul(pbt[:, t * 64:(t + 1) * 64], lhsT=rows_b[:, 192:320], rhs=rows_b[:, 128:192],
                                          start=False, stop=True), [b_rowsb], [pbtb])
        S.op("act", lambda e: e.activation(out=bt_t[:].rearrange("p a b -> p (a b)"), in_=pbt[:, 0:128], func=AF.Silu), [pbtb], [bt_b])
        bct_t, bct_b = bct_r.next()
        pbc, pbcb = psC.next()
        for which in range(2):
            for k in range(4):
                o = tok0 + k
                S.op("pe", lambda e: e.matmul(pbc[0:64, which * 256:(which + 1) * 256], lhsT=cm[:, k, 128 + 64 * which:192 + 64 * which],
                                              rhs=bcpre[:, o:o + 256], start=(k == 0), stop=(k == 3)), rbc + [b_cm], [pbcb])
        S.op("act", lambda e: e.activation(out=bct_t[:, 0, :], in_=pbc[0:64, 0:256], func=AF.Silu, bias=prm[0:64, 9:10], scale=1.0),
             [pbcb, b_prm], [bct_b])
        S.op("act", lambda e: e.activation(out=bct_t[:, 1, :], in_=pbc[0:64, 256:512], func=AF.Silu, bias=cbt[:, 0:1], scale=1.0),
             [pbcb, b_prm], [bct_b])
        xdt_t, xdt_b = xdt_r.next()
        xw_t, xw_b = xw_r.next()
        for t in range(2):
            for h in range(2):
                S.op("dve", lambda e: e.tensor_scalar(out=xdt_t[:, t, h * 64:(h + 1) * 64], in0=xs_t[:, t, h * 64:(h + 1) * 64],
                                                      scalar1=dtv[:, b0 + t, h:h + 1], scalar2=None, op0=ALU.mult), [xs_b, b_dt], [xdt_b])
                S.op("dve", lambda e: e.tensor_scalar(out=xw_t[:, t, h * 64:(h + 1) * 64], in0=xs_t[:, t, h * 64:(h + 1) * 64],
                                                      scalar1=dtv[:, b0 + t, h:h + 1], scalar2=wdec[:, b0 + t, h:h + 1],
                                                      op0=ALU.mult, op1=ALU.mult), [xs_b, b_dt, b_acs], [xw_b])
        pcb, pcbb = psA.next()
        S.op("pe", lambda e: e.matmul(pcb[:, 0:256], lhsT=bct_t[:, 0, 0:128], rhs=bct_t[:, 1, 0:256], start=True, stop=True), [bct_b], [pcbb])
        S.op("pe", lambda e: e.matmul(pcb[:, 256:384], lhsT=bct_t[:, 0, 128:256], rhs=bct_t[:, 1, 128:256], start=True, stop=True), [bct_b], [pcbb])
        pyo, pyob = psB.next()
        for t in range(2):
            S.op("pe", lambda e: e.matmul(pyo[:, t * 128:(t + 1) * 128], lhsT=bct_t[:, 1, t * 128:(t + 1) * 128], rhs=prevb[:],
                                          start=True, stop=True), [bct_b, b_pb], [pyob])
        yo_t, yo_b = yo_r.next()
        for t in range(2):
            for h in range(2):
                S.op("act", lambda e: e.activation(out=yo_t[:, t, h * 64:(h + 1) * 64], in_=pyo[:, t * 128 + h * 64:t * 128 + (h + 1) * 64],
                                                   func=AF.Copy, scale=eacs[:, b0 + t, h:h + 1]), [pyob, b_acs], [yo_b])
        py, pyb = psC.next()
        for h in range(2):
            at_t, at_b = at_r.next()
            S.op("dve", lambda e: e.tensor_scalar(out=at_t[:, 0, :], in0=trirow0_f, scalar1=av[:, b0, h:h + 1], scalar2=None, op0=ALU.mult),
                 [b_cf, b_a], [at_b])
            S.op("dve", lambda e: e.tensor_scalar(out=at_t[:, 1, 0:128], in0=tri_f, scalar1=av[:, b0 + 1, h:h + 1], scalar2=None, op0=ALU.mult),
                 [b_cf, b_a], [at_b])
            pr, prb = psM
            S.op("pe", lambda e: e.matmul(pr[:, 0:256], lhsT=ones_f, rhs=at_t[:, 0, :], start=True, stop=False), [at_b, b_cf], [prb])
            S.op("pe", lambda e: e.matmul(pr[:, 128:256], lhsT=ones_f, rhs=at_t[:, 1, 0:128], start=False, stop=False), [at_b, b_cf], [prb])
            S.op("pe", lambda e: e.matmul(pr[:, 0:128], lhsT=ident_f, rhs=neg_f, start=False, stop=True), [b_cf], [prb])
            S.op("pe", lambda e: e.matmul(pr[:, 256:384], lhsT=ones_f, rhs=at_t[:, 0, 128:256], start=True, stop=False), [at_b, b_cf], [prb])
            S.op("pe", lambda e: e.matmul(pr[:, 256:384], lhsT=ones_f, rhs=at_t[:, 1, 0:128], start=False, stop=False), [at_b, b_cf], [prb])
            S.op("pe", lambda e: e.matmul(pr[:, 256:384], lhsT=ident_f, rhs=neg_f, start=False, stop=True), [b_cf], [prb])
            lt_t, lt_b = lt_r.next()
            S.op("act", lambda e: e.activation(out=lt_t[:, 0:256], in_=pr[:, 0:256], func=AF.Exp, bias=nacs[:, b0, h:h + 1], scale=1.0),
                 [prb, b_acs], [lt_b])
            S.op("act", lambda e: e.activation(out=lt_t[:, 256:384], in_=pr[:, 256:384], func=AF.Exp, bias=nacs[:, b0 + 1, h:h + 1], scale=1.0),
                 [prb, b_acs], [lt_b])
            st_t, st_b = st_r.next()
            S.op("dve", lambda e: e.tensor_tensor(out=st_t[:, 0:384], in0=pcb[:, 0:384], in1=lt_t[:, 0:384], op=ALU.mult), [pcbb, lt_b], [st_b])
            hs = slice(h * 64, (h + 1) * 64)
            S.op("pe", lambda e: e.matmul(py[:, h * 64:(h + 1) * 64], lhsT=st_t[:, 0:128], rhs=xdt_t[:, 0, hs], start=True, stop=True),
                 [st_b, xdt_b], [pyb])
            S.op("pe", lambda e: e.matmul(py[:, 128 + h * 64:128 + (h + 1) * 64], lhsT=st_t[:, 128:256], rhs=xdt_t[:, 0, hs], start=True, stop=False),
                 [st_b, xdt_b], [pyb])
            S.op("pe", lambda e: e.matmul(py[:, 128 + h * 64:128 + (h + 1) * 64], lhsT=st_t[:, 256:384], rhs=xdt_t[:, 1, hs], start=False, stop=True),
                 [st_b, xdt_b], [pyb])
        yt_t, yt_b = yt_r.next()
        S.op("dve", lambda e: e.tensor_tensor(out=yo_t[:].rearrange("p a b -> p (a b)"), in0=py[:, 0:256], in1=yo_t[:].rearrange("p a b -> p (a b)"),
                                              op=ALU.add), [pyb, yo_b], [yo_b])
        for h in range(2):
            S.op("dve", lambda e: e.scalar_tensor_tensor(out=yt_t[:, :, h * 64:(h + 1) * 64], in0=xs_t[:, :, h * 64:(h + 1) * 64],
                                                         scalar=prm[:, 14 + h:15 + h], in1=yo_t[:, :, h * 64:(h + 1) * 64],
                                                         op0=ALU.mult, op1=ALU.add), [xs_b, yo_b, b_prm], [yt_b])
        pT, pTb = psA.next()
        for t in range(2):
            S.op("pe", lambda e: e.matmul(pT[:, t * 128:(t + 1) * 128], lhsT=yt_t[:, t, :], rhs=ident_b, start=True, stop=True),
                 [yt_b, b_cb], [pTb])
        ys_t, ys_b = yst_r.next()
        S.op("act", lambda e: e.copy(out=ys_t[:], in_=pT[:, 0:256]), [pTb], [ys_b])
        S.dma("act", O2[j, 128:256, tok0 % NT:tok0 % NT + 256], ys_t[:], reads=[ys_b], adds=[o2buf])
        pst, pstb = psB.next()
        for t in range(2):
            S.op("pe", lambda e: e.matmul(pst[0:64, 0:128], lhsT=bt_t[:, t, :], rhs=xw_t[:, t, :], start=(t == 0), stop=(t == 1)),
                 [bt_b, xw_b], [pstb])
        for h in range(2):
            S.op("dve", lambda e: e.scalar_tensor_tensor(out=prev32[:, h * 64:(h + 1) * 64], in0=prev32[:, h * 64:(h + 1) * 64],
                                                         scalar=dch[0:64, c, h:h + 1], in1=pst[0:64, h * 64:(h + 1) * 64],
                                                         op0=ALU.mult, op1=ALU.add), [b_p32, pstb, b_acs], [b_p32])
        S.op("dve", lambda e: e.tensor_copy(out=prevb[:], in_=prev32[:]), [b_p32], [b_pb])


def host_mix_params(inp, l, g):
    G = g // 2
    cw = inp["ssm_conv_w"][l]
    cbv = inp["ssm_conv_b"][l]
    f = lambda a: np.ascontiguousarray(a, dtype=np.float32)
    xsl = slice(g * 128, (g + 1) * 128)
    bsl = slice(512 + G * 64, 512 + (G + 1) * 64)
    csl = slice(640 + G * 64, 640 + (G + 1) * 64)
    hp = np.concatenate([inp["ssm_dt_bias"][l][2 * g:2 * g + 2], inp["ssm_a_log"][l][2 * g:2 * g + 2], inp["ssm_d"][l][2 * g:2 * g + 2]])
    return {
        "consts": host_consts(),
        "cwA": f(inp["sc_conv_w"][l][:, g * 64:(g + 1) * 64].T),
        "xw": f(cw[:, xsl].T),
        "xb": f(cbv[xsl][:, None]),
        "bcw": f(np.concatenate([cw[:, bsl], cw[:, csl]], axis=1).T),
        "bcb": f(np.concatenate([cbv[bsl], cbv[csl]])[:, None]),
        "cbias": f(cbv[csl][:, None]),
        "xb_row": f(cbv[xsl][None, :]),
        "bb_row": f(cbv[bsl][None, :]),
        "hp": f(np.broadcast_to(hp[None, :], (128, 6))),
    }


def tokb_weight_specs(W):
    return [
        ("winb", W["w_in"], 8, [[WIN_GROUPS[0]], [WIN_GROUPS[3]]] + [[g] for g in WIN_GROUPS[6:]]),
        ("wsc", W["w_sc_out"], 2, [[(0, 1024)]]),
        ("wsb", W["w_sb_out"], 2, [[(0, 1024)]]),
        ("wssm", W["w_ssm_out"], 4, [[(0, 1024)]]),
        ("wo", W["w_o"], 8, [[(0, 512)], [(512, 512)]]),
        ("wf1", W["w_ffn_in"], 8, [[(256 * j, 256), (FFH + 256 * j, 256)] for j in range(11)]),
        ("wf2", W["w_ffn_out"], NFC, [[(256 * j, 256)] for j in range(4)]),
    ]


def tok_b(C, T, xT_in, xT_out, Pin, wg, nw, ws):
    S = C.S
    xv = xT_in.rearrange("(kc p) t -> p kc t", p=128)
    ov = xT_out.rearrange("(kc p) t -> p kc t", p=128)
    outbuf = Buf()
    wsc = C.sb([128, 2, 1024], BF16, "wsc"); wsb = C.sb([128, 2, 1024], BF16, "wsb"); wssm = C.sb([128, 4, 1024], BF16, "wssm")
    b_wres = Buf()
    S.dma("sp", wsc[:], wg["wsc"][0][0], reads=[wg["wsc"][0][1]], adds=[b_wres])
    S.dma("sp", wsb[:], wg["wsb"][0][0], reads=[wg["wsb"][0][1]], adds=[b_wres])
    S.dma("sp", wssm[:], wg["wssm"][0][0], reads=[wg["wssm"][0][1]], adds=[b_wres])
    nwt = C.sb([128, 4], F32, "nwt")
    S.dma("sp", nwt[:], nw, adds=[b_wres])
    ya_in = C.sb([128, 2, TT], BF16, "ya_in"); b_yain = Buf()
    yb = C.sb([128, 2, TT], BF16, "yb"); b_yb = Buf()
    yc_in = C.sb([128, 4, TT], BF16, "yc_in"); b_ycin = Buf()
    ya = C.sb([128, 2, TT], BF16, "ya"); b_ya = Buf()
    gated = C.sb([128, 4, TT], F32, "gated"); b_gated = Buf()
    yc = C.sb([128, 4, TT], BF16, "yc"); b_yc = Buf()
    merged = C.sb([128, 8, TT], BF16, "merged"); b_merged = Buf()
    mo = C.sb([128, 8, TT], F32, "mo"); b_mo = Buf()
    act_a = C.sb([128, NFC, TT], BF16, "act_a"); b_acta = Buf()
    sig = Rot([(C.sb([128, TT], F32, "sig"), Buf()) for _ in range(3)])
    winb = wg["winb"]

    def post_norm_residual(gpcol):
        rms_stats(C, T, mo, b_mo, 8, T.ones_d)
        for m in range(8):
            tt, tb = T.rtmp.next()
            S.op("dve", lambda e: e.scalar_tensor_tensor(out=tt[:], in0=mo[:, m, :], scalar=T.vec[:, gpcol + m:gpcol + m + 1], in1=T.rstd[:],
                                                         op0=ALU.mult, op1=ALU.mult), [b_mo, T.b_vec, T.b_rstd], [tb])
            S.op("pool", lambda e: e.tensor_tensor(out=T.xt[:, m, :], in0=T.xt[:, m, :], in1=tt[:], op=ALU.add), [tb, T.b_xt], [T.b_xt])

    for ti in range(NTT):
        t0 = ti * TT
        S.dma("sp", T.xt[:], xv[:, :, t0:t0 + TT], writes=[T.b_xt])
        S.dma("sp", ya_in[0:64, 0, :], Pin[0, 0:64, t0:t0 + TT], writes=[b_yain])
        S.dma("sp", ya_in[64:128, 0, :], Pin[1, 0:64, t0:t0 + TT], adds=[b_yain])
        S.dma("sp", ya_in[0:64, 1, :], Pin[2, 0:64, t0:t0 + TT], adds=[b_yain])
        S.dma("sp", ya_in[64:128, 1, :], Pin[3, 0:64, t0:t0 + TT], adds=[b_yain])
        S.dma("sp", yb[0:64, 0, :], Pin[0, 64:128, t0:t0 + TT], writes=[b_yb])
        S.dma("sp", yb[64:128, 0, :], Pin[1, 64:128, t0:t0 + TT], adds=[b_yb])
        S.dma("sp", yb[0:64, 1, :], Pin[2, 64:128, t0:t0 + TT], adds=[b_yb])
        S.dma("sp", yb[64:128, 1, :], Pin[3, 64:128, t0:t0 + TT], adds=[b_yb])
        S.dma("sp", yc_in[:, 0, :], Pin[0, 128:256, t0:t0 + TT], writes=[b_ycin])
        for g in range(1, 4):
            S.dma("sp", yc_in[:, g, :], Pin[g, 128:256, t0:t0 + TT], adds=[b_ycin])
        norm_mod(C, T, 0, 8)
        w0, b0 = ws.load(winb[0])
        w3, b3 = ws.load(winb[1])
        for k in range(2):
            pp, ppb = proj_chunk(C, T, w0, b0, k * 128)
            S.op("dve", lambda e: e.tensor_tensor(out=ya[:, k, :], in0=pp[:], in1=ya_in[:, k, :], op=ALU.mult), [ppb, b_yain], [b_ya])
        for k in range(4):
            pp, ppb = proj_chunk(C, T, w3, b3, k * 128)
            st, sb_ = sig.next()
            S.op("act", lambda e: e.activation(out=st[:], in_=pp[:], func=AF.Silu), [ppb], [sb_])
            S.op("dve", lambda e: e.tensor_tensor(out=gated[:, k, :], in0=st[:], in1=yc_in[:, k, :], op=ALU.mult), [sb_, b_ycin], [b_gated])
        S.op("act", lambda e: e.activation(out=T.sq[:, 0:4, :], in_=gated[:], func=AF.Square), [b_gated], [T.b_sq])
        for grp in range(2):
            pt, pb = T.psum.next()
            for kk in range(2):
                S.op("pe", lambda e: e.matmul(pt[:], lhsT=T.ones_g[:], rhs=T.sq[:, 2 * grp + kk, :], start=(kk == 0), stop=(kk == 1)),
                     [T.b_sq, T.b_const], [pb])
            S.op("act", lambda e: e.activation(out=T.rstd[:], in_=pt[:], func=AF.Sqrt, bias=T.eps_t[:, 0:1], scale=1.0), [pb, T.b_const], [T.b_rstd])
            S.op("dve", lambda e: e.reciprocal(out=T.rstd[:], in_=T.rstd[:]), [T.b_rstd], [T.b_rstd])
            for kk in range(2):
                k = 2 * grp + kk
                S.op("dve", lambda e: e.scalar_tensor_tensor(out=yc[:, k, :], in0=gated[:, k, :], scalar=nwt[:, k:k + 1], in1=T.rstd[:],
                                                             op0=ALU.mult, op1=ALU.mult), [b_gated, b_wres, T.b_rstd], [b_yc])
        for half in range(2):
            wga, bga = ws.load(winb[2 + half])
            wgb, bgb = ws.load(winb[4 + half])
            wgc, bgc = ws.load(winb[6 + half])
            for mm in range(4):
                m = half * 4 + mm
                ms = slice(m * 128, (m + 1) * 128)
                acc = None
                for (wgt, bgt, ysrc, ybuf, nk, wres) in ((wga, bga, ya, b_ya, 2, wsc), (wgb, bgb, yb, b_yb, 2, wsb), (wgc, bgc, yc, b_yc, 4, wssm)):
                    pg, pgb = proj_chunk(C, T, wgt, bgt, mm * 128)
                    st, sb_ = sig.next()
                    S.op("act", lambda e: e.activation(out=st[:], in_=pg[:], func=AF.Sigmoid), [pgb], [sb_])
                    pbr, pbrb = T.psum.next()
                    for k in range(nk):
                        S.op("pe", lambda e: e.matmul(pbr[:], lhsT=wres[:, k, ms], rhs=ysrc[:, k, :], start=(k == 0), stop=(k == nk - 1)),
                             [b_wres, ybuf], [pbrb])
                    S.op("dve", lambda e: e.tensor_tensor(out=st[:], in0=st[:], in1=pbr[:], op=ALU.mult), [sb_, pbrb], [sb_])
                    if acc is None:
                        acc = (st, sb_)
                    else:
                        S.op("pool", lambda e: e.tensor_tensor(out=acc[0][:], in0=acc[0][:], in1=st[:], op=ALU.add), [acc[1], sb_], [acc[1]])
                S.op("pool", lambda e: e.tensor_copy(out=merged[:, m, :], in_=acc[0][:]), [acc[1]], [b_merged])
        for half in range(2):
            wo_v, wo_b = ws.load(wg["wo"][half])
            for mm in range(4):
                m = half * 4 + mm
                pt, pb = T.psum.next()
                for k in range(8):
                    S.op("pe", lambda e: e.matmul(pt[:], lhsT=wo_v[:, k, mm * 128:(mm + 1) * 128], rhs=merged[:, k, :], start=(k == 0), stop=(k == 7)),
                         [wo_b, b_merged], [pb])
                S.op("act", lambda e: e.copy(out=mo[:, m, :], in_=pt[:]), [pb], [b_mo])
        post_norm_residual(16)
        norm_mod(C, T, 24, 32)
        for j2 in range(11):
            wf, wfb = ws.load(wg["wf1"][j2])
            for jj in range(2):
                j = 2 * j2 + jj
                pgt, pgtb = proj_chunk(C, T, wf, wfb, jj * 128)
                pup, pupb = proj_chunk(C, T, wf, wfb, 256 + jj * 128)
                st, sb_ = sig.next()
                S.op("act", lambda e: e.activation(out=st[:], in_=pgt[:], func=AF.Silu), [pgtb], [sb_])
                S.op("dve", lambda e: e.tensor_tensor(out=act_a[:, j, :], in0=st[:], in1=pup[:], op=ALU.mult), [sb_, pupb], [b_acta])
        for mq in range(4):
            w2, w2b = ws.load(wg["wf2"][mq])
            for mm in range(2):
                m = mq * 2 + mm
                pt, pb = T.psum.next()
                for j in range(NFC):
                    S.op("pe", lambda e: e.matmul(pt[:], lhsT=w2[:, j, mm * 128:(mm + 1) * 128], rhs=act_a[:, j, :], start=(j == 0), stop=(j == NFC - 1)),
                         [w2b, b_acta], [pb])
                S.op("act", lambda e: e.copy(out=mo[:, m, :], in_=pt[:]), [pb], [b_mo])
        post_norm_residual(40)
        S.dma("act", ov[:, :, t0:t0 + TT], T.xt[:], reads=[T.b_xt], adds=[outbuf])
    return outbuf


def host_tokb_params(inp, l):
    f = lambda a: np.ascontiguousarray(a, dtype=np.float32)
    return {"w_sc_out": f(inp["w_sc_out"][l]), "w_sb_out": f(inp["w_sb_out"][l]), "w_ssm_out": f(inp["w_ssm_out"][l]),
            "w_o": f(inp["w_o"][l]), "w_ffn_in": f(inp["w_ffn_in"][l]), "w_ffn_out": f(inp["w_ffn_out"][l]),
            "nw": f(inp["ssm_norm_w"][l].reshape(4, 128).T)}


def _di(nc, name, shape, dt=F32):
    return nc.dram_tensor(name, list(shape), dt, kind="ExternalInput").ap()


def _do(nc, name, shape, dt=F32):
    return nc.dram_tensor(name, list(shape), dt, kind="ExternalOutput").ap()


def _mod_aps(nc, sfx=""):
    return {"cT": _di(nc, "cT" + sfx, [128, 8]), "mod_w": _di(nc, "mod_w" + sfx, [1024, 6144]),
            "mod_b6": _di(nc, "mod_b6" + sfx, [128, 48]), "g4": _di(nc, "g4" + sfx, [128, 32])}


def _mix_aps(nc, sfx=""):
    d = lambda n, s: _di(nc, n + sfx, s)
    return {"consts": d("consts", [128, 896]), "cwA": d("cwA", [64, 3]), "xw": d("xw", [128, 4]), "xb": d("xb", [128, 1]),
            "bcw": d("bcw", [128, 4]), "bcb": d("bcb", [128, 1]), "cbias": d("cbias", [64, 1]),
            "xb_row": d("xb_row", [1, 128]), "bb_row": d("bb_row", [1, 64]), "hp": d("hp", [128, 6])}


def _tokb_w_aps(nc, sfx=""):
    d = lambda n, s: _di(nc, n + sfx, s)
    return {"w_in": d("w_in", [1024, 5896]), "w_sc_out": d("w_sc_out", [256, 1024]), "w_sb_out": d("w_sb_out", [256, 1024]),
            "w_ssm_out": d("w_ssm_out", [512, 1024]), "w_o": d("w_o", [1024, 1024]), "w_ffn_in": d("w_ffn_in", [1024, 2 * FFH]),
            "w_ffn_out": d("w_ffn_out", [FFH, 1024])}


def build_tok_a():
    nc = bass.Bass("TRN2", target_bir_lowering=False)
    with contextlib.ExitStack() as es:
        C = Ctx(nc, es)
        xT = _di(nc, "xT", [1024, NT])
        P = _mod_aps(nc)
        w_in = _di(nc, "w_in", [1024, 5896])
        send = _do(nc, "send", [4, 512, NT], BF16)
        send_dt = _do(nc, "send_dt", [4, 2, NT])
        wgs = prep_weights(C, [("win", w_in, 8, [[g] for g in WIN_GROUPS[:6]])])
        T = tok_setup(C, P)
        compute_mod(C, T, P)
        ws = WStream(C, nbuf=3)
        tok_a(C, T, xT, wgs["win"], send, send_dt, ws)
        C.S.finish()
    return nc


def build_mix():
    nc = bass.Bass("TRN2", target_bir_lowering=False)
    with contextlib.ExitStack() as es:
        C = Ctx(nc, es)
        R = _di(nc, "R", [4, 512, NT], BF16)
        Rdt = _di(nc, "Rdt", [4, 2, NT])
        O2 = _do(nc, "O2", [4, 256, NT], BF16)
        P = _mix_aps(nc)
        mix_piece(C, R, Rdt, O2, P)
        C.S.finish()
    return nc


def build_tok_b():
    nc = bass.Bass("TRN2", target_bir_lowering=False)
    with contextlib.ExitStack() as es:
        C = Ctx(nc, es)
        xT = _di(nc, "xT", [1024, NT])
        P = _mod_aps(nc)
        W = _tokb_w_aps(nc)
        nw = _di(nc, "nw", [128, 4])
        Pin = _di(nc, "Pin", [4, 256, NT], BF16)
        xo = _do(nc, "xo", [1024, NT])
        wgs = prep_weights(C, tokb_weight_specs(W))
        T = tok_setup(C, P)
        compute_mod(C, T, P)
        ws = WStream(C, nbuf=3)
        tok_b(C, T, xT, xo, Pin, wgs, nw, ws)
        C.S.finish()
    return nc


def host_mod_params(inp, l, b):
    f = lambda a: np.ascontiguousarray(a, dtype=np.float32)
    return {
        "cT": f(inp["c"][b].reshape(8, 128).T),
        "mod_w": f(inp["mod_w"][l]),
        "mod_b6": f(inp["mod_b"][l].reshape(48, 128).T),
        "g4": f(np.stack([inp["g_pre_mix"][l], inp["g_post_mix"][l], inp["g_pre_ffn"][l], inp["g_post_ffn"][l]]).reshape(32, 128).T),
    }


def kernel_unfused(**inp):
    inp = {k: np.asarray(v) for k, v in inp.items()}
    x = inp["x"]
    cores = list(range(8))
    xT = [np.ascontiguousarray(x[c // 4, (c % 4) * NT:(c % 4 + 1) * NT, :].T) for c in cores]
    nc_a, nc_m, nc_b = build_tok_a(), build_mix(), build_tok_b()
    for l in range(DEPTH):
        w_in = np.ascontiguousarray(inp["w_in"][l], dtype=np.float32)
        maps = []
        for c in cores:
            m = host_mod_params(inp, l, c // 4)
            m["xT"] = xT[c]
            m["w_in"] = w_in
            maps.append(m)
        ra = run_bass_kernel_spmd(nc_a, maps, core_ids=cores).results
        maps = []
        for c in cores:
            b, g = c // 4, c % 4
            m = host_mix_params(inp, l, g)
            m["R"] = np.stack([np.asarray(ra[4 * b + j]["send"])[g] for j in range(4)])
            m["Rdt"] = np.stack([np.asarray(ra[4 * b + j]["send_dt"])[g] for j in range(4)])
            maps.append(m)
        rm = run_bass_kernel_spmd(nc_m, maps, core_ids=cores).results
        del ra
        maps = []
        tb = host_tokb_params(inp, l)
        for c in cores:
            b, g = c // 4, c % 4
            m = host_mod_params(inp, l, b)
            m.update(tb)
            m["xT"] = xT[c]
            m["w_in"] = w_in
            m["Pin"] = np.stack([np.asarray(rm[4 * b + j]["O2"])[g] for j in range(4)])
            maps.append(m)
        rb = run_bass_kernel_spmd(nc_b, maps, core_ids=cores).results
        del rm
        xT = [np.ascontiguousarray(np.asarray(rb[c]["xo"])) for c in cores]
    out = np.empty((NB, SEQ, D), np.float32)
    for c in cores:
        out[c // 4, (c % 4) * NT:(c % 4 + 1) * NT, :] = xT[c].T
    return out


def kernel(**inp):
    return kernel_unfused(**inp)
```
